# Optimizing a Trainium2 kernel written in Bass

```python
import jax
import jax.numpy as jnp
from jax import lax
import numpy as np

D_MODEL = 1024
BATCH = 8
SEQ = 4096
DEPTH = 1

MLA_HEADS = 8
MLA_NOPE = 64
MLA_ROPE = 32
MLA_V = 64
MLA_Q_RANK = 384
MLA_KV_RANK = 256
ROPE_THETA = 10000.0
Q_BLOCK = 128
M_HEADS = 4
M_DK = 64
M_DV = 128
M_CHUNK = 64
CONV_W = 4
F_BIAS_LO = 3.0
F_BIAS_HI = 6.0
MLA_OUT = MLA_HEADS * MLA_V
M_OUT = M_HEADS * M_DV
D_MIX = MLA_OUT + M_OUT
D_FF = ((-(-8 * D_MODEL // 3)) + 255) // 256 * 256
IN_SPLITS = (MLA_Q_RANK, MLA_KV_RANK, MLA_ROPE, 2 * M_HEADS * M_DK, M_OUT, M_OUT, M_HEADS, M_HEADS)
D_IN = sum(IN_SPLITS)
IN_OFFSETS = tuple(int(v) for v in np.cumsum(IN_SPLITS)[:-1])
EPS = 1e-6

kernel_name = "hybrid_mla_mlstm_adaln_layer"


def rms_norm(x, g):
    xf = x.astype(jnp.float32)
    y = xf * lax.rsqrt(jnp.mean(xf * xf, axis=-1, keepdims=True) + EPS)
    return (y * g.astype(jnp.float32)).astype(x.dtype)


def head_norm(y, g, n_heads):
    b, s, w = y.shape
    yh = y.reshape(b, s, n_heads, w // n_heads)
    return rms_norm(yh, g.reshape(n_heads, w // n_heads)).reshape(b, s, w)


def apply_rope(x, positions):
    half = x.shape[-1] // 2
    inv = ROPE_THETA ** (-jnp.arange(half, dtype=jnp.float32) / half)
    ang = positions.astype(jnp.float32)[..., None] * inv
    cos = jnp.cos(ang)[:, :, None, :]
    sin = jnp.sin(ang)[:, :, None, :]
    xf = x.astype(jnp.float32)
    x1, x2 = xf[..., :half], xf[..., half:]
    return jnp.concatenate([x1 * cos - x2 * sin, x2 * cos + x1 * sin], axis=-1).astype(x.dtype)


def mla(q_lat, kv_lat, kr_lat, positions, g_q, w_uq, g_kv, w_ukv):
    b, s, _ = q_lat.shape
    q = (rms_norm(q_lat, g_q) @ w_uq).reshape(b, s, MLA_HEADS, MLA_NOPE + MLA_ROPE)
    q_nope = q[..., :MLA_NOPE]
    q_rope = apply_rope(q[..., MLA_NOPE:], positions)
    kv = (rms_norm(kv_lat, g_kv) @ w_ukv).reshape(b, s, MLA_HEADS, MLA_NOPE + MLA_V)
    k_nope = kv[..., :MLA_NOPE]
    v = kv[..., MLA_NOPE:]
    k_rope = apply_rope(kr_lat[:, :, None, :], positions)[:, :, 0, :]
    scale = (MLA_NOPE + MLA_ROPE) ** -0.5
    nb = s // Q_BLOCK
    qn_b = q_nope.reshape(b, nb, Q_BLOCK, MLA_HEADS, MLA_NOPE).transpose(1, 0, 2, 3, 4)
    qr_b = q_rope.reshape(b, nb, Q_BLOCK, MLA_HEADS, MLA_ROPE).transpose(1, 0, 2, 3, 4)
    starts = jnp.arange(nb, dtype=jnp.int32) * Q_BLOCK
    kpos = jnp.arange(s, dtype=jnp.int32)

    def block(args):
        qn, qr, start = args
        sc = (jnp.einsum('bqhd,bkhd->bhqk', qn, k_nope, preferred_element_type=jnp.float32)
              + jnp.einsum('bqhr,bkr->bhqk', qr, k_rope, preferred_element_type=jnp.float32)) * scale
        qpos = start + jnp.arange(Q_BLOCK, dtype=jnp.int32)
        causal = kpos[None, :] <= qpos[:, None]
        sc = jnp.where(causal, sc, -jnp.inf)
        p = jax.nn.softmax(sc, axis=-1).astype(v.dtype)
        return jnp.einsum('bhqk,bkhd->bqhd', p, v)

    out = lax.map(block, (qn_b, qr_b, starts))
    return out.transpose(1, 0, 2, 3, 4).reshape(b, s, MLA_HEADS * MLA_V)


def causal_conv(x, w, bias):
    ch = x.shape[-1]
    y = lax.conv_general_dilated(x, w[:, None, :].astype(x.dtype), window_strides=(1,),
                                 padding=[(CONV_W - 1, 0)], dimension_numbers=('NWC', 'WIO', 'NWC'),
                                 feature_group_count=ch)
    return y + bias


def mlstm(q, k, v, o_pre, i_pre, f_pre):
    f32 = jnp.float32
    b, s, h, _ = q.shape
    L = M_CHUNK
    nc = s // L

    def chunk(t):
        return t.reshape(b, nc, L, h, t.shape[-1]).transpose(0, 3, 1, 2, 4)

    qc = chunk(q.astype(f32))
    kc = chunk(k.astype(f32) * (M_DK ** -0.5))
    vc = chunk(v.astype(f32))
    ig = i_pre.astype(f32).reshape(b, nc, L, h).transpose(0, 3, 1, 2)
    lf = jax.nn.log_sigmoid(f_pre.astype(f32)).reshape(b, nc, L, h).transpose(0, 3, 1, 2)
    bcum = jnp.cumsum(lf, axis=-1)
    b_tot = bcum[..., -1]

    g = b_tot[..., None] - bcum + ig
    m_loc = jnp.max(g, axis=-1)
    wgt = jnp.exp(g - m_loc[..., None])
    dC = jnp.einsum('bhcl,bhclv,bhclk->bhcvk', wgt, vc, kc)
    dn = jnp.einsum('bhcl,bhclk->bhck', wgt, kc)

    def step(carry, inp):
        C, n, m = carry
        dC_c, dn_c, ml_c, bt_c = inp
        m_new = jnp.maximum(bt_c + m, ml_c)
        a = jnp.exp(bt_c + m - m_new)
        e = jnp.exp(ml_c - m_new)
        C_new = a[..., None, None] * C + e[..., None, None] * dC_c
        n_new = a[..., None] * n + e[..., None] * dn_c
        return (C_new, n_new, m_new), (C, n, m)

    init = (jnp.zeros((b, h, M_DV, M_DK), f32), jnp.zeros((b, h, M_DK), f32), jnp.zeros((b, h), f32))
    xs = (dC.transpose(2, 0, 1, 3, 4), dn.transpose(2, 0, 1, 3), m_loc.transpose(2, 0, 1), b_tot.transpose(2, 0, 1))
    _, (C0, n0, m0) = lax.scan(step, init, xs)
    C0 = C0.transpose(1, 2, 0, 3, 4)
    n0 = n0.transpose(1, 2, 0, 3)
    m0 = m0.transpose(1, 2, 0)

    causal = jnp.tril(jnp.ones((L, L), dtype=bool))
    logD = jnp.where(causal, bcum[..., :, None] - bcum[..., None, :] + ig[..., None, :], -jnp.inf)
    log_inter = bcum + m0[..., None]
    m_t = jnp.maximum(log_inter, jnp.max(logD, axis=-1))
    Dm = jnp.exp(logD - m_t[..., None])
    inter = jnp.exp(log_inter - m_t)
    sc = jnp.einsum('bhctk,bhcsk->bhcts', qc, kc) * Dm
    num = jnp.einsum('bhcts,bhcsv->bhctv', sc, vc) + inter[..., None] * jnp.einsum('bhctk,bhcvk->bhctv', qc, C0)
    den = jnp.sum(sc, axis=-1) + inter * jnp.einsum('bhctk,bhck->bhct', qc, n0)
    hh = num / jnp.maximum(jnp.abs(den), jnp.exp(-m_t))[..., None]
    hh = hh.transpose(0, 2, 3, 1, 4).reshape(b, s, h * M_DV)
    return (jax.nn.sigmoid(o_pre.astype(f32)) * hh).astype(o_pre.dtype)


def setup_inputs(seed: int = 0) -> dict:
    key = jax.random.key(seed)
    ks = jax.random.split(key, 24)
    f32 = jnp.float32

    def nrm(k, shape, scale):
        return jax.random.normal(k, shape, f32) * scale

    def gain(k, shape):
        return 1.0 + 0.02 * jax.random.normal(k, shape, f32)

    L = DEPTH
    x = nrm(ks[0], (BATCH, SEQ, D_MODEL), 1.0)
    c = nrm(ks[1], (BATCH, D_MODEL), 1.0)
    positions = jnp.broadcast_to(jnp.arange(SEQ, dtype=jnp.int32)[None, :], (BATCH, SEQ))
    w_ada = nrm(ks[2], (L, D_MODEL, 6 * D_MODEL), D_MODEL ** -0.5)
    b_ada = nrm(ks[3], (L, 6 * D_MODEL), 0.02)
    g_mix = gain(ks[4], (L, D_MODEL))
    w_in = nrm(ks[5], (L, D_MODEL, D_IN), D_MODEL ** -0.5)
    g_q = gain(ks[6], (L, MLA_Q_RANK))
    w_uq = nrm(ks[7], (L, MLA_Q_RANK, MLA_HEADS * (MLA_NOPE + MLA_ROPE)), MLA_Q_RANK ** -0.5)
    g_kv = gain(ks[8], (L, MLA_KV_RANK))
    w_ukv = nrm(ks[9], (L, MLA_KV_RANK, MLA_HEADS * (MLA_NOPE + MLA_V)), MLA_KV_RANK ** -0.5)
    conv_w = nrm(ks[10], (L, CONV_W, 2 * M_HEADS * M_DK), CONV_W ** -0.5)
    conv_b = nrm(ks[11], (L, 2 * M_HEADS * M_DK), 0.02)
    i_bias = nrm(ks[12], (L, M_HEADS), 0.1)
    f_bias = jnp.linspace(F_BIAS_LO, F_BIAS_HI, M_HEADS, dtype=f32)[None, :] + nrm(ks[13], (L, M_HEADS), 0.1)
    b_gates = jnp.concatenate([i_bias, f_bias], axis=-1)
    g_out_mla = gain(ks[14], (L, MLA_OUT))
    g_out_mlstm = gain(ks[15], (L, M_OUT))
    w_out = nrm(ks[16], (L, D_MIX, D_MODEL), D_MIX ** -0.5)
    g_ffn = gain(ks[17], (L, D_MODEL))
    w_gate = nrm(ks[18], (L, D_MODEL, D_FF), D_MODEL ** -0.5)
    w_up = nrm(ks[19], (L, D_MODEL, D_FF), D_MODEL ** -0.5)
    w_down = nrm(ks[20], (L, D_FF, D_MODEL), D_FF ** -0.5)
    g_final = gain(ks[21], (D_MODEL,))
    return {"x": x, "c": c, "positions": positions, "w_ada": w_ada, "b_ada": b_ada,
            "g_mix": g_mix, "w_in": w_in, "g_q": g_q, "w_uq": w_uq, "g_kv": g_kv, "w_ukv": w_ukv,
            "conv_w": conv_w, "conv_b": conv_b, "b_gates": b_gates,
            "g_out_mla": g_out_mla, "g_out_mlstm": g_out_mlstm, "w_out": w_out,
            "g_ffn": g_ffn, "w_gate": w_gate, "w_up": w_up, "w_down": w_down, "g_final": g_final}


def reference(x, c, positions, w_ada, b_ada, g_mix, w_in, g_q, w_uq, g_kv, w_ukv,
              conv_w, conv_b, b_gates, g_out_mla, g_out_mlstm, w_out,
              g_ffn, w_gate, w_up, w_down, g_final):
    b, s, _ = x.shape
    cond = jax.nn.silu(c)
    for l in range(DEPTH):
        mod = cond @ w_ada[l] + b_ada[l]
        sh_a, sc_a, gt_a, sh_f, sc_f, gt_f = [m[:, None, :] for m in jnp.split(mod, 6, axis=-1)]

        h = rms_norm(x, g_mix[l]) * (1.0 + sc_a) + sh_a
        z = h @ w_in[l]
        q_lat, kv_lat, kr_lat, z_qk, z_v, z_o, z_i, z_f = jnp.split(z, IN_OFFSETS, axis=-1)
        y_a = mla(q_lat, kv_lat, kr_lat, positions, g_q[l], w_uq[l], g_kv[l], w_ukv[l])
        qk = jax.nn.silu(causal_conv(z_qk, conv_w[l], conv_b[l]))
        q_m = qk[..., :M_HEADS * M_DK].reshape(b, s, M_HEADS, M_DK)
        k_m = qk[..., M_HEADS * M_DK:].reshape(b, s, M_HEADS, M_DK)
        v_m = z_v.reshape(b, s, M_HEADS, M_DV)
        y_b = mlstm(q_m, k_m, v_m, z_o, z_i + b_gates[l, :M_HEADS], z_f + b_gates[l, M_HEADS:])
        y = jnp.concatenate([head_norm(y_a, g_out_mla[l], MLA_HEADS),
                             head_norm(y_b, g_out_mlstm[l], M_HEADS)], axis=-1)
        x = x + gt_a * (y @ w_out[l])

        h = rms_norm(x, g_ffn[l]) * (1.0 + sc_f) + sh_f
        x = x + gt_f * ((jax.nn.silu(h @ w_gate[l]) * (h @ w_up[l])) @ w_down[l])
    return rms_norm(x, g_final)
```

```python
import math
from contextlib import ExitStack

import numpy as np
import concourse.bass as bass
import concourse.mybir as mybir
from concourse.bass_utils import run_bass_kernel_spmd

F32 = mybir.dt.float32
BF16 = mybir.dt.bfloat16
I32 = mybir.dt.int32
AF = mybir.ActivationFunctionType
ALU = mybir.AluOpType
AX = mybir.AxisListType

S = 4096
D = 1024
NT = 8
DFF = 2816
NJ = 22
EPS = 1e-6
SCALE = 96 ** -0.5


class Tl:
    __slots__ = ("name", "w", "r")

    def __init__(self, name=""):
        self.name = name
        self.w = None
        self.r = []


class Op:
    __slots__ = ("eng", "fn", "deps", "dma", "signal", "token", "pre", "prog")

    def __init__(self, eng, fn, dma):
        self.eng = eng
        self.fn = fn
        self.dma = dma
        self.deps = []
        self.signal = False
        self.token = None
        self.pre = None


class Prog:
    ENGS = ("tensor", "vector", "scalar", "gpsimd", "sync")
    NRING = 6

    _uid = [0]

    def __init__(self):
        self.ops = {e: [] for e in self.ENGS}

    @classmethod
    def _sem(cls, nc, semstack):
        cls._uid[0] += 1
        return semstack.enter_context(nc.semaphore("sem%d" % cls._uid[0]))

    def add(self, eng, fn, reads=(), writes=(), dma=False):
        op = Op(eng, fn, dma)
        op.prog = self
        deps = {}

        def need(d, kind):
            if d is None or d.prog is not self:
                return
            if d.dma:
                deps[id(d)] = d
                return
            if d.eng == eng and not dma and eng == "tensor":
                return
            deps[id(d)] = d

        for t in reads:
            need(t.w, "raw")
        for t in writes:
            need(t.w, "waw")
            for r in t.r:
                need(r, "war")
        op.deps = list(deps.values())
        for d in op.deps:
            d.signal = True
        for t in reads:
            t.r.append(op)
        for t in writes:
            t.w = op
            t.r = []
        self.ops[eng].append(op)
        return op

    def pe(self, fn, reads=(), writes=()):
        return self.add("tensor", fn, reads, writes)

    def dve(self, fn, reads=(), writes=()):
        return self.add("vector", fn, reads, writes)

    def act(self, fn, reads=(), writes=()):
        return self.add("scalar", fn, reads, writes)

    def pool(self, fn, reads=(), writes=()):
        return self.add("gpsimd", fn, reads, writes)

    def dma(self, q, out, in_, reads=(), writes=(), **kw):
        return self.add(q, lambda e: e.dma_start(out=out, in_=in_, **kw), reads, writes, dma=True)

    def emit(self, nc, semstack, prev_finals):
        esem = {e: self._sem(nc, semstack) for e in self.ENGS}
        qsem = {}
        for e in self.ENGS:
            if any(o.dma for o in self.ops[e]):
                qsem[e] = [self._sem(nc, semstack) for _ in range(self.NRING)]
        finals = {}
        for e in self.ENGS:
            last = None
            for o in self.ops[e]:
                if not o.dma:
                    last = o
            if last is not None:
                last.signal = True
            cnt = 0
            nd = 0
            for o in self.ops[e]:
                if o.dma:
                    slot = nd % self.NRING
                    rnd = nd // self.NRING
                    if rnd > 0:
                        o.pre = (qsem[e][slot], 16 * rnd)
                    o.token = (qsem[e][slot], 16 * (rnd + 1))
                    finals[("q", e, slot)] = o.token
                    nd += 1
                elif o.signal:
                    cnt += 1
                    o.token = (esem[e], cnt)
                    finals[("e", e)] = o.token

        with nc.Block() as block:
            def run(e):
                def body(eng):
                    waited = {}

                    def wait(tok):
                        sem, val = tok
                        k = id(sem)
                        if waited.get(k, 0) < val:
                            eng.wait_ge(sem, val)
                            waited[k] = val

                    for tok in prev_finals:
                        wait(tok)
                    for o in self.ops[e]:
                        for d in o.deps:
                            wait(d.token)
                        if o.pre is not None:
                            wait(o.pre)
                        ins = o.fn(eng)
                        if o.dma:
                            ins.then_inc(o.token[0], 16)
                        elif o.signal:
                            ins.then_inc(o.token[0], 1)
                    if e == "sync":
                        for k, tok in finals.items():
                            if k[0] == "q":
                                wait(tok)
                return body

            block.tensor(run("tensor"))
            block.vector(run("vector"))
            block.scalar(run("scalar"))
            block.gpsimd(run("gpsimd"))
            block.sync(run("sync"))
        return list(finals.values())


class Ring:
    def __init__(self, items):
        self.items = items
        self.i = 0

    def get(self):
        it = self.items[self.i % len(self.items)]
        self.i += 1
        return it


def build_program(dbg=None, phases=("A3", "B")):
    nc = bass.Bass("TRN2", target_bir_lowering=False)

    def din(name, shape, dt=F32):
        return nc.dram_tensor(name, list(shape), dt, kind="ExternalInput").ap()

    x_d = din("x", [S, D])
    cT_d = din("cT", [128, 8])
    pos_d = din("pos", [1, S], I32)
    wada_d = din("w_ada", [D, 6 * D])
    bada_d = din("badaT", [128, 48])
    gmix_d = din("gmixT", [128, 8])
    gffn_d = din("gffnT", [128, 8])
    gfin_d = din("gfin", [1, D])
    winl_d = din("winl", [D, 704])
    winm_d = din("winm", [D, 1544])
    gq_d = din("gqT", [128, 3])
    gkv_d = din("gkvT", [128, 2])
    wq_d = din("wq", [384, 8 * 256])
    wkv_d = din("wkv", [256, 8 * 192])
    convw_d = din("convw", [64, 32])
    convb_d = din("convb", [64, 8])
    bg_d = din("bg", [4, 2])
    gomla_d = din("gomlaT", [128, 4])
    gomls_d = din("gomlsT", [128, 4])
    wout_d = din("w_out", [D, D])
    wg_d = din("w_gate", [D, DFF])
    wu_d = din("w_up", [D, DFF])
    wd_d = din("w_down", [DFF, D])
    ident_d = din("ident", [128, 128])
    tri_d = din("tri", [128, 128])
    sel4_d = din("sel4", [4, 256])
    ropec_d = din("ropec", [32, 4])
    out_d = nc.dram_tensor("out", [S, D], F32, kind="ExternalOutput").ap()
    dbg_outs = {}
    if dbg:
        for name, shape in dbg.items():
            dbg_outs[name] = nc.dram_tensor("dbg_" + name, list(shape), F32, kind="ExternalOutput").ap()

    semstack = ExitStack()
    top = ExitStack()
    with semstack, top:
        uid = [0]

        def sbt(stack, name, shape, dt=F32):
            uid[0] += 1
            return stack.enter_context(nc.sbuf_tensor("s%d_%s" % (uid[0], name), list(shape), dt)), Tl(name)

        def pst(stack, name, shape, dt=F32):
            uid[0] += 1
            return stack.enter_context(nc.psum_tensor("p%d_%s" % (uid[0], name), list(shape), dt)), Tl(name)

        identf, Tidentf = sbt(top, "identf", [128, 128])
        identb, Tidentb = sbt(top, "identb", [128, 128], BF16)
        trib, Ttrib = sbt(top, "trib", [128, 128], BF16)
        onesf, Tonesf = sbt(top, "onesf", [128, 128])
        onesb, Tonesb = sbt(top, "onesb", [128, 128], BF16)
        sel4, Tsel4 = sbt(top, "sel4", [4, 256])
        ropec, Tropec = sbt(top, "ropec", [32, 4])
        modc, Tmodc = sbt(top, "modc", [128, 48])
        scA, TscA = sbt(top, "scA", [128, 8])
        scF, TscF = sbt(top, "scF", [128, 8])
        gq, Tgq = sbt(top, "gq", [128, 3])
        gkv, Tgkv = sbt(top, "gkv", [128, 2])
        convw, Tconvw = sbt(top, "convw", [64, 32])
        convb, Tconvb = sbt(top, "convb", [64, 8])
        bg, Tbg = sbt(top, "bg", [4, 2])
        gomla, Tgomla = sbt(top, "gomla", [128, 4])
        gomls, Tgomls = sbt(top, "gomls", [128, 4])
        yTa, TyTa = sbt(top, "yTa", [128, 4, S], BF16)
        Tpar = Tl("params")

        finals = []

        def dump(P, name, ap, T):
            if name in dbg_outs:
                P.dma("gpsimd", dbg_outs[name], ap, reads=[T])

        def fe_stats(P, fe, xb, Txb):
            junk, Tjunk = fe["junk"].get()
            st, Tst = fe["stat"].get()
            xn, Txn = fe["xn"].get()
            P.act(lambda e: e.activation(out=junk[:], in_=xb, func=AF.Square, accum_out=st[:, 0:1]),
                  reads=[Txb], writes=[Tjunk, Tst])
            P.dve(lambda e: e.tensor_scalar(out=st[:, 1:2], in0=st[:, 0:1], scalar1=1.0 / D, scalar2=EPS,
                                            op0=ALU.mult, op1=ALU.add), reads=[Tst], writes=[Tst])
            P.act(lambda e: e.activation(out=st[:, 2:3], in_=st[:, 1:2], func=AF.Ln), reads=[Tst], writes=[Tst])
            P.act(lambda e: e.activation(out=st[:, 3:4], in_=st[:, 2:3], func=AF.Exp, scale=-0.5), reads=[Tst], writes=[Tst])
            P.dve(lambda e: e.tensor_scalar(out=xn[:], in0=xb, scalar1=st[:, 3:4], scalar2=None, op0=ALU.mult),
                  reads=[Txb, Tst], writes=[Txn])
            return xn, Txn

        def fe_trans(P, fe, xn, Txn, sc, sh, Tsc, hT, ThT, blk):
            pT, TpT = fe["pT"].get()
            for c in range(8):
                P.pe(lambda e, c=c: e.transpose(pT[:, c * 128:(c + 1) * 128], xn[:, c * 128:(c + 1) * 128], identb[:]),
                     reads=[Txn, Tidentb], writes=[TpT])
            for c in range(8):
                P.act(lambda e, c=c: e.activation(out=hT[:, c, blk * 128:(blk + 1) * 128], in_=pT[:, c * 128:(c + 1) * 128],
                                                  func=AF.Identity, bias=sh[:, c:c + 1], scale=sc[:, c:c + 1]),
                      reads=[TpT, Tsc], writes=[ThT[c]])

        def frontend(P, fe, xb, Txb, sc, sh, Tsc, hT, ThT, blk):
            xn, Txn = fe_stats(P, fe, xb, Txb)
            fe_trans(P, fe, xn, Txn, sc, sh, Tsc, hT, ThT, blk)

        def fe_tile(P, fe, xbr, t, sc, sh, Tsc, hT, ThT):
            xs = []
            for blk in range(4):
                xb, Txb = xbr.get()
                r0 = t * 512 + blk * 128
                P.dma("sync", xb[:], x_d[r0:r0 + 128, :], writes=[Txb])
                xs.append((xb, Txb))
            xn = [None] * 4
            xn[0] = fe_stats(P, fe, xs[0][0][:], xs[0][1])
            for blk in range(4):
                if blk + 1 < 4:
                    xn[blk + 1] = fe_stats(P, fe, xs[blk + 1][0][:], xs[blk + 1][1])
                fe_trans(P, fe, xn[blk][0], xn[blk][1], sc, sh, Tsc, hT, ThT, blk)

        def fe_tile_thunks(P, fe, xbr, t, sc, sh, Tsc, hT, ThT):
            xs = [None] * 4
            xn = [None] * 4

            def load():
                for blk in range(4):
                    xb, Txb = xbr.get()
                    r0 = t * 512 + blk * 128
                    P.dma("sync", xb[:], x_d[r0:r0 + 128, :], writes=[Txb])
                    xs[blk] = (xb, Txb)

            def stats(blk):
                def f():
                    xn[blk] = fe_stats(P, fe, xs[blk][0][:], xs[blk][1])
                return f

            def trans(blk):
                def f():
                    fe_trans(P, fe, xn[blk][0], xn[blk][1], sc, sh, Tsc, hT, ThT, blk)
                return f
            return [load, stats(0), stats(1), trans(0), stats(2), trans(1), stats(3), trans(2), trans(3)]

        def merge(streams):
            items = []
            for si, st_ in enumerate(streams):
                n = len(st_)
                for j, th in enumerate(st_):
                    items.append(((j + 0.5) / n, si, j, th))
            items.sort(key=lambda t: (t[0], t[1], t[2]))
            for it in items:
                it[3]()

        def make_fe(stack, nxn=2, npt=2):
            fe = {}
            fe["junk"] = Ring([sbt(stack, "fe_junk%d" % i, [128, D], BF16) for i in range(1)])
            fe["stat"] = Ring([sbt(stack, "fe_st%d" % i, [128, 4]) for i in range(4)])
            fe["xn"] = Ring([sbt(stack, "fe_xn%d" % i, [128, D], BF16) for i in range(nxn)])
            fe["pT"] = Ring([pst(stack, "fe_pT%d" % i, [128, D], BF16) for i in range(npt)])
            return fe

        def rsqrt_inplace(P, buf, Tbuf, src, Tsrc, scale, eps):
            P.dve(lambda e: e.tensor_scalar(out=buf, in0=src, scalar1=scale, scalar2=eps, op0=ALU.mult, op1=ALU.add),
                  reads=[Tsrc], writes=[Tbuf])
            P.act(lambda e: e.activation(out=buf, in_=buf, func=AF.Ln), reads=[Tbuf], writes=[Tbuf])
            P.act(lambda e: e.activation(out=buf, in_=buf, func=AF.Exp, scale=-0.5), reads=[Tbuf], writes=[Tbuf])

        with ExitStack() as ph:
            P = Prog()
            cT, TcT = sbt(ph, "cT", [128, 8])
            silc, Tsilc = sbt(ph, "silc", [128, 8])
            bada, Tbada = sbt(ph, "bada", [128, 48])
            gmix, Tgmix = sbt(ph, "gmix", [128, 8])
            gffn, Tgffn = sbt(ph, "gffn", [128, 8])
            tmp8, Ttmp8 = sbt(ph, "tmp8", [128, 8])
            wst = Ring([sbt(ph, "wada%d" % i, [128, 8, 512]) for i in range(2)])
            pm, Tpm = pst(ph, "pm", [128, 512])

            for dst, src in ((identf, ident_d), (sel4, sel4_d), (ropec, ropec_d), (gq, gq_d), (gkv, gkv_d),
                             (convw, convw_d), (convb, convb_d), (bg, bg_d), (gomla, gomla_d), (gomls, gomls_d)):
                P.dma("sync", dst[:], src, writes=[Tpar])
            Tidentf.w = Tpar.w
            P.dma("sync", cT[:], cT_d, writes=[TcT])
            P.dma("sync", bada[:], bada_d, writes=[Tbada])
            P.dma("sync", gmix[:], gmix_d, writes=[Tgmix])
            P.dma("sync", gffn[:], gffn_d, writes=[Tgffn])
            P.dma("gpsimd", identb[:], ident_d, writes=[Tidentb])
            P.dma("gpsimd", trib[:], tri_d, writes=[Ttrib])
            P.dve(lambda e: e.memset(onesf[:], 1.0), writes=[Tonesf])
            P.dve(lambda e: e.memset(onesb[:], 1.0), writes=[Tonesb])
            P.act(lambda e: e.activation(out=silc[:], in_=cT[:], func=AF.Silu), reads=[TcT], writes=[Tsilc])
            wada_v = wada_d.rearrange("(k p) n -> p k n", p=128)
            for pc in range(12):
                wt, Twt = wst.get()
                P.dma("sync", wt[:], wada_v[:, :, pc * 512:(pc + 1) * 512], writes=[Twt])
                for jj in range(4):
                    j = pc * 4 + jj
                    for k in range(8):
                        P.pe(lambda e, wt=wt, jj=jj, j=j, k=k: e.matmul(
                            pm[:, j:j + 1], lhsT=wt[:, k, jj * 128:(jj + 1) * 128], rhs=silc[:, k:k + 1],
                            start=(k == 0), stop=(k == 7)), reads=[Twt, Tsilc], writes=[Tpm])
            P.dve(lambda e: e.tensor_tensor(out=modc[:], in0=pm[:, 0:48], in1=bada[:], op=ALU.add),
                  reads=[Tpm, Tbada], writes=[Tmodc])
            P.dve(lambda e: e.tensor_scalar(out=tmp8[:], in0=modc[:, 8:16], scalar1=1.0, scalar2=None, op0=ALU.add),
                  reads=[Tmodc], writes=[Ttmp8])
            P.dve(lambda e: e.tensor_tensor(out=scA[:], in0=tmp8[:], in1=gmix[:], op=ALU.mult),
                  reads=[Ttmp8, Tgmix], writes=[TscA])
            P.dve(lambda e: e.tensor_scalar(out=tmp8[:], in0=modc[:, 32:40], scalar1=1.0, scalar2=None, op0=ALU.add),
                  reads=[Tmodc, TscA], writes=[Ttmp8])
            P.dve(lambda e: e.tensor_tensor(out=scF[:], in0=tmp8[:], in1=gffn[:], op=ALU.mult),
                  reads=[Ttmp8, Tgffn], writes=[TscF])
            dump(P, "modc", modc[:], Tmodc)
            finals = P.emit(nc, semstack, finals)

        shA = modc[:, 0:8]
        shF = modc[:, 24:32]

        with ExitStack() as pa:
            qnT, TqnT = sbt(pa, "qnT", [128, 3, S], BF16)
            kvnT, TkvnT = sbt(pa, "kvnT", [128, 2, S], BF16)
            krT, TkrT = sbt(pa, "krT", [32, S], BF16)
            cosT, TcosT = sbt(pa, "cosT", [32, S])
            sinT, TsinT = sbt(pa, "sinT", [32, S])
            wq, Twq = sbt(pa, "wq", [128, 3, 8 * 256], BF16)
            wkv, Twkv = sbt(pa, "wkv", [128, 2, 8 * 192], BF16)

            with ExitStack() as ph:
                P = Prog()
                winl, Twinl = sbt(ph, "winl", [128, 8, 704], BF16)
                xbr = Ring([sbt(ph, "xb%d" % i, [128, D]) for i in range(4)])
                hTr = Ring([(sbt(ph, "hT%d" % i, [128, 8, 512], BF16)[0], [Tl("hT%d_%d" % (i, c)) for c in range(8)]) for i in range(2)])
                fe = make_fe(ph)
                sq, Tsq = sbt(ph, "sq", [128, 3, 512])
                rstd, Trstd = sbt(ph, "rstd", [128, 512])
                posi, Tposi = sbt(ph, "posi", [32, 512], I32)
                ry, Try = sbt(ph, "ry", [32, 512])
                rn, Trn = sbt(ph, "rn", [32, 512], I32)
                rf, Trf = sbt(ph, "rf", [32, 512])
                rg, Trg = sbt(ph, "rg", [32, 512])
                t1, Tt1 = sbt(ph, "t1", [32, 512])
                t2, Tt2 = sbt(ph, "t2", [32, 512])
                pl = Ring([pst(ph, "pl%d" % i, [128, 512]) for i in range(6)])

                P.dma("gpsimd", winl[:], winl_d.rearrange("(k p) n -> p k n", p=128), writes=[Twinl])
                P.dma("gpsimd", wq[:], wq_d.rearrange("(k p) n -> p k n", p=128), writes=[Twq])
                P.dma("gpsimd", wkv[:], wkv_d.rearrange("(k p) n -> p k n", p=128), writes=[Twkv])

                hTs = {}

                def a1_front(i):
                    hTs[i] = hTr.get()
                    fe_tile(P, fe, xbr, i, scA, shA, TscA, hTs[i][0], hTs[i][1])

                def a1_tile(i):
                    cols = slice(i * 512, (i + 1) * 512)
                    hT, ThT = hTs.pop(i)

                    def r0():
                        P.dma("sync", posi[:], pos_d[:, cols].to_broadcast([32, 512]), writes=[Tposi])
                        P.dve(lambda e: e.tensor_copy(out=ry[:], in_=posi[:]), reads=[Tposi], writes=[Try])
                        P.dve(lambda e: e.tensor_scalar(out=ry[:], in0=ry[:], scalar1=ropec[:, 0:1], scalar2=None, op0=ALU.mult),
                              reads=[Try, Tpar], writes=[Try])

                    def rw(which):
                        def f():
                            if which == 1:
                                P.dve(lambda e: e.tensor_scalar(out=ry[:], in0=ry[:], scalar1=0.25, scalar2=None, op0=ALU.add),
                                      reads=[Try], writes=[Try])
                            P.dve(lambda e: e.tensor_copy(out=rn[:], in_=ry[:]), reads=[Try], writes=[Trn])
                            P.dve(lambda e: e.tensor_copy(out=rf[:], in_=rn[:]), reads=[Trn], writes=[Trf])
                            P.dve(lambda e: e.tensor_tensor(out=rg[:], in0=ry[:], in1=rf[:], op=ALU.subtract),
                                  reads=[Try, Trf], writes=[Trg])
                            if which == 0:
                                P.act(lambda e: e.activation(out=sinT[:, cols], in_=rg[:], func=AF.Sin, scale=ropec[:, 2:3]),
                                      reads=[Trg, Tpar], writes=[TsinT])
                            else:
                                P.act(lambda e: e.activation(out=cosT[:, cols], in_=rg[:], func=AF.Sin, scale=ropec[:, 3:4]),
                                      reads=[Trg, Tpar], writes=[TcosT])
                        return f
                    ROPE = [r0, rw(0), rw(1)]

                    st_ = {}

                    def inproj(name, c0, c1, M):
                        def f():
                            pt, Tpt = pl.get()
                            for k in range(8):
                                P.pe(lambda e, k=k: e.matmul(pt[0:M, :], lhsT=winl[:, k, c0:c1], rhs=hT[:, k, :],
                                                             start=(k == 0), stop=(k == 7)),
                                     reads=[Twinl, ThT[k]], writes=[Tpt])
                            st_[name] = (pt, Tpt)
                        return f

                    def lat_stats(names, nch):
                        def f():
                            for m, nm in enumerate(names):
                                pt, Tpt = st_[nm]
                                P.act(lambda e, m=m, pt=pt: e.activation(out=sq[:, m, :], in_=pt[:], func=AF.Square),
                                      reads=[Tpt], writes=[Tsq])
                            ps_, Tps = pl.get()
                            for m in range(nch):
                                P.pe(lambda e, m=m: e.matmul(ps_[:], lhsT=onesf[:], rhs=sq[:, m, :], start=(m == 0), stop=(m == nch - 1)),
                                     reads=[Tsq, Tonesf], writes=[Tps])
                            rsqrt_inplace(P, rstd[:], Trstd, ps_[:], Tps, 1.0 / (128 * nch), EPS)
                        return f

                    def lat_final(names, g, dst, Tdst):
                        def f():
                            for m, nm in enumerate(names):
                                pt, Tpt = st_.pop(nm)
                                P.dve(lambda e, m=m, pt=pt: e.scalar_tensor_tensor(
                                    out=dst[:, m, cols], in0=pt[:], scalar=g[:, m:m + 1], in1=rstd[:], op0=ALU.mult, op1=ALU.mult),
                                    reads=[Tpt, Trstd, Tpar], writes=[Tdst])
                        return f

                    def kr_rope():
                        pkr, Tpkr = st_.pop("kr")
                        pks, Tpks = st_.pop("ks")
                        P.dve(lambda e: e.tensor_tensor(out=t1[:], in0=pkr[0:32, :], in1=cosT[:, cols], op=ALU.mult),
                              reads=[Tpkr, TcosT], writes=[Tt1])
                        P.dve(lambda e: e.tensor_tensor(out=t2[:], in0=pks[0:32, :], in1=sinT[:, cols], op=ALU.mult),
                              reads=[Tpks, TsinT], writes=[Tt2])
                        P.dve(lambda e: e.tensor_tensor(out=krT[:, cols], in0=t1[:], in1=t2[:], op=ALU.add),
                              reads=[Tt1, Tt2], writes=[TkrT])
                    qn_ = ["q0", "q1", "q2"]
                    kn_ = ["kv0", "kv1"]
                    MAT = [inproj("q%d" % m, m * 128, (m + 1) * 128, 128) for m in range(3)]
                    MAT += [inproj("kv%d" % m, 384 + m * 128, 384 + (m + 1) * 128, 128) for m in range(2)]
                    MAT += [lat_stats(qn_, 3), lat_final(qn_, gq, qnT, TqnT),
                            inproj("kr", 640, 672, 32), inproj("ks", 672, 704, 32),
                            lat_stats(kn_, 2), lat_final(kn_, gkv, kvnT, TkvnT), kr_rope]

                    FE = []
                    if i + 1 < NT:
                        hTs[i + 1] = hTr.get()
                        FE = fe_tile_thunks(P, fe, xbr, i + 1, scA, shA, TscA, hTs[i + 1][0], hTs[i + 1][1])
                    merge([st for st in (FE, ROPE, MAT) if st])

                a1_front(0)
                for i in range(NT):
                    a1_tile(i)
                dump(P, "qnT0", qnT[:, 0, 0:512], TqnT)
                dump(P, "kvnT1", kvnT[:, 1, 512:1024], TkvnT)
                dump(P, "krT", krT[:, 0:1024], TkrT)
                finals = P.emit(nc, semstack, finals)

            with ExitStack() as ph:
                P = Prog()
                QTs = [sbt(ph, "QT%d" % i, [128, S], BF16) for i in range(2)]
                KTs = [sbt(ph, "KT%d" % i, [128, S], BF16) for i in range(2)]
                Vas = [sbt(ph, "Va%d" % i, [128, 32, 128], BF16) for i in range(2)]
                sqqr = Ring([sbt(ph, "sqq%d" % i, [128, 512], BF16) for i in range(2)])
                sqkr = Ring([sbt(ph, "sqk%d" % i, [128, 512], BF16) for i in range(2)])
                mxs = [sbt(ph, "mx%d" % i, [33, 32]) for i in range(2)]
                ptr = Ring([sbt(ph, "pt%d" % i, [128, 512], BF16) for i in range(6)])
                osqr = Ring([sbt(ph, "osq%d" % i, [128, 512], BF16) for i in range(2)])
                lsqr = Ring([sbt(ph, "lsq%d" % i, [128, 512], BF16) for i in range(2)])
                rsr = Ring([sbt(ph, "rs%d" % i, [128, 512]) for i in range(2)])
                lrwr = Ring([sbt(ph, "lrw%d" % i, [128, 512]) for i in range(2)])
                a1, Ta1 = sbt(ph, "a1", [32, 512])
                a2, Ta2 = sbt(ph, "a2", [32, 512])
                pp = Ring([pst(ph, "pp%d" % i, [128, 512]) for i in range(3)])
                pps = Ring([pst(ph, "pps%d" % i, [128, 512]) for i in range(3)])
                po = Ring([pst(ph, "po%d" % i, [128, 512]) for i in range(2)])

                for b in range(2):
                    QT, TQT = QTs[b]
                    KT, TKT = KTs[b]
                    Va, TVa = Vas[b]
                    P.pool(lambda e, QT=QT: e.memset(QT[:], 0.0), writes=[TQT])
                    P.pool(lambda e, KT=KT: e.memset(KT[:], 0.0), writes=[TKT])
                    P.pool(lambda e, KT=KT: e.memset(KT[32:33, :], 1.0), writes=[TKT])
                    P.pool(lambda e, Va=Va: e.memset(Va[:], 0.0), writes=[TVa])
                    lc = 64 if b == 0 else 0
                    P.pool(lambda e, Va=Va, lc=lc: e.memset(Va[:, :, lc:lc + 1], 1.0), writes=[TVa])

                def head_ctx(h):
                    b = h % 2
                    return dict(h=h, b=b, vb=64 * b, lrow=(64 if b == 0 else 0), M=(65 if b == 0 else 128),
                                QT=QTs[b][0], TQT=QTs[b][1], KT=KTs[b][0], TKT=KTs[b][1], Va=Vas[b][0], TVa=Vas[b][1], sq={})

                def prep_start(hc):
                    pass

                def prep_a(hc, i):
                    h, vb = hc["h"], hc["vb"]
                    QT, TQT, KT, TKT, Va, TVa = hc["QT"], hc["TQT"], hc["KT"], hc["TKT"], hc["Va"], hc["TVa"]
                    cols = slice(i * 512, (i + 1) * 512)
                    X, TX = pp.get()
                    for k in range(3):
                        P.pe(lambda e, k=k: e.matmul(X[:], lhsT=wq[:, k, h * 256:h * 256 + 128], rhs=qnT[:, k, cols],
                                                     start=(k == 0), stop=(k == 2)), reads=[Twq, TqnT], writes=[TX])
                    Y, TY = pp.get()
                    for k in range(2):
                        P.pe(lambda e, k=k: e.matmul(Y[:], lhsT=wkv[:, k, h * 192:h * 192 + 128], rhs=kvnT[:, k, cols],
                                                     start=(k == 0), stop=False), reads=[Twkv, TkvnT], writes=[TY])
                    for k in range(3):
                        P.pe(lambda e, k=k: e.matmul(Y[:], lhsT=wq[:, k, h * 256 + 128:h * 256 + 256], rhs=qnT[:, k, cols],
                                                     start=False, stop=(k == 2)), reads=[Twq, TqnT], writes=[TY])
                    Z, TZ = pp.get()
                    for blk in range(4):
                        for k in range(2):
                            P.pe(lambda e, k=k, blk=blk: e.matmul(
                                Z[:, blk * 64:(blk + 1) * 64], lhsT=kvnT[:, k, i * 512 + blk * 128:i * 512 + (blk + 1) * 128],
                                rhs=wkv[:, k, h * 192 + 128:h * 192 + 192], start=(k == 0), stop=(k == 1)),
                                reads=[Twkv, TkvnT], writes=[TZ])
                    P.dve(lambda e: e.tensor_scalar(out=QT[64:128, cols], in0=X[64:128, :], scalar1=SCALE, scalar2=None, op0=ALU.mult),
                          reads=[TX], writes=[TQT])
                    P.dve(lambda e: e.scalar_tensor_tensor(out=a1[:], in0=X[0:32, :], scalar=SCALE, in1=cosT[:, cols],
                                                           op0=ALU.mult, op1=ALU.mult), reads=[TX, TcosT], writes=[Ta1])
                    P.dve(lambda e: e.scalar_tensor_tensor(out=a2[:], in0=Y[0:32, :], scalar=SCALE, in1=sinT[:, cols],
                                                           op0=ALU.mult, op1=ALU.mult), reads=[TY, TsinT], writes=[Ta2])
                    P.dve(lambda e: e.tensor_tensor(out=QT[0:32, cols], in0=a1[:], in1=a2[:], op=ALU.add),
                          reads=[Ta1, Ta2], writes=[TQT])
                    P.dve(lambda e: e.tensor_copy(out=KT[64:128, cols], in_=Y[64:128, :]), reads=[TY], writes=[TKT])
                    P.dve(lambda e: e.tensor_copy(out=KT[0:32, cols], in_=krT[:, cols]), reads=[TkrT], writes=[TKT])
                    P.dve(lambda e: e.tensor_copy(out=Va[:, 4 * i:4 * i + 4, vb:vb + 64], in_=Z[:, 0:256].rearrange("p (b v) -> p b v", v=64)),
                          reads=[TZ], writes=[TVa])
                    sqq, Tsqq = sqqr.get()
                    sqk, Tsqk = sqkr.get()
                    P.dve(lambda e: e.tensor_tensor(out=sqq[0:32, :], in0=QT[0:32, cols], in1=QT[0:32, cols], op=ALU.mult), reads=[TQT], writes=[Tsqq])
                    P.dve(lambda e: e.tensor_tensor(out=sqq[64:128, :], in0=QT[64:128, cols], in1=QT[64:128, cols], op=ALU.mult), reads=[TQT], writes=[Tsqq])
                    P.dve(lambda e: e.tensor_tensor(out=sqk[0:32, :], in0=KT[0:32, cols], in1=KT[0:32, cols], op=ALU.mult), reads=[TKT], writes=[Tsqk])
                    P.dve(lambda e: e.tensor_tensor(out=sqk[64:128, :], in0=KT[64:128, cols], in1=KT[64:128, cols], op=ALU.mult), reads=[TKT], writes=[Tsqk])
                    hc["sq"][i] = (sqq, Tsqq, sqk, Tsqk)

                def prep_b(hc, i):
                    sqq, Tsqq, sqk, Tsqk = hc["sq"].pop(i)
                    mx, Tmx = mxs[hc["b"]]
                    for (sq_, Tsq_, col) in ((sqq, Tsqq, i), (sqk, Tsqk, 16 + i)):
                        pss, Tpss = pp.get()
                        P.pe(lambda e, pss=pss, sq_=sq_: e.matmul(pss[0:33, :], lhsT=onesb[0:32, 0:33], rhs=sq_[0:32, :], start=True, stop=False),
                             reads=[Tsq_, Tonesb], writes=[Tpss])
                        P.pe(lambda e, pss=pss, sq_=sq_: e.matmul(pss[0:33, :], lhsT=onesb[64:128, 0:33], rhs=sq_[64:128, :], start=False, stop=True),
                             reads=[Tsq_, Tonesb], writes=[Tpss])
                        P.dve(lambda e, pss=pss, col=col: e.reduce_max(out=mx[:, col:col + 1], in_=pss[0:33, :], axis=AX.X), reads=[Tpss], writes=[Tmx])

                def prep_finish(hc):
                    QT, TQT, KT, TKT = hc["QT"], hc["TQT"], hc["KT"], hc["TKT"]
                    mx, Tmx = mxs[hc["b"]]
                    prep_b(hc, NT - 1)
                    P.dve(lambda e: e.reduce_max(out=mx[:, 8:9], in_=mx[:, 0:8], axis=AX.X), reads=[Tmx], writes=[Tmx])
                    P.dve(lambda e: e.reduce_max(out=mx[:, 24:25], in_=mx[:, 16:24], axis=AX.X), reads=[Tmx], writes=[Tmx])
                    P.dve(lambda e: e.tensor_tensor(out=mx[:, 25:26], in0=mx[:, 8:9], in1=mx[:, 24:25], op=ALU.mult), reads=[Tmx], writes=[Tmx])
                    P.act(lambda e: e.activation(out=mx[:, 26:27], in_=mx[:, 25:26], func=AF.Ln), reads=[Tmx], writes=[Tmx])
                    P.act(lambda e: e.activation(out=mx[:, 27:28], in_=mx[:, 26:27], func=AF.Exp, scale=0.5), reads=[Tmx], writes=[Tmx])
                    P.dve(lambda e: e.tensor_scalar(out=mx[:, 28:29], in0=mx[:, 27:28], scalar1=-1.05, scalar2=None, op0=ALU.mult),
                          reads=[Tmx], writes=[Tmx])
                    P.dve(lambda e: e.tensor_scalar(out=QT[32:33, :], in0=KT[32:33, :], scalar1=mx[32:33, 28:29], scalar2=None, op0=ALU.mult),
                          reads=[Tmx, TKT], writes=[TQT])

                def attention(hc, hook, LA=2, EPI_DELAY=2, EPI_DELAY2=5):
                    h, vb, lrow, M = hc["h"], hc["vb"], hc["lrow"], hc["M"]
                    QT, TQT, KT, TKT, Va, TVa = hc["QT"], hc["TQT"], hc["KT"], hc["TKT"], hc["Va"], hc["TVa"]
                    steps = [(i, kb) for i in range(NT) for kb in range(4 * i + 4)]
                    ctx = {}
                    Ob = {}
                    pending = []

                    def score(i, kb):
                        c0 = max(0, kb - 4 * i) * 128
                        n = 512 - c0
                        ps_, Tps = pps.get()
                        P.pe(lambda e: e.matmul(ps_[:, 0:n], lhsT=KT[:, kb * 128:(kb + 1) * 128], rhs=QT[:, i * 512 + c0:(i + 1) * 512], start=True, stop=True),
                             reads=[TKT, TQT], writes=[Tps])
                        ctx[(i, kb)] = (ps_, Tps, c0, n)

                    def rest(idx, i, kb):
                        ps_, Tps, c0, n = ctx.pop((i, kb))
                        nkb = 4 * i + 4
                        if kb == 0:
                            Ob[i] = po.get()
                        O, TO = Ob[i]
                        pt, Tpt = ptr.get()
                        P.act(lambda e: e.activation(out=pt[:, 0:n], in_=ps_[:, 0:n], func=AF.Exp), reads=[Tps], writes=[Tpt])
                        if kb >= 4 * i:
                            P.pool(lambda e: e.tensor_tensor(out=pt[:, 0:128], in0=pt[:, 0:128], in1=trib[:], op=ALU.mult),
                                   reads=[Tpt, Ttrib], writes=[Tpt])
                        P.pe(lambda e: e.matmul(O[0:M, c0:512], lhsT=Va[:, kb, 0:M], rhs=pt[:, 0:n], start=(kb == 0), stop=(kb == nkb - 1)),
                             reads=[TVa, Tpt], writes=[TO])
                        if kb == nkb - 1:
                            st = epi_a(i)
                            pending.append((idx + EPI_DELAY, lambda: epi_b(i, st)))
                        if hook is not None:
                            hook(i, kb, nkb)

                    def epi_a(i):
                        O, TO = Ob.pop(i)
                        osq, Tosq = osqr.get()
                        lsq, Tlsq = lsqr.get()
                        P.act(lambda e: e.activation(out=osq[vb:vb + 64, :], in_=O[vb:vb + 64, :], func=AF.Square), reads=[TO], writes=[Tosq])
                        lrw, Tlrw = lrwr.get()
                        P.dve(lambda e: e.tensor_copy(out=lrw[lrow:lrow + 1, :], in_=O[lrow:lrow + 1, :]), reads=[TO], writes=[Tlrw])
                        P.dve(lambda e: e.scalar_tensor_tensor(out=lsq[lrow:lrow + 1, :], in0=lrw[lrow:lrow + 1, :], scalar=64 * EPS, in1=lrw[lrow:lrow + 1, :],
                                                               op0=ALU.mult, op1=ALU.mult), reads=[Tlrw], writes=[Tlsq])
                        return (O, TO, osq, Tosq, lsq, Tlsq)

                    def epi_b(i, st):
                        O, TO, osq, Tosq, lsq, Tlsq = st
                        pn_, Tpn = pp.get()
                        P.pe(lambda e: e.matmul(pn_[:], lhsT=onesb[vb:vb + 64, :], rhs=osq[vb:vb + 64, :], start=True, stop=False),
                             reads=[Tosq, Tonesb], writes=[Tpn])
                        P.pe(lambda e: e.matmul(pn_[:], lhsT=onesb[lrow:lrow + 1, :], rhs=lsq[lrow:lrow + 1, :], start=False, stop=True),
                             reads=[Tlsq, Tonesb], writes=[Tpn])
                        pending.append((pending_idx[0] + EPI_DELAY2, lambda: epi_c(i, st, pn_, Tpn)))
                        pending.sort(key=lambda t: t[0])

                    def epi_c(i, st, pn_, Tpn):
                        O, TO, osq, Tosq, lsq, Tlsq = st
                        cols = slice(i * 512, (i + 1) * 512)
                        rs, Trs = rsr.get()
                        P.act(lambda e: e.activation(out=rs[vb:vb + 64, :], in_=pn_[vb:vb + 64, :], func=AF.Ln, scale=1.0 / 64), reads=[Tpn], writes=[Trs])
                        P.act(lambda e: e.activation(out=rs[vb:vb + 64, :], in_=rs[vb:vb + 64, :], func=AF.Exp, scale=-0.5), reads=[Trs], writes=[Trs])
                        P.dve(lambda e: e.scalar_tensor_tensor(
                            out=yTa[vb:vb + 64, h // 2, cols], in0=O[vb:vb + 64, :], scalar=gomla[vb:vb + 64, h // 2:h // 2 + 1], in1=rs[vb:vb + 64, :],
                            op0=ALU.mult, op1=ALU.mult), reads=[TO, Trs, Tpar], writes=[TyTa])

                    pending_idx = [0]
                    for idx in range(len(steps) + LA):
                        pending_idx[0] = idx
                        if idx < len(steps):
                            score(*steps[idx])
                        if idx >= LA:
                            rest(idx, *steps[idx - LA])
                        while pending and pending[0][0] <= idx:
                            pending.pop(0)[1]()
                    while pending:
                        pending_idx[0] += 1
                        pending.pop(0)[1]()

                hcs = [head_ctx(h) for h in range(8)]
                prep_start(hcs[0])
                for i in range(NT):
                    prep_a(hcs[0], i)
                    if i > 0:
                        prep_b(hcs[0], i - 1)
                prep_finish(hcs[0])
                for h in range(8):
                    if h + 1 < 8:
                        nxt = hcs[h + 1]
                        prep_start(nxt)
                        prep_a(nxt, 0)

                        def hook(i, kb, nkb, nxt=nxt):
                            if kb == nkb - 1 and i < NT - 1:
                                prep_b(nxt, i)
                                prep_a(nxt, i + 1)
                            if i == NT - 1 and kb == 6:
                                prep_finish(nxt)
                        attention(hcs[h], hook)
                    else:
                        attention(hcs[h], None)
                    if h == 0:
                        dump(P, "QT0", hcs[0]["QT"][:, 0:512], hcs[0]["TQT"])
                        dump(P, "KT0", hcs[0]["KT"][:, 0:512], hcs[0]["TKT"])
                dump(P, "yTa0", yTa[:, 0, 0:1024], TyTa)
                dump(P, "yTa3", yTa[:, 3, 3072:4096], TyTa)
                finals = P.emit(nc, semstack, finals)

        yTb, TyTb = sbt(top, "yTb", [128, 4, S], BF16)
        if "A3" in phases:
            with ExitStack() as ph:
                P = Prog()
                winm, Twinm = sbt(ph, "winm", [128, 8, 1544], BF16)
                xbr = Ring([sbt(ph, "m_xb%d" % i, [128, D]) for i in range(4)])
                hTs = [(sbt(ph, "m_hT%d" % b_, [128, 8, 512], BF16)[0], [Tl("m_hT%d_%d" % (b_, c)) for c in range(8)]) for b_ in range(2)]
                fe = make_fe(ph, nxn=2, npt=1)
                zqk = sbt(ph, "zqk", [64, 8, 516], BF16)[0]
                dgw, Tdgw = sbt(ph, "dgw", [64, 32, 64], BF16)
                ser = Ring([sbt(ph, "se%d" % i, [64, 512]) for i in range(2)])
                nconvb, Tnconvb = sbt(ph, "nconvb", [64, 8])
                Tz = [Tl("z%d" % g) for g in range(8)]
                qT, TqT = sbt(ph, "m_qT", [64, 4, 512], BF16)
                kTf, TkTf = sbt(ph, "kTf", [64, 4, 512])
                kpT, TkpT = sbt(ph, "kpT", [64, 4, 512], BF16)
                kptok, Tkptok = sbt(ph, "kptok", [128, 16, 64], BF16)
                vaug, Tvaug = sbt(ph, "vaug", [128, 4, 4, 129], BF16)
                sigo, Tsigo = sbt(ph, "sigo", [128, 4, 512])
                g_ig, Tg_ig = sbt(ph, "g_ig", [4, 512])
                g_x, Tg_x = sbt(ph, "g_x", [4, 512])
                g_a, Tg_a = sbt(ph, "g_a", [4, 512])
                g_b, Tg_b = sbt(ph, "g_b", [4, 512])
                g_lf, Tg_lf = sbt(ph, "g_lf", [4, 512])
                g_bc, Tg_bc = sbt(ph, "g_bc", [4, 512])
                g_u, Tg_u = sbt(ph, "g_u", [4, 512])
                g_w, Tg_w = sbt(ph, "g_w", [4, 512])
                g_cl, Tg_cl = sbt(ph, "g_cl", [4, 512])
                msk, Tmsk = sbt(ph, "msk", [4, 512])
                sm, Tsm = sbt(ph, "sm", [4, 64])
                clampc, Tclampc = sbt(ph, "clampc", [128, 16])
                abc, Tabc = sbt(ph, "abc", [64, 16])
                Sst, TS = sbt(ph, "Sst", [64, 4, 129])
                Sh, TSh = sbt(ph, "Sh", [64, 4, 129])
                Shb, TShb = sbt(ph, "Shb", [64, 4, 129], BF16)
                sTm, TsTm = sbt(ph, "sTm", [128, 4, 128], BF16)
                ybr = Ring([sbt(ph, "yb%d" % i, [128, 128]) for i in range(4)])
                ybnr = Ring([sbt(ph, "ybn%d" % i, [128, 128], BF16) for i in range(4)])
                junk2, Tjunk2 = sbt(ph, "junk2", [128, 128], BF16)
                dtmp, Tdtmp = sbt(ph, "dtmp", [128, 8])
                ss2, Tss2 = sbt(ph, "ss2", [128, 8])
                ptk, Tptk = pst(ph, "ptk", [128, 1024], BF16)
                pty, Tpty = pst(ph, "pty", [128, 1024], BF16)
                pr = Ring([pst(ph, "pr%d" % i, [128, 512]) for i in range(5)])

                P.dma("gpsimd", winm[:], winm_d.rearrange("(k p) n -> p k n", p=128), writes=[Twinm])
                P.pool(lambda e: e.memset(zqk[:], 0.0), writes=Tz)
                P.dve(lambda e: e.tensor_scalar(out=nconvb[:], in0=convb[:], scalar1=-1.0, scalar2=None, op0=ALU.mult), writes=[Tnconvb])
                for gj in range(32):
                    P.dve(lambda e, gj=gj: e.tensor_scalar(out=dgw[:, gj, :], in0=identb[0:64, 0:64], scalar1=convw[:, gj:gj + 1], scalar2=None, op0=ALU.mult),
                          writes=[Tdgw])
                P.pool(lambda e: e.memset(Sst[:], 0.0), writes=[TS])
                P.pool(lambda e: e.memset(sm[:], 0.0), writes=[Tsm])
                P.pool(lambda e: e.memset(vaug[:], 1.0), writes=[Tvaug])
                P.pool(lambda e: e.memset(msk[:], 1.0), writes=[Tmsk])
                P.pool(lambda e: e.memset(msk[:].rearrange("p (c t) -> p c t", t=128)[:, :, 0:1], 0.0), writes=[Tmsk])

                def v3(ap):
                    return ap.rearrange("p (c t) -> p c t", t=128)

                def a3_front(i):
                    fe_tile(P, fe, xbr, i, scA, shA, TscA, hTs[i % 2][0], hTs[i % 2][1])

                def a3_tile(i):
                    hT, ThT = hTs[i % 2]

                    def proj(c0, c1, M):
                        pt, Tpt = pr.get()
                        for k in range(8):
                            P.pe(lambda e, k=k: e.matmul(pt[0:M, :], lhsT=winm[:, k, c0:c1], rhs=hT[:, k, :], start=(k == 0), stop=(k == 7)),
                                 reads=[Twinm, ThT[k]], writes=[Tpt])
                        return pt, Tpt

                    def qk_proj(g):
                        def f():
                            pt, Tpt = proj(g * 64, (g + 1) * 64, 64)
                            P.act(lambda e: e.copy(out=zqk[:, g, 4:516], in_=pt[0:64, :]), reads=[Tpt], writes=[Tz[g]])
                        return f

                    def qk_conv(g):
                        def f():
                            pc_, Tpc_ = pr.get()
                            for j in range(4):
                                P.pe(lambda e, j=j: e.matmul(pc_[0:64, :], lhsT=dgw[:, g * 4 + j, :], rhs=zqk[:, g, 1 + j:1 + j + 512], start=(j == 0), stop=(j == 3)),
                                     reads=[Tz[g], Tdgw], writes=[Tpc_])
                            se, Tse = ser.get()
                            P.act(lambda e: e.activation(out=se[:], in_=pc_[0:64, :], func=AF.Exp, bias=nconvb[:, g:g + 1], scale=-1.0), reads=[Tpc_], writes=[Tse])
                            P.act(lambda e: e.activation(out=se[:], in_=se[:], func=AF.Ln, bias=1.0), reads=[Tse], writes=[Tse])
                            P.act(lambda e: e.activation(out=se[:], in_=se[:], func=AF.Exp, scale=-1.0), reads=[Tse], writes=[Tse])
                            if g < 4:
                                P.dve(lambda e: e.scalar_tensor_tensor(out=qT[:, g, :], in0=pc_[0:64, :], scalar=convb[:, g:g + 1], in1=se[:], op0=ALU.add, op1=ALU.mult),
                                      reads=[Tpc_, Tse], writes=[TqT])
                            else:
                                P.dve(lambda e: e.scalar_tensor_tensor(out=kTf[:, g - 4, :], in0=pc_[0:64, :], scalar=convb[:, g:g + 1], in1=se[:], op0=ALU.add, op1=ALU.mult),
                                      reads=[Tpc_, Tse], writes=[TkTf])
                            P.dve(lambda e: e.tensor_copy(out=zqk[:, g, 1:4], in_=zqk[:, g, 513:516]), reads=[Tz[g]], writes=[Tz[g]])
                        return f
                    QK = [qk_proj(0)]
                    for g in range(8):
                        if g + 1 < 8:
                            QK.append(qk_proj(g + 1))
                        QK.append(qk_conv(g))

                    Rb = sm[:, 16:20].unsqueeze(2).to_broadcast([4, 4, 128])
                    gst = {}

                    def g_proj():
                        gst["pi"] = proj(1536, 1540, 4)
                        gst["pf"] = proj(1540, 1544, 4)
                        pi, Tpi = gst["pi"]
                        pf, Tpf = gst["pf"]
                        P.dve(lambda e: e.tensor_scalar(out=g_ig[:], in0=pi[0:4, :], scalar1=bg[:, 0:1], scalar2=None, op0=ALU.add), reads=[Tpi], writes=[Tg_ig])
                        P.dve(lambda e: e.tensor_scalar(out=g_x[:], in0=pf[0:4, :], scalar1=bg[:, 1:2], scalar2=-1.0, op0=ALU.add, op1=ALU.mult), reads=[Tpf], writes=[Tg_x])
                    G = [
                        g_proj,
                        lambda: P.dve(lambda e: e.scalar_tensor_tensor(out=g_a[:], in0=g_x[:], scalar=-1.0, in1=g_x[:], op0=ALU.mult, op1=ALU.max), reads=[Tg_x], writes=[Tg_a]),
                        lambda: P.act(lambda e: e.activation(out=g_b[:], in_=g_a[:], func=AF.Exp, scale=-1.0), reads=[Tg_a], writes=[Tg_b]),
                        lambda: P.act(lambda e: e.activation(out=g_b[:], in_=g_b[:], func=AF.Ln, bias=1.0), reads=[Tg_b], writes=[Tg_b]),
                        lambda: P.dve(lambda e: e.tensor_scalar(out=g_a[:], in0=g_x[:], scalar1=0.0, scalar2=None, op0=ALU.max), reads=[Tg_x, Tg_b], writes=[Tg_a]),
                        lambda: P.dve(lambda e: e.scalar_tensor_tensor(out=g_lf[:], in0=g_b[:], scalar=-1.0, in1=g_a[:], op0=ALU.mult, op1=ALU.subtract),
                                      reads=[Tg_b, Tg_a], writes=[Tg_lf]),
                        lambda: P.dve(lambda e: e.tensor_tensor_scan(out=g_bc[:], data0=msk[:], data1=g_lf[:], initial=0.0, op0=ALU.mult, op1=ALU.add),
                                      reads=[Tmsk, Tg_lf], writes=[Tg_bc]),
                        lambda: P.dve(lambda e: e.tensor_tensor(out=g_u[:], in0=g_ig[:], in1=g_bc[:], op=ALU.subtract), reads=[Tg_ig, Tg_bc], writes=[Tg_u]),
                        lambda: P.dve(lambda e: e.tensor_reduce(out=sm[:, 0:4], in_=v3(g_u[:]), axis=AX.X, op=ALU.max), reads=[Tg_u], writes=[Tsm]),
                        lambda: P.dve(lambda e: e.tensor_copy(out=sm[:, 4:8].unsqueeze(2), in_=v3(g_bc[:])[:, :, 127:128]), reads=[Tg_bc, Tsm], writes=[Tsm]),
                        lambda: P.dve(lambda e: e.tensor_tensor_scan(out=sm[:, 8:12], data0=sm[:, 0:4], data1=sm[:, 4:8], initial=sm[:, 48:49], op0=ALU.max, op1=ALU.add),
                                      reads=[Tsm], writes=[Tsm]),
                        lambda: P.dve(lambda e: e.tensor_copy(out=sm[:, 12:13], in_=sm[:, 48:49]), reads=[Tsm], writes=[Tsm]),
                        lambda: P.dve(lambda e: e.tensor_copy(out=sm[:, 13:16], in_=sm[:, 8:11]), reads=[Tsm], writes=[Tsm]),
                        lambda: P.dve(lambda e: e.tensor_copy(out=sm[:, 48:49], in_=sm[:, 11:12]), reads=[Tsm], writes=[Tsm]),
                        lambda: P.dve(lambda e: e.tensor_tensor(out=sm[:, 16:20], in0=sm[:, 12:16], in1=sm[:, 0:4], op=ALU.max), reads=[Tsm], writes=[Tsm]),
                        lambda: P.dve(lambda e: e.tensor_tensor(out=sm[:, 20:24], in0=sm[:, 12:16], in1=sm[:, 16:20], op=ALU.subtract), reads=[Tsm], writes=[Tsm]),
                        lambda: P.act(lambda e: e.activation(out=sm[:, 24:28], in_=sm[:, 20:24], func=AF.Exp), reads=[Tsm], writes=[Tsm]),
                        lambda: P.dve(lambda e: e.tensor_tensor(out=v3(g_w[:]), in0=v3(g_u[:]), in1=Rb, op=ALU.subtract), reads=[Tg_u, Tsm], writes=[Tg_w]),
                        lambda: P.act(lambda e: e.activation(out=g_w[:], in_=g_w[:], func=AF.Exp), reads=[Tg_w], writes=[Tg_w]),
                        lambda: P.dve(lambda e: e.tensor_tensor(out=v3(g_cl[:]), in0=v3(g_bc[:]), in1=Rb, op=ALU.add), reads=[Tg_bc, Tsm], writes=[Tg_cl]),
                        lambda: P.act(lambda e: e.activation(out=g_cl[:], in_=g_cl[:], func=AF.Exp, scale=-1.0), reads=[Tg_cl], writes=[Tg_cl]),
                    ]

                    def v_proj(c):
                        def f():
                            pv, Tpv = pr.get()
                            for k in range(8):
                                P.pe(lambda e, k=k: e.matmul(pv[:], lhsT=hT[:, k, c * 128:(c + 1) * 128], rhs=winm[:, k, 512:1024], start=(k == 0), stop=(k == 7)),
                                     reads=[Twinm, ThT[k]], writes=[Tpv])
                            P.act(lambda e: e.copy(out=vaug[:, c, :, 0:128], in_=pv[:].rearrange("p (h v) -> p h v", v=128)), reads=[Tpv], writes=[Tvaug])
                        return f

                    def o_proj(c):
                        def f():
                            po_, Tpo = pr.get()
                            for k in range(8):
                                P.pe(lambda e, k=k: e.matmul(po_[:], lhsT=hT[:, k, c * 128:(c + 1) * 128], rhs=winm[:, k, 1024:1536], start=(k == 0), stop=(k == 7)),
                                     reads=[Twinm, ThT[k]], writes=[Tpo])
                            P.act(lambda e: e.activation(out=sigo[:, c, :], in_=po_[:], func=AF.Exp, scale=-1.0), reads=[Tpo], writes=[Tsigo])
                            P.act(lambda e: e.activation(out=sigo[:, c, :], in_=sigo[:, c, :], func=AF.Ln, bias=1.0), reads=[Tsigo], writes=[Tsigo])
                            P.act(lambda e: e.activation(out=sigo[:, c, :], in_=sigo[:, c, :], func=AF.Exp, scale=-1.0), reads=[Tsigo], writes=[Tsigo])
                        return f
                    VO = []
                    for c in range(4):
                        VO.append(v_proj(c))
                        VO.append(o_proj(c))

                    FE = []
                    if i + 1 < NT:
                        FE = fe_tile_thunks(P, fe, xbr, i + 1, scA, shA, TscA, hTs[(i + 1) % 2][0], hTs[(i + 1) % 2][1])
                    merge([st_ for st_ in (QK, G, VO, FE) if st_])

                    def kscale(h):
                        pw, Tpw = pr.get()
                        P.pe(lambda e: e.matmul(pw[0:64, :], lhsT=sel4[0:4, h * 64:(h + 1) * 64], rhs=g_w[0:4, :], start=True, stop=True),
                             reads=[Tg_w], writes=[Tpw])
                        P.dve(lambda e: e.scalar_tensor_tensor(out=kpT[:, h, :], in0=kTf[:, h, :], scalar=0.125, in1=pw[0:64, :], op0=ALU.mult, op1=ALU.mult),
                              reads=[TkTf, Tpw], writes=[TkpT])
                    for h in range(4):
                        kscale(h)
                    pc, Tpc = pr.get()
                    for c in range(4):
                        P.pe(lambda e, c=c: e.matmul(pc[:, c * 4:(c + 1) * 4], lhsT=g_cl[0:4, c * 128:(c + 1) * 128], rhs=identf[0:4, 0:4], start=True, stop=True),
                             reads=[Tg_cl], writes=[Tpc])
                    P.act(lambda e: e.copy(out=clampc[:], in_=pc[:, 0:16]), reads=[Tpc], writes=[Tclampc])
                    P.dve(lambda e: e.tensor_tensor(out=sm[:, 32:48].rearrange("p (c h) -> p c h", h=4), in0=sm[:, 24:28].unsqueeze(2).to_broadcast([4, 4, 4]),
                                                    in1=identf[0:4, 0:4].unsqueeze(1).to_broadcast([4, 4, 4]), op=ALU.mult), reads=[Tsm], writes=[Tsm])
                    pa_, Tpa = pr.get()
                    P.pe(lambda e: e.matmul(pa_[0:64, 0:16], lhsT=onesf[0:4, 0:64], rhs=sm[0:4, 32:48], start=True, stop=True), reads=[Tsm], writes=[Tpa])
                    P.act(lambda e: e.copy(out=abc[:], in_=pa_[0:64, 0:16]), reads=[Tpa], writes=[Tabc])
                    for c in range(4):
                        for h in range(4):
                            P.pe(lambda e, c=c, h=h: e.transpose(ptk[:, (c * 4 + h) * 64:(c * 4 + h + 1) * 64], kpT[0:64, h, c * 128:(c + 1) * 128], identb[0:64, 0:64]),
                                 reads=[TkpT], writes=[Tptk])
                    P.act(lambda e: e.copy(out=kptok[:].rearrange("p a b -> p (a b)"), in_=ptk[:]), reads=[Tptk], writes=[Tkptok])
                    cst = {}

                    def X(c):
                        cc = slice(c * 128, (c + 1) * 128)
                        P.dve(lambda e: e.tensor_tensor(out=Sh[:], in0=Sst[:], in1=abc[:, c * 4:(c + 1) * 4].unsqueeze(2).to_broadcast([64, 4, 129]), op=ALU.mult),
                              reads=[TS, Tabc], writes=[TSh])
                        P.act(lambda e: e.copy(out=Shb[:], in_=Sh[:]), reads=[TSh], writes=[TShb])
                        ps_, Tps = pr.get()
                        for h in range(4):
                            P.pe(lambda e, h=h: e.matmul(ps_[:, h * 128:(h + 1) * 128], lhsT=kpT[0:64, h, cc], rhs=qT[0:64, h, cc], start=True, stop=True),
                                 reads=[TkpT, TqT], writes=[Tps])
                        P.dve(lambda e: e.tensor_tensor(out=sTm[:], in0=ps_[:].rearrange("p (h t) -> p h t", t=128),
                                                        in1=trib[:].unsqueeze(1).to_broadcast([128, 4, 128]), op=ALU.mult), reads=[Tps], writes=[TsTm])
                        pns = []
                        for j in range(2):
                            pn, Tpn = pr.get()
                            pd, Tpd = pr.get()
                            for hh in range(2):
                                h = 2 * j + hh
                                P.pe(lambda e, h=h, hh=hh, pn=pn: e.matmul(pn[:, hh * 129:(hh + 1) * 129], lhsT=sTm[:, h, :], rhs=vaug[:, c, h, :], start=True, stop=False),
                                     reads=[TsTm, Tvaug], writes=[Tpn])
                                P.pe(lambda e, h=h, hh=hh, pn=pn: e.matmul(pn[:, hh * 129:(hh + 1) * 129], lhsT=qT[0:64, h, cc], rhs=Shb[0:64, h, :], start=False, stop=True),
                                     reads=[TqT, TShb], writes=[Tpn])
                            for hh in range(2):
                                h = 2 * j + hh
                                P.pe(lambda e, h=h, hh=hh, pd=pd: e.matmul(pd[0:64, hh * 129:(hh + 1) * 129], lhsT=kptok[:, c * 4 + h, :], rhs=vaug[:, c, h, :], start=True, stop=True),
                                     reads=[Tkptok, Tvaug], writes=[Tpd])
                            P.dve(lambda e, j=j, pd=pd: e.tensor_tensor(out=Sst[:, 2 * j:2 * j + 2, :], in0=Sh[:, 2 * j:2 * j + 2, :],
                                                                        in1=pd[0:64, 0:258].rearrange("p (h v) -> p h v", v=129), op=ALU.add), reads=[TSh, Tpd], writes=[TS])
                            pns.append((pn, Tpn))
                        cst[c] = {"pns": pns, "ybs": []}

                    def D1(c):
                        for j in range(2):
                            pn, Tpn = cst[c]["pns"][j]
                            pnv = pn[:, 0:258].rearrange("p (h v) -> p h v", v=129)
                            P.dve(lambda e, pnv=pnv: e.tensor_copy(out=dtmp[:, 6:8].unsqueeze(2), in_=pnv[:, :, 128:129]), reads=[Tpn], writes=[Tdtmp])
                            P.dve(lambda e: e.scalar_tensor_tensor(out=dtmp[:, 0:2], in0=dtmp[:, 6:8], scalar=-1.0, in1=dtmp[:, 6:8],
                                                                   op0=ALU.mult, op1=ALU.max), reads=[Tdtmp], writes=[Tdtmp])
                            P.dve(lambda e, j=j: e.tensor_tensor(out=dtmp[:, 2:4], in0=dtmp[:, 0:2], in1=clampc[:, c * 4 + 2 * j:c * 4 + 2 * j + 2], op=ALU.max),
                                  reads=[Tdtmp, Tclampc], writes=[Tdtmp])
                            P.dve(lambda e: e.reciprocal(out=dtmp[:, 4:6], in_=dtmp[:, 2:4]), reads=[Tdtmp], writes=[Tdtmp])
                            for hh in range(2):
                                h = 2 * j + hh
                                yb, Tyb = ybr.get()
                                P.dve(lambda e, h=h, hh=hh, yb=yb, pn=pn: e.scalar_tensor_tensor(out=yb[:], in0=pn[:, hh * 129:hh * 129 + 128], scalar=dtmp[:, 4 + hh:5 + hh],
                                                                                                  in1=sigo[:, c, h * 128:(h + 1) * 128], op0=ALU.mult, op1=ALU.mult),
                                      reads=[Tpn, Tdtmp, Tsigo], writes=[Tyb])
                                P.act(lambda e, h=h, yb=yb: e.activation(out=junk2[:], in_=yb[:], func=AF.Square, accum_out=ss2[:, h:h + 1]),
                                      reads=[Tyb], writes=[Tjunk2, Tss2])
                                cst[c]["ybs"].append((yb, Tyb))

                    def D2(c):
                        tok = slice(i * 512 + c * 128, i * 512 + (c + 1) * 128)
                        ybs = cst.pop(c)["ybs"]
                        P.dve(lambda e: e.tensor_scalar(out=ss2[:, 4:8], in0=ss2[:, 0:4], scalar1=1.0 / 128, scalar2=EPS, op0=ALU.mult, op1=ALU.add),
                              reads=[Tss2], writes=[Tss2])
                        P.act(lambda e: e.activation(out=ss2[:, 4:8], in_=ss2[:, 4:8], func=AF.Ln), reads=[Tss2], writes=[Tss2])
                        P.act(lambda e: e.activation(out=ss2[:, 4:8], in_=ss2[:, 4:8], func=AF.Exp, scale=-0.5), reads=[Tss2], writes=[Tss2])
                        for h in range(4):
                            yb, Tyb = ybs[h]
                            ybn, Tybn = ybnr.get()
                            P.dve(lambda e, h=h, yb=yb, ybn=ybn: e.tensor_scalar(out=ybn[:], in0=yb[:], scalar1=ss2[:, 4 + h:5 + h], scalar2=None, op0=ALU.mult),
                                  reads=[Tyb, Tss2], writes=[Tybn])
                            P.pe(lambda e, h=h, ybn=ybn: e.transpose(pty[:, h * 128:(h + 1) * 128], ybn[:], identb[:]), reads=[Tybn], writes=[Tpty])
                            P.act(lambda e, h=h: e.mul(out=yTb[:, h, tok], in_=pty[:, h * 128:(h + 1) * 128], mul=gomls[:, h:h + 1]), reads=[Tpty], writes=[TyTb])

                    X(0)
                    D1(0)
                    for c in range(1, 4):
                        X(c)
                        D2(c - 1)
                        D1(c)
                    D2(3)

                a3_front(0)
                for i in range(NT):
                    a3_tile(i)
                if "yTb" in dbg_outs:
                    P.dma("gpsimd", dbg_outs["yTb"], yTb[:], reads=[TyTb])
                finals = P.emit(nc, semstack, finals)

        if "B" in phases:
            with ExitStack() as ph:
                P = Prog()
                wout, Twout = sbt(ph, "wout", [128, 8, D], BF16)
                gta, Tgta = sbt(ph, "gta", [128, D])
                gtf, Tgtf = sbt(ph, "gtf", [128, D])
                gfin, Tgfin = sbt(ph, "gfinb", [128, D])
                dgr = Ring([sbt(ph, "dg%d" % i, [128, 128]) for i in range(2)])
                x1sets = [[sbt(ph, "x1_%d_%d" % (s_, i), [128, D]) for i in range(4)] for s_ in range(2)]
                hfT = sbt(ph, "hfT", [128, 8, 512], BF16)[0]
                ThfT = [Tl("hfT%d" % c) for c in range(8)]
                aT, TaT = sbt(ph, "aT", [128, NJ, 512], BF16)
                sgr = Ring([sbt(ph, "sg%d" % i, [128, 512]) for i in range(4)])
                wgr = Ring([sbt(ph, "wgp%d" % i, [128, 8, 256], BF16) for i in range(2)])
                wur = Ring([sbt(ph, "wup%d" % i, [128, 8, 256], BF16) for i in range(2)])
                wdr = Ring([sbt(ph, "wdp%d" % i, [128, 2, 512], BF16) for i in range(5)])
                fe = make_fe(ph, nxn=4, npt=2)
                pb = Ring([pst(ph, "pb%d" % i, [128, 512]) for i in range(6)])

                P.dma("gpsimd", wout[:], wout_d.rearrange("(k p) n -> p k n", p=128), writes=[Twout])
                P.dma("sync", gfin[:], gfin_d.to_broadcast([128, D]), writes=[Tgfin])

                def bcast(dst, Tdst, col0):
                    for c in range(8):
                        dg, Tdg = dgr.get()
                        P.dve(lambda e, c=c, dg=dg: e.tensor_scalar(out=dg[:], in0=identf[:], scalar1=modc[:, col0 + c:col0 + c + 1], scalar2=None, op0=ALU.mult),
                              writes=[Tdg])
                        pk_, Tpk_ = pb.get()
                        P.pe(lambda e, dg=dg, pk_=pk_: e.matmul(pk_[:, 0:128], lhsT=onesf[:], rhs=dg[:], start=True, stop=True), reads=[Tdg], writes=[Tpk_])
                        P.act(lambda e, c=c, pk_=pk_: e.copy(out=dst[:, c * 128:(c + 1) * 128], in_=pk_[:, 0:128]), reads=[Tpk_], writes=[Tdst])
                bcast(gta, Tgta, 16)
                bcast(gtf, Tgtf, 40)
                wg_v = wg_d.rearrange("(k p) n -> p k n", p=128)
                wu_v = wu_d.rearrange("(k p) n -> p k n", p=128)
                wd_v = wd_d.rearrange("(j p) n -> p j n", p=128)
                xns = {}

                def pro1(t):
                    xs = x1sets[t % 2]
                    for blk in range(4):
                        r0 = t * 512 + blk * 128
                        P.dma("sync", xs[blk][0][:], x_d[r0:r0 + 128, :], writes=[xs[blk][1]])

                    def outproj(blk, half):
                        hs = slice(half * 512, (half + 1) * 512)
                        xb, Txb = xs[blk]
                        po_, Tpo = pb.get()
                        tok = slice(t * 512 + blk * 128, t * 512 + (blk + 1) * 128)
                        for k in range(8):
                            src, Tsrc = (yTa, TyTa) if k < 4 else (yTb, TyTb)
                            P.pe(lambda e, k=k, src=src: e.matmul(po_[:], lhsT=src[:, k % 4, tok], rhs=wout[:, k, hs], start=(k == 0), stop=(k == 7)),
                                 reads=[Twout, Tsrc], writes=[Tpo])
                        sg, Tsg = sgr.get()
                        P.dve(lambda e: e.tensor_tensor(out=sg[:], in0=po_[:], in1=gta[:, hs], op=ALU.mult), reads=[Tpo, Tgta], writes=[Tsg])
                        P.dve(lambda e: e.tensor_tensor(out=xb[:, hs], in0=sg[:], in1=xb[:, hs], op=ALU.add), reads=[Tsg, Txb], writes=[Txb])
                    for blk in range(4):
                        for half in range(2):
                            outproj(blk, half)
                    xns[t] = [fe_stats(P, fe, xs[blk][0][:], xs[blk][1]) for blk in range(4)]
                    if t == 0 and "x1" in dbg_outs:
                        for blk in range(4):
                            P.dma("sync", dbg_outs["x1"][blk * 128:(blk + 1) * 128, :], xs[blk][0][:], reads=[xs[blk][1]])

                def pro2(t):
                    for blk, (xn, Txn) in enumerate(xns.pop(t)):
                        fe_trans(P, fe, xn, Txn, scF, shF, TscF, hfT, ThfT, blk)

                pieces = []
                for t_ in range(NT):
                    for jp in range(NJ // 2):
                        pieces.append(("g", t_, jp, 0))
                        pieces.append(("u", t_, jp, 0))
                    for half in range(2):
                        for jp in range(NJ // 2):
                            pieces.append(("d", t_, jp, half))
                wslot = {}
                wstate = {"issued": 0}

                def issue_upto(n):
                    while wstate["issued"] < min(n, len(pieces)):
                        kind, t_, jp, half = pieces[wstate["issued"]]
                        if kind == "g":
                            buf, Tb = wgr.get()
                            P.dma("gpsimd", buf[:], wg_v[:, :, jp * 256:(jp + 1) * 256], writes=[Tb])
                        elif kind == "u":
                            buf, Tb = wur.get()
                            P.dma("gpsimd", buf[:], wu_v[:, :, jp * 256:(jp + 1) * 256], writes=[Tb])
                        else:
                            buf, Tb = wdr.get()
                            P.dma("gpsimd", buf[:], wd_v[:, 2 * jp:2 * jp + 2, half * 512:(half + 1) * 512], writes=[Tb])
                        wslot[wstate["issued"]] = (buf, Tb)
                        wstate["issued"] += 1

                def take(kind, t_, jp, half):
                    n = wstate.setdefault("next", 0)
                    assert pieces[n] == (kind, t_, jp, half), (pieces[n], kind, t_, jp, half)
                    issue_upto(n + 1)
                    wstate["next"] = n + 1
                    return wslot.pop(n)

                def advance():
                    issue_upto(wstate.get("next", 0) + 4)

                def up(t, jp):
                    wgp, Twgp = take("g", t, jp, 0)
                    wup, Twup = take("u", t, jp, 0)
                    for jj in range(2):
                        j = 2 * jp + jj
                        pg, Tpg = pb.get()
                        pu, Tpu = pb.get()
                        for k in range(8):
                            P.pe(lambda e, k=k, jj=jj, pg=pg: e.matmul(pg[:], lhsT=wgp[:, k, jj * 128:(jj + 1) * 128], rhs=hfT[:, k, :], start=(k == 0), stop=(k == 7)),
                                 reads=[Twgp, ThfT[k]], writes=[Tpg])
                        for k in range(8):
                            P.pe(lambda e, k=k, jj=jj, pu=pu: e.matmul(pu[:], lhsT=wup[:, k, jj * 128:(jj + 1) * 128], rhs=hfT[:, k, :], start=(k == 0), stop=(k == 7)),
                                 reads=[Twup, ThfT[k]], writes=[Tpu])
                        sg, Tsg = sgr.get()
                        P.act(lambda e, pg=pg, sg=sg: e.activation(out=sg[:], in_=pg[:], func=AF.Silu), reads=[Tpg], writes=[Tsg])
                        P.dve(lambda e, j=j, pu=pu, sg=sg: e.tensor_tensor(out=aT[:, j, :], in0=sg[:], in1=pu[:], op=ALU.mult), reads=[Tsg, Tpu], writes=[TaT])
                    advance()

                def down(t, half):
                    xs = x1sets[t % 2]
                    hs = slice(half * 512, (half + 1) * 512)
                    accs = [pb.get() for _ in range(4)]

                    def piece(jp):
                        wdp, Twdp = take("d", t, jp, half)
                        for jj in range(2):
                            j = 2 * jp + jj
                            for blk in range(4):
                                P.pe(lambda e, j=j, jj=jj, blk=blk: e.matmul(accs[blk][0][:], lhsT=aT[:, j, blk * 128:(blk + 1) * 128], rhs=wdp[:, jj, :],
                                                                             start=(j == 0), stop=(j == NJ - 1)), reads=[TaT, Twdp], writes=[accs[blk][1]])
                        advance()
                    for jp in range(NJ // 2):
                        piece(jp)
                    evs = []
                    for blk in range(4):
                        sg, Tsg = sgr.get()
                        P.act(lambda e, blk=blk, sg=sg: e.copy(out=sg[:], in_=accs[blk][0][:]), reads=[accs[blk][1]], writes=[Tsg])
                        evs.append((sg, Tsg))
                    for blk in range(4):
                        xb, Txb = xs[blk]
                        sg, Tsg = evs[blk]
                        P.dve(lambda e, sg=sg: e.tensor_tensor(out=sg[:], in0=sg[:], in1=gtf[:, hs], op=ALU.mult), reads=[Tsg, Tgtf], writes=[Tsg])
                        P.dve(lambda e, xb=xb, sg=sg: e.tensor_tensor(out=xb[:, hs], in0=sg[:], in1=xb[:, hs], op=ALU.add), reads=[Tsg, Txb], writes=[Txb])

                def final(t, blk):
                    xb, Txb = x1sets[t % 2][blk]
                    st, Tst = fe["stat"].get()
                    junk, Tjunk = fe["junk"].get()
                    P.act(lambda e: e.activation(out=junk[:], in_=xb[:], func=AF.Square, accum_out=st[:, 0:1]), reads=[Txb], writes=[Tjunk, Tst])
                    P.dve(lambda e: e.tensor_scalar(out=st[:, 1:2], in0=st[:, 0:1], scalar1=1.0 / D, scalar2=EPS, op0=ALU.mult, op1=ALU.add), reads=[Tst], writes=[Tst])
                    P.act(lambda e: e.activation(out=st[:, 2:3], in_=st[:, 1:2], func=AF.Sqrt), reads=[Tst], writes=[Tst])
                    P.dve(lambda e: e.reciprocal(out=st[:, 3:4], in_=st[:, 2:3]), reads=[Tst], writes=[Tst])
                    P.dve(lambda e: e.scalar_tensor_tensor(out=xb[:], in0=xb[:], scalar=st[:, 3:4], in1=gfin[:], op0=ALU.mult, op1=ALU.mult),
                          reads=[Txb, Tst, Tgfin], writes=[Txb])
                    r0 = t * 512 + blk * 128
                    P.dma("sync", out_d[r0:r0 + 128, :], xb[:], reads=[Txb])

                advance()
                pro1(0)
                pro2(0)
                for i in range(NT):
                    for jp in range(NJ // 2):
                        up(i, jp)
                        if jp == 3 and i + 1 < NT:
                            pro1(i + 1)
                    if i + 1 < NT:
                        pro2(i + 1)
                    down(i, 0)
                    down(i, 1)
                    for blk in range(4):
                        final(i, blk)
                finals = P.emit(nc, semstack, finals)
    return nc


def _col(v, n):
    return np.ascontiguousarray(np.asarray(v, np.float32).reshape(n, 128).T)


def shared_inputs(inp):
    f = lambda a: np.ascontiguousarray(np.asarray(a, np.float32))
    w_in = f(inp["w_in"][0])
    winl = np.concatenate([w_in[:, 0:672], w_in[:, 656:672], w_in[:, 640:656]], axis=1)
    winm = w_in[:, 672:2216]
    w_uq = f(inp["w_uq"][0]).reshape(384, 8, 96)
    wq = np.zeros((384, 8, 256), np.float32)
    wq[:, :, 0:32] = w_uq[:, :, 64:96]
    wq[:, :, 64:128] = w_uq[:, :, 0:64]
    wq[:, :, 128:144] = w_uq[:, :, 80:96]
    wq[:, :, 144:160] = w_uq[:, :, 64:80]
    w_ukv = f(inp["w_ukv"][0]).reshape(256, 8, 128)
    wkv = np.zeros((256, 8, 192), np.float32)
    wkv[:, :, 64:128] = w_ukv[:, :, 0:64]
    wkv[:, :, 128:192] = w_ukv[:, :, 64:128]
    conv_w = f(inp["conv_w"][0])
    convw = np.ascontiguousarray(conv_w.T.reshape(8, 64, 4).transpose(1, 0, 2).reshape(64, 32))
    convb = np.ascontiguousarray(f(inp["conv_b"][0]).reshape(8, 64).T)
    bg = np.ascontiguousarray(f(inp["b_gates"][0]).reshape(2, 4).T)
    ident = np.eye(128, dtype=np.float32)
    tri = np.triu(np.ones((128, 128), np.float32))
    sel4 = np.zeros((4, 256), np.float32)
    for h in range(4):
        sel4[h, h * 64:(h + 1) * 64] = 1.0
    inv = (10000.0 ** (-np.arange(16, dtype=np.float64) / 16.0))
    ropec = np.zeros((32, 4), np.float32)
    ropec[:, 0] = np.tile(inv / (2 * np.pi), 2)
    TWO_PI = 6.28318
    ropec[:, 1] = np.concatenate([-np.ones(16), np.ones(16)])
    ropec[:, 2] = ropec[:, 1] * TWO_PI
    ropec[:, 3] = TWO_PI
    return {
        "w_ada": f(inp["w_ada"][0]), "badaT": _col(inp["b_ada"][0], 48), "gmixT": _col(inp["g_mix"][0], 8),
        "gffnT": _col(inp["g_ffn"][0], 8), "gfin": f(inp["g_final"]).reshape(1, D),
        "winl": np.ascontiguousarray(winl), "winm": np.ascontiguousarray(winm),
        "gqT": _col(inp["g_q"][0], 3), "gkvT": _col(inp["g_kv"][0], 2),
        "wq": wq.reshape(384, 8 * 256), "wkv": wkv.reshape(256, 8 * 192),
        "convw": convw, "convb": convb, "bg": bg,
        "gomlaT": _col(inp["g_out_mla"][0], 4), "gomlsT": _col(inp["g_out_mlstm"][0], 4),
        "w_out": f(inp["w_out"][0]), "w_gate": f(inp["w_gate"][0]), "w_up": f(inp["w_up"][0]),
        "w_down": f(inp["w_down"][0]),
        "ident": ident, "tri": tri, "sel4": sel4, "ropec": ropec,
    }


def core_inputs(inp, shared, b):
    d = dict(shared)
    d["x"] = np.ascontiguousarray(np.asarray(inp["x"][b], np.float32))
    d["cT"] = _col(inp["c"][b], 8)
    d["pos"] = np.ascontiguousarray(np.asarray(inp["positions"][b], np.int32).reshape(1, S))
    return d


def kernel(**inputs):
    nc = build_program()
    shared = shared_inputs(inputs)
    in_maps = [core_inputs(inputs, shared, b) for b in range(8)]
    res = run_bass_kernel_spmd(nc, in_maps, core_ids=list(range(8)))
    return np.stack([np.asarray(r["out"], np.float32) for r in res.results], axis=0)
```

```python
import math
from contextlib import ExitStack

import numpy as np
import concourse.bass as bass
import concourse.mybir as mybir
from concourse.bass_utils import run_bass_kernel_spmd

F32 = mybir.dt.float32
BF16 = mybir.dt.bfloat16
I32 = mybir.dt.int32
AF = mybir.ActivationFunctionType
ALU = mybir.AluOpType
AX = mybir.AxisListType

S = 4096
D = 1024
NT = 8
DFF = 2816
NJ = 22
EPS = 1e-6
SCALE = 96 ** -0.5


class Tl:
    __slots__ = ("name", "w", "r")

    def __init__(self, name=""):
        self.name = name
        self.w = None
        self.r = []


class Op:
    __slots__ = ("eng", "fn", "deps", "dma", "signal", "token", "pre", "prog")

    def __init__(self, eng, fn, dma):
        self.eng = eng
        self.fn = fn
        self.dma = dma
        self.deps = []
        self.signal = False
        self.token = None
        self.pre = None


class Prog:
    ENGS = ("tensor", "vector", "scalar", "gpsimd", "sync")
    NRING = 6

    _uid = [0]

    def __init__(self):
        self.ops = {e: [] for e in self.ENGS}

    @classmethod
    def _sem(cls, nc, semstack):
        cls._uid[0] += 1
        return semstack.enter_context(nc.semaphore("sem%d" % cls._uid[0]))

    def add(self, eng, fn, reads=(), writes=(), dma=False):
        op = Op(eng, fn, dma)
        op.prog = self
        deps = {}

        def need(d, kind):
            if d is None or d.prog is not self:
                return
            if d.dma:
                deps[id(d)] = d
                return
            if d.eng == eng and not dma and eng == "tensor":
                return
            deps[id(d)] = d

        for t in reads:
            need(t.w, "raw")
        for t in writes:
            need(t.w, "waw")
            for r in t.r:
                need(r, "war")
        op.deps = list(deps.values())
        for d in op.deps:
            d.signal = True
        for t in reads:
            t.r.append(op)
        for t in writes:
            t.w = op
            t.r = []
        self.ops[eng].append(op)
        return op

    def pe(self, fn, reads=(), writes=()):
        return self.add("tensor", fn, reads, writes)

    def dve(self, fn, reads=(), writes=()):
        return self.add("vector", fn, reads, writes)

    def act(self, fn, reads=(), writes=()):
        return self.add("scalar", fn, reads, writes)

    def pool(self, fn, reads=(), writes=()):
        return self.add("gpsimd", fn, reads, writes)

    def dma(self, q, out, in_, reads=(), writes=(), **kw):
        return self.add(q, lambda e: e.dma_start(out=out, in_=in_, **kw), reads, writes, dma=True)

    def emit(self, nc, semstack, prev_finals):
        esem = {e: self._sem(nc, semstack) for e in self.ENGS}
        qsem = {}
        for e in self.ENGS:
            if any(o.dma for o in self.ops[e]):
                qsem[e] = [self._sem(nc, semstack) for _ in range(self.NRING)]
        finals = {}
        for e in self.ENGS:
            last = None
            for o in self.ops[e]:
                if not o.dma:
                    last = o
            if last is not None:
                last.signal = True
            cnt = 0
            nd = 0
            for o in self.ops[e]:
                if o.dma:
                    slot = nd % self.NRING
                    rnd = nd // self.NRING
                    if rnd > 0:
                        o.pre = (qsem[e][slot], 16 * rnd)
                    o.token = (qsem[e][slot], 16 * (rnd + 1))
                    finals[("q", e, slot)] = o.token
                    nd += 1
                elif o.signal:
                    cnt += 1
                    o.token = (esem[e], cnt)
                    finals[("e", e)] = o.token

        with nc.Block() as block:
            def run(e):
                def body(eng):
                    waited = {}

                    def wait(tok):
                        sem, val = tok
                        k = id(sem)
                        if waited.get(k, 0) < val:
                            eng.wait_ge(sem, val)
                            waited[k] = val

                    for tok in prev_finals:
                        wait(tok)
                    for o in self.ops[e]:
                        for d in o.deps:
                            wait(d.token)
                        if o.pre is not None:
                            wait(o.pre)
                        ins = o.fn(eng)
                        if o.dma:
                            ins.then_inc(o.token[0], 16)
                        elif o.signal:
                            ins.then_inc(o.token[0], 1)
                    if e == "sync":
                        for k, tok in finals.items():
                            if k[0] == "q":
                                wait(tok)
                return body

            block.tensor(run("tensor"))
            block.vector(run("vector"))
            block.scalar(run("scalar"))
            block.gpsimd(run("gpsimd"))
            block.sync(run("sync"))
        return list(finals.values())


class Ring:
    def __init__(self, items):
        self.items = items
        self.i = 0

    def get(self):
        it = self.items[self.i % len(self.items)]
        self.i += 1
        return it


def build_program(dbg=None, phases=("A3", "B")):
    nc = bass.Bass("TRN2", target_bir_lowering=False)

    def din(name, shape, dt=F32):
        return nc.dram_tensor(name, list(shape), dt, kind="ExternalInput").ap()

    x_d = din("x", [S, D])
    cT_d = din("cT", [128, 8])
    pos_d = din("pos", [1, S], I32)
    wada_d = din("w_ada", [D, 6 * D])
    bada_d = din("badaT", [128, 48])
    gmix_d = din("gmixT", [128, 8])
    gffn_d = din("gffnT", [128, 8])
    gfin_d = din("gfin", [1, D])
    winl_d = din("winl", [D, 704])
    winm_d = din("winm", [D, 1544])
    gq_d = din("gqT", [128, 3])
    gkv_d = din("gkvT", [128, 2])
    wq_d = din("wq", [384, 8 * 256])
    wkv_d = din("wkv", [256, 8 * 192])
    convw_d = din("convw", [64, 32])
    convb_d = din("convb", [64, 8])
    bg_d = din("bg", [4, 2])
    gomla_d = din("gomlaT", [128, 4])
    gomls_d = din("gomlsT", [128, 4])
    wout_d = din("w_out", [D, D])
    wg_d = din("w_gate", [D, DFF])
    wu_d = din("w_up", [D, DFF])
    wd_d = din("w_down", [DFF, D])
    ident_d = din("ident", [128, 128])
    tri_d = din("tri", [128, 128])
    sel4_d = din("sel4", [4, 256])
    ropec_d = din("ropec", [32, 4])
    out_d = nc.dram_tensor("out", [S, D], F32, kind="ExternalOutput").ap()
    dbg_outs = {}
    if dbg:
        for name, shape in dbg.items():
            dbg_outs[name] = nc.dram_tensor("dbg_" + name, list(shape), F32, kind="ExternalOutput").ap()

    semstack = ExitStack()
    top = ExitStack()
    with semstack, top:
        uid = [0]

        def sbt(stack, name, shape, dt=F32):
            uid[0] += 1
            return stack.enter_context(nc.sbuf_tensor("s%d_%s" % (uid[0], name), list(shape), dt)), Tl(name)

        def pst(stack, name, shape, dt=F32):
            uid[0] += 1
            return stack.enter_context(nc.psum_tensor("p%d_%s" % (uid[0], name), list(shape), dt)), Tl(name)

        identf, Tidentf = sbt(top, "identf", [128, 128])
        identb, Tidentb = sbt(top, "identb", [128, 128], BF16)
        trib, Ttrib = sbt(top, "trib", [128, 128], BF16)
        onesf, Tonesf = sbt(top, "onesf", [128, 128])
        onesb, Tonesb = sbt(top, "onesb", [128, 128], BF16)
        sel4, Tsel4 = sbt(top, "sel4", [4, 256])
        ropec, Tropec = sbt(top, "ropec", [32, 4])
        modc, Tmodc = sbt(top, "modc", [128, 48])
        scA, TscA = sbt(top, "scA", [128, 8])
        scF, TscF = sbt(top, "scF", [128, 8])
        gq, Tgq = sbt(top, "gq", [128, 3])
        gkv, Tgkv = sbt(top, "gkv", [128, 2])
        convw, Tconvw = sbt(top, "convw", [64, 32])
        convb, Tconvb = sbt(top, "convb", [64, 8])
        bg, Tbg = sbt(top, "bg", [4, 2])
        gomla, Tgomla = sbt(top, "gomla", [128, 4])
        gomls, Tgomls = sbt(top, "gomls", [128, 4])
        yTa, TyTa = sbt(top, "yTa", [128, 4, S], BF16)
        Tpar = Tl("params")

        finals = []

        def dump(P, name, ap, T):
            if name in dbg_outs:
                P.dma("gpsimd", dbg_outs[name], ap, reads=[T])

        def fe_stats(P, fe, xb, Txb):
            junk, Tjunk = fe["junk"].get()
            st, Tst = fe["stat"].get()
            xn, Txn = fe["xn"].get()
            P.act(lambda e: e.activation(out=junk[:], in_=xb, func=AF.Square, accum_out=st[:, 0:1]),
                  reads=[Txb], writes=[Tjunk, Tst])
            P.dve(lambda e: e.tensor_scalar(out=st[:, 1:2], in0=st[:, 0:1], scalar1=1.0 / D, scalar2=EPS,
                                            op0=ALU.mult, op1=ALU.add), reads=[Tst], writes=[Tst])
            P.act(lambda e: e.activation(out=st[:, 2:3], in_=st[:, 1:2], func=AF.Ln), reads=[Tst], writes=[Tst])
            P.act(lambda e: e.activation(out=st[:, 3:4], in_=st[:, 2:3], func=AF.Exp, scale=-0.5), reads=[Tst], writes=[Tst])
            P.dve(lambda e: e.tensor_scalar(out=xn[:], in0=xb, scalar1=st[:, 3:4], scalar2=None, op0=ALU.mult),
                  reads=[Txb, Tst], writes=[Txn])
            return xn, Txn

        def fe_trans(P, fe, xn, Txn, sc, sh, Tsc, hT, ThT, blk):
            pT, TpT = fe["pT"].get()
            for c in range(8):
                P.pe(lambda e, c=c: e.transpose(pT[:, c * 128:(c + 1) * 128], xn[:, c * 128:(c + 1) * 128], identb[:]),
                     reads=[Txn, Tidentb], writes=[TpT])
            for c in range(8):
                if fe.get("evac", "act") == "dve":
                    P.dve(lambda e, c=c: e.tensor_scalar(out=hT[:, c, blk * 128:(blk + 1) * 128], in0=pT[:, c * 128:(c + 1) * 128],
                                                         scalar1=sc[:, c:c + 1], scalar2=sh[:, c:c + 1], op0=ALU.mult, op1=ALU.add),
                          reads=[TpT, Tsc], writes=[ThT[c]])
                else:
                    P.act(lambda e, c=c: e.activation(out=hT[:, c, blk * 128:(blk + 1) * 128], in_=pT[:, c * 128:(c + 1) * 128],
                                                      func=AF.Identity, bias=sh[:, c:c + 1], scale=sc[:, c:c + 1]),
                          reads=[TpT, Tsc], writes=[ThT[c]])

        def frontend(P, fe, xb, Txb, sc, sh, Tsc, hT, ThT, blk):
            xn, Txn = fe_stats(P, fe, xb, Txb)
            fe_trans(P, fe, xn, Txn, sc, sh, Tsc, hT, ThT, blk)

        def fe_tile(P, fe, xbr, t, sc, sh, Tsc, hT, ThT):
            xs = []
            for blk in range(4):
                xb, Txb = xbr.get()
                r0 = t * 512 + blk * 128
                P.dma("sync", xb[:], x_d[r0:r0 + 128, :], writes=[Txb])
                xs.append((xb, Txb))
            xn = [None] * 4
            xn[0] = fe_stats(P, fe, xs[0][0][:], xs[0][1])
            for blk in range(4):
                if blk + 1 < 4:
                    xn[blk + 1] = fe_stats(P, fe, xs[blk + 1][0][:], xs[blk + 1][1])
                fe_trans(P, fe, xn[blk][0], xn[blk][1], sc, sh, Tsc, hT, ThT, blk)

        def fe_tile_thunks(P, fe, xbr, t, sc, sh, Tsc, hT, ThT):
            xs = [None] * 4
            xn = [None] * 4

            def load():
                for blk in range(4):
                    xb, Txb = xbr.get()
                    r0 = t * 512 + blk * 128
                    P.dma("sync", xb[:], x_d[r0:r0 + 128, :], writes=[Txb])
                    xs[blk] = (xb, Txb)

            def stats(blk):
                def f():
                    xn[blk] = fe_stats(P, fe, xs[blk][0][:], xs[blk][1])
                return f

            def trans(blk):
                def f():
                    fe_trans(P, fe, xn[blk][0], xn[blk][1], sc, sh, Tsc, hT, ThT, blk)
                return f
            return [load, stats(0), stats(1), trans(0), stats(2), trans(1), stats(3), trans(2), trans(3)]

        def merge(streams):
            items = []
            for si, st_ in enumerate(streams):
                n = len(st_)
                for j, th in enumerate(st_):
                    items.append(((j + 0.5) / n, si, j, th))
            items.sort(key=lambda t: (t[0], t[1], t[2]))
            for it in items:
                it[3]()

        def make_fe(stack, nxn=2, npt=2):
            fe = {}
            fe["junk"] = Ring([sbt(stack, "fe_junk%d" % i, [128, D], BF16) for i in range(1)])
            fe["stat"] = Ring([sbt(stack, "fe_st%d" % i, [128, 4]) for i in range(4)])
            fe["xn"] = Ring([sbt(stack, "fe_xn%d" % i, [128, D], BF16) for i in range(nxn)])
            fe["pT"] = Ring([pst(stack, "fe_pT%d" % i, [128, D], BF16) for i in range(npt)])
            return fe

        def rsqrt_inplace(P, buf, Tbuf, src, Tsrc, scale, eps):
            P.dve(lambda e: e.tensor_scalar(out=buf, in0=src, scalar1=scale, scalar2=eps, op0=ALU.mult, op1=ALU.add),
                  reads=[Tsrc], writes=[Tbuf])
            P.act(lambda e: e.activation(out=buf, in_=buf, func=AF.Ln), reads=[Tbuf], writes=[Tbuf])
            P.act(lambda e: e.activation(out=buf, in_=buf, func=AF.Exp, scale=-0.5), reads=[Tbuf], writes=[Tbuf])

        with ExitStack() as ph:
            P = Prog()
            cT, TcT = sbt(ph, "cT", [128, 8])
            silc, Tsilc = sbt(ph, "silc", [128, 8])
            bada, Tbada = sbt(ph, "bada", [128, 48])
            gmix, Tgmix = sbt(ph, "gmix", [128, 8])
            gffn, Tgffn = sbt(ph, "gffn", [128, 8])
            tmp8, Ttmp8 = sbt(ph, "tmp8", [128, 8])
            wst = Ring([sbt(ph, "wada%d" % i, [128, 8, 512]) for i in range(2)])
            pm, Tpm = pst(ph, "pm", [128, 512])

            for dst, src in ((identf, ident_d), (sel4, sel4_d), (ropec, ropec_d), (gq, gq_d), (gkv, gkv_d),
                             (convw, convw_d), (convb, convb_d), (bg, bg_d), (gomla, gomla_d), (gomls, gomls_d)):
                P.dma("sync", dst[:], src, writes=[Tpar])
            Tidentf.w = Tpar.w
            P.dma("sync", cT[:], cT_d, writes=[TcT])
            P.dma("sync", bada[:], bada_d, writes=[Tbada])
            P.dma("sync", gmix[:], gmix_d, writes=[Tgmix])
            P.dma("sync", gffn[:], gffn_d, writes=[Tgffn])
            P.dma("gpsimd", identb[:], ident_d, writes=[Tidentb])
            P.dma("gpsimd", trib[:], tri_d, writes=[Ttrib])
            P.dve(lambda e: e.memset(onesf[:], 1.0), writes=[Tonesf])
            P.dve(lambda e: e.memset(onesb[:], 1.0), writes=[Tonesb])
            P.act(lambda e: e.activation(out=silc[:], in_=cT[:], func=AF.Silu), reads=[TcT], writes=[Tsilc])
            wada_v = wada_d.rearrange("(k p) n -> p k n", p=128)
            for pc in range(12):
                wt, Twt = wst.get()
                P.dma("sync", wt[:], wada_v[:, :, pc * 512:(pc + 1) * 512], writes=[Twt])
                for jj in range(4):
                    j = pc * 4 + jj
                    for k in range(8):
                        P.pe(lambda e, wt=wt, jj=jj, j=j, k=k: e.matmul(
                            pm[:, j:j + 1], lhsT=wt[:, k, jj * 128:(jj + 1) * 128], rhs=silc[:, k:k + 1],
                            start=(k == 0), stop=(k == 7)), reads=[Twt, Tsilc], writes=[Tpm])
            P.dve(lambda e: e.tensor_tensor(out=modc[:], in0=pm[:, 0:48], in1=bada[:], op=ALU.add),
                  reads=[Tpm, Tbada], writes=[Tmodc])
            P.dve(lambda e: e.tensor_scalar(out=tmp8[:], in0=modc[:, 8:16], scalar1=1.0, scalar2=None, op0=ALU.add),
                  reads=[Tmodc], writes=[Ttmp8])
            P.dve(lambda e: e.tensor_tensor(out=scA[:], in0=tmp8[:], in1=gmix[:], op=ALU.mult),
                  reads=[Ttmp8, Tgmix], writes=[TscA])
            P.dve(lambda e: e.tensor_scalar(out=tmp8[:], in0=modc[:, 32:40], scalar1=1.0, scalar2=None, op0=ALU.add),
                  reads=[Tmodc, TscA], writes=[Ttmp8])
            P.dve(lambda e: e.tensor_tensor(out=scF[:], in0=tmp8[:], in1=gffn[:], op=ALU.mult),
                  reads=[Ttmp8, Tgffn], writes=[TscF])
            dump(P, "modc", modc[:], Tmodc)
            finals = P.emit(nc, semstack, finals)

        shA = modc[:, 0:8]
        shF = modc[:, 24:32]

        with ExitStack() as pa:
            qnT, TqnT = sbt(pa, "qnT", [128, 3, S], BF16)
            kvnT, TkvnT = sbt(pa, "kvnT", [128, 2, S], BF16)
            krT, TkrT = sbt(pa, "krT", [32, S], BF16)
            cosT, TcosT = sbt(pa, "cosT", [32, S])
            sinT, TsinT = sbt(pa, "sinT", [32, S])
            wq, Twq = sbt(pa, "wq", [128, 3, 8 * 256], BF16)
            wkv, Twkv = sbt(pa, "wkv", [128, 2, 8 * 192], BF16)

            with ExitStack() as ph:
                P = Prog()
                winl, Twinl = sbt(ph, "winl", [128, 8, 704], BF16)
                xbr = Ring([sbt(ph, "xb%d" % i, [128, D]) for i in range(4)])
                hTr = Ring([(sbt(ph, "hT%d" % i, [128, 8, 512], BF16)[0], [Tl("hT%d_%d" % (i, c)) for c in range(8)]) for i in range(2)])
                fe = make_fe(ph)
                fe["evac"] = "dve"
                sq, Tsq = sbt(ph, "sq", [128, 3, 512])
                rstd, Trstd = sbt(ph, "rstd", [128, 512])
                posi, Tposi = sbt(ph, "posi", [32, 512], I32)
                ry, Try = sbt(ph, "ry", [32, 512])
                rn, Trn = sbt(ph, "rn", [32, 512], I32)
                rf, Trf = sbt(ph, "rf", [32, 512])
                rg, Trg = sbt(ph, "rg", [32, 512])
                t1, Tt1 = sbt(ph, "t1", [32, 512])
                t2, Tt2 = sbt(ph, "t2", [32, 512])
                pl = Ring([pst(ph, "pl%d" % i, [128, 512]) for i in range(6)])

                P.dma("gpsimd", winl[:], winl_d.rearrange("(k p) n -> p k n", p=128), writes=[Twinl])
                P.dma("gpsimd", wq[:], wq_d.rearrange("(k p) n -> p k n", p=128), writes=[Twq])
                P.dma("gpsimd", wkv[:], wkv_d.rearrange("(k p) n -> p k n", p=128), writes=[Twkv])

                hTs = {}

                def a1_front(i):
                    hTs[i] = hTr.get()
                    fe_tile(P, fe, xbr, i, scA, shA, TscA, hTs[i][0], hTs[i][1])

                def a1_tile(i):
                    cols = slice(i * 512, (i + 1) * 512)
                    hT, ThT = hTs.pop(i)

                    def r0():
                        P.dma("sync", posi[:], pos_d[:, cols].to_broadcast([32, 512]), writes=[Tposi])
                        P.dve(lambda e: e.tensor_copy(out=ry[:], in_=posi[:]), reads=[Tposi], writes=[Try])
                        P.dve(lambda e: e.tensor_scalar(out=ry[:], in0=ry[:], scalar1=ropec[:, 0:1], scalar2=None, op0=ALU.mult),
                              reads=[Try, Tpar], writes=[Try])

                    def rw(which):
                        def f():
                            if which == 1:
                                P.dve(lambda e: e.tensor_scalar(out=ry[:], in0=ry[:], scalar1=0.25, scalar2=None, op0=ALU.add),
                                      reads=[Try], writes=[Try])
                            P.dve(lambda e: e.tensor_copy(out=rn[:], in_=ry[:]), reads=[Try], writes=[Trn])
                            P.dve(lambda e: e.tensor_copy(out=rf[:], in_=rn[:]), reads=[Trn], writes=[Trf])
                            P.dve(lambda e: e.tensor_tensor(out=rg[:], in0=ry[:], in1=rf[:], op=ALU.subtract),
                                  reads=[Try, Trf], writes=[Trg])
                            if which == 0:
                                P.act(lambda e: e.activation(out=sinT[:, cols], in_=rg[:], func=AF.Sin, scale=ropec[:, 2:3]),
                                      reads=[Trg, Tpar], writes=[TsinT])
                            else:
                                P.act(lambda e: e.activation(out=cosT[:, cols], in_=rg[:], func=AF.Sin, scale=ropec[:, 3:4]),
                                      reads=[Trg, Tpar], writes=[TcosT])
                        return f
                    ROPE = [r0, rw(0), rw(1)]

                    st_ = {}

                    def inproj(name, c0, c1, M):
                        def f():
                            pt, Tpt = pl.get()
                            for k in range(8):
                                P.pe(lambda e, k=k: e.matmul(pt[0:M, :], lhsT=winl[:, k, c0:c1], rhs=hT[:, k, :],
                                                             start=(k == 0), stop=(k == 7)),
                                     reads=[Twinl, ThT[k]], writes=[Tpt])
                            st_[name] = (pt, Tpt)
                        return f

                    def lat_stats(names, nch):
                        def f():
                            for m, nm in enumerate(names):
                                pt, Tpt = st_[nm]
                                P.act(lambda e, m=m, pt=pt: e.activation(out=sq[:, m, :], in_=pt[:], func=AF.Square),
                                      reads=[Tpt], writes=[Tsq])
                            ps_, Tps = pl.get()
                            for m in range(nch):
                                P.pe(lambda e, m=m: e.matmul(ps_[:], lhsT=onesf[:], rhs=sq[:, m, :], start=(m == 0), stop=(m == nch - 1)),
                                     reads=[Tsq, Tonesf], writes=[Tps])
                            rsqrt_inplace(P, rstd[:], Trstd, ps_[:], Tps, 1.0 / (128 * nch), EPS)
                        return f

                    def lat_final(names, g, dst, Tdst):
                        def f():
                            for m, nm in enumerate(names):
                                pt, Tpt = st_.pop(nm)
                                P.dve(lambda e, m=m, pt=pt: e.scalar_tensor_tensor(
                                    out=dst[:, m, cols], in0=pt[:], scalar=g[:, m:m + 1], in1=rstd[:], op0=ALU.mult, op1=ALU.mult),
                                    reads=[Tpt, Trstd, Tpar], writes=[Tdst])
                        return f

                    def kr_rope():
                        pkr, Tpkr = st_.pop("kr")
                        pks, Tpks = st_.pop("ks")
                        P.dve(lambda e: e.tensor_tensor(out=t1[:], in0=pkr[0:32, :], in1=cosT[:, cols], op=ALU.mult),
                              reads=[Tpkr, TcosT], writes=[Tt1])
                        P.dve(lambda e: e.tensor_tensor(out=t2[:], in0=pks[0:32, :], in1=sinT[:, cols], op=ALU.mult),
                              reads=[Tpks, TsinT], writes=[Tt2])
                        P.dve(lambda e: e.tensor_tensor(out=krT[:, cols], in0=t1[:], in1=t2[:], op=ALU.add),
                              reads=[Tt1, Tt2], writes=[TkrT])
                    qn_ = ["q0", "q1", "q2"]
                    kn_ = ["kv0", "kv1"]
                    MAT = [inproj("q%d" % m, m * 128, (m + 1) * 128, 128) for m in range(3)]
                    MAT += [inproj("kv%d" % m, 384 + m * 128, 384 + (m + 1) * 128, 128) for m in range(2)]
                    MAT += [lat_stats(qn_, 3), lat_final(qn_, gq, qnT, TqnT),
                            inproj("kr", 640, 672, 32), inproj("ks", 672, 704, 32),
                            lat_stats(kn_, 2), lat_final(kn_, gkv, kvnT, TkvnT), kr_rope]

                    FE = []
                    if i + 1 < NT:
                        hTs[i + 1] = hTr.get()
                        FE = fe_tile_thunks(P, fe, xbr, i + 1, scA, shA, TscA, hTs[i + 1][0], hTs[i + 1][1])
                    merge([st for st in (FE, ROPE, MAT) if st])

                a1_front(0)
                for i in range(NT):
                    a1_tile(i)
                dump(P, "qnT0", qnT[:, 0, 0:512], TqnT)
                dump(P, "kvnT1", kvnT[:, 1, 512:1024], TkvnT)
                dump(P, "krT", krT[:, 0:1024], TkrT)
                finals = P.emit(nc, semstack, finals)

            with ExitStack() as ph:
                P = Prog()
                QTs = [sbt(ph, "QT%d" % i, [128, S], BF16) for i in range(2)]
                KTs = [sbt(ph, "KT%d" % i, [128, S], BF16) for i in range(2)]
                Vas = [sbt(ph, "Va%d" % i, [128, 32, 128], BF16) for i in range(2)]
                sqqr = Ring([sbt(ph, "sqq%d" % i, [128, 512], BF16) for i in range(2)])
                sqkr = Ring([sbt(ph, "sqk%d" % i, [128, 512], BF16) for i in range(2)])
                mxs = [sbt(ph, "mx%d" % i, [33, 32]) for i in range(2)]
                ptr = Ring([sbt(ph, "pt%d" % i, [128, 512], BF16) for i in range(6)])
                osqr = Ring([sbt(ph, "osq%d" % i, [128, 512], BF16) for i in range(2)])
                lsqr = Ring([sbt(ph, "lsq%d" % i, [128, 512], BF16) for i in range(2)])
                rsr = Ring([sbt(ph, "rs%d" % i, [128, 512]) for i in range(2)])
                lrwr = Ring([sbt(ph, "lrw%d" % i, [128, 512]) for i in range(2)])
                a1, Ta1 = sbt(ph, "a1", [32, 512])
                a2, Ta2 = sbt(ph, "a2", [32, 512])
                pp = Ring([pst(ph, "pp%d" % i, [128, 512]) for i in range(3)])
                pps = Ring([pst(ph, "pps%d" % i, [128, 512]) for i in range(3)])
                po = Ring([pst(ph, "po%d" % i, [128, 512]) for i in range(2)])

                for b in range(2):
                    QT, TQT = QTs[b]
                    KT, TKT = KTs[b]
                    Va, TVa = Vas[b]
                    P.pool(lambda e, QT=QT: e.memset(QT[:], 0.0), writes=[TQT])
                    P.pool(lambda e, KT=KT: e.memset(KT[:], 0.0), writes=[TKT])
                    P.pool(lambda e, KT=KT: e.memset(KT[32:33, :], 1.0), writes=[TKT])
                    P.pool(lambda e, Va=Va: e.memset(Va[:], 0.0), writes=[TVa])
                    lc = 64 if b == 0 else 0
                    P.pool(lambda e, Va=Va, lc=lc: e.memset(Va[:, :, lc:lc + 1], 1.0), writes=[TVa])

                def head_ctx(h):
                    b = h % 2
                    return dict(h=h, b=b, vb=64 * b, lrow=(64 if b == 0 else 0), M=(65 if b == 0 else 128),
                                QT=QTs[b][0], TQT=QTs[b][1], KT=KTs[b][0], TKT=KTs[b][1], Va=Vas[b][0], TVa=Vas[b][1], sq={})

                def prep_start(hc):
                    pass

                def prep_a(hc, i):
                    h, vb = hc["h"], hc["vb"]
                    QT, TQT, KT, TKT, Va, TVa = hc["QT"], hc["TQT"], hc["KT"], hc["TKT"], hc["Va"], hc["TVa"]
                    cols = slice(i * 512, (i + 1) * 512)
                    X, TX = pp.get()
                    for k in range(3):
                        P.pe(lambda e, k=k: e.matmul(X[:], lhsT=wq[:, k, h * 256:h * 256 + 128], rhs=qnT[:, k, cols],
                                                     start=(k == 0), stop=(k == 2)), reads=[Twq, TqnT], writes=[TX])
                    Y, TY = pp.get()
                    for k in range(2):
                        P.pe(lambda e, k=k: e.matmul(Y[:], lhsT=wkv[:, k, h * 192:h * 192 + 128], rhs=kvnT[:, k, cols],
                                                     start=(k == 0), stop=False), reads=[Twkv, TkvnT], writes=[TY])
                    for k in range(3):
                        P.pe(lambda e, k=k: e.matmul(Y[:], lhsT=wq[:, k, h * 256 + 128:h * 256 + 256], rhs=qnT[:, k, cols],
                                                     start=False, stop=(k == 2)), reads=[Twq, TqnT], writes=[TY])
                    Z, TZ = pp.get()
                    for blk in range(4):
                        for k in range(2):
                            P.pe(lambda e, k=k, blk=blk: e.matmul(
                                Z[:, blk * 64:(blk + 1) * 64], lhsT=kvnT[:, k, i * 512 + blk * 128:i * 512 + (blk + 1) * 128],
                                rhs=wkv[:, k, h * 192 + 128:h * 192 + 192], start=(k == 0), stop=(k == 1)),
                                reads=[Twkv, TkvnT], writes=[TZ])
                    P.dve(lambda e: e.tensor_scalar(out=QT[64:128, cols], in0=X[64:128, :], scalar1=SCALE, scalar2=None, op0=ALU.mult),
                          reads=[TX], writes=[TQT])
                    P.dve(lambda e: e.scalar_tensor_tensor(out=a1[:], in0=X[0:32, :], scalar=SCALE, in1=cosT[:, cols],
                                                           op0=ALU.mult, op1=ALU.mult), reads=[TX, TcosT], writes=[Ta1])
                    P.dve(lambda e: e.scalar_tensor_tensor(out=a2[:], in0=Y[0:32, :], scalar=SCALE, in1=sinT[:, cols],
                                                           op0=ALU.mult, op1=ALU.mult), reads=[TY, TsinT], writes=[Ta2])
                    P.dve(lambda e: e.tensor_tensor(out=QT[0:32, cols], in0=a1[:], in1=a2[:], op=ALU.add),
                          reads=[Ta1, Ta2], writes=[TQT])
                    P.dve(lambda e: e.tensor_copy(out=KT[64:128, cols], in_=Y[64:128, :]), reads=[TY], writes=[TKT])
                    P.dve(lambda e: e.tensor_copy(out=KT[0:32, cols], in_=krT[:, cols]), reads=[TkrT], writes=[TKT])
                    P.dve(lambda e: e.tensor_copy(out=Va[:, 4 * i:4 * i + 4, vb:vb + 64], in_=Z[:, 0:256].rearrange("p (b v) -> p b v", v=64)),
                          reads=[TZ], writes=[TVa])
                    sqq, Tsqq = sqqr.get()
                    sqk, Tsqk = sqkr.get()
                    P.dve(lambda e: e.tensor_tensor(out=sqq[0:32, :], in0=QT[0:32, cols], in1=QT[0:32, cols], op=ALU.mult), reads=[TQT], writes=[Tsqq])
                    P.dve(lambda e: e.tensor_tensor(out=sqq[64:128, :], in0=QT[64:128, cols], in1=QT[64:128, cols], op=ALU.mult), reads=[TQT], writes=[Tsqq])
                    P.dve(lambda e: e.tensor_tensor(out=sqk[0:32, :], in0=KT[0:32, cols], in1=KT[0:32, cols], op=ALU.mult), reads=[TKT], writes=[Tsqk])
                    P.dve(lambda e: e.tensor_tensor(out=sqk[64:128, :], in0=KT[64:128, cols], in1=KT[64:128, cols], op=ALU.mult), reads=[TKT], writes=[Tsqk])
                    hc["sq"][i] = (sqq, Tsqq, sqk, Tsqk)

                def prep_b(hc, i):
                    sqq, Tsqq, sqk, Tsqk = hc["sq"].pop(i)
                    mx, Tmx = mxs[hc["b"]]
                    for (sq_, Tsq_, col) in ((sqq, Tsqq, i), (sqk, Tsqk, 16 + i)):
                        pss, Tpss = pp.get()
                        P.pe(lambda e, pss=pss, sq_=sq_: e.matmul(pss[0:33, :], lhsT=onesb[0:32, 0:33], rhs=sq_[0:32, :], start=True, stop=False),
                             reads=[Tsq_, Tonesb], writes=[Tpss])
                        P.pe(lambda e, pss=pss, sq_=sq_: e.matmul(pss[0:33, :], lhsT=onesb[64:128, 0:33], rhs=sq_[64:128, :], start=False, stop=True),
                             reads=[Tsq_, Tonesb], writes=[Tpss])
                        P.dve(lambda e, pss=pss, col=col: e.reduce_max(out=mx[:, col:col + 1], in_=pss[0:33, :], axis=AX.X), reads=[Tpss], writes=[Tmx])

                def prep_finish(hc):
                    QT, TQT, KT, TKT = hc["QT"], hc["TQT"], hc["KT"], hc["TKT"]
                    mx, Tmx = mxs[hc["b"]]
                    prep_b(hc, NT - 1)
                    P.dve(lambda e: e.reduce_max(out=mx[:, 8:9], in_=mx[:, 0:8], axis=AX.X), reads=[Tmx], writes=[Tmx])
                    P.dve(lambda e: e.reduce_max(out=mx[:, 24:25], in_=mx[:, 16:24], axis=AX.X), reads=[Tmx], writes=[Tmx])
                    P.dve(lambda e: e.tensor_tensor(out=mx[:, 25:26], in0=mx[:, 8:9], in1=mx[:, 24:25], op=ALU.mult), reads=[Tmx], writes=[Tmx])
                    P.act(lambda e: e.activation(out=mx[:, 26:27], in_=mx[:, 25:26], func=AF.Ln), reads=[Tmx], writes=[Tmx])
                    P.act(lambda e: e.activation(out=mx[:, 27:28], in_=mx[:, 26:27], func=AF.Exp, scale=0.5), reads=[Tmx], writes=[Tmx])
                    P.dve(lambda e: e.tensor_scalar(out=mx[:, 28:29], in0=mx[:, 27:28], scalar1=-1.05, scalar2=None, op0=ALU.mult),
                          reads=[Tmx], writes=[Tmx])
                    P.dve(lambda e: e.tensor_scalar(out=QT[32:33, :], in0=KT[32:33, :], scalar1=mx[32:33, 28:29], scalar2=None, op0=ALU.mult),
                          reads=[Tmx, TKT], writes=[TQT])

                def attention(hc, hook, LA=2, EPI_DELAY=2, EPI_DELAY2=5):
                    h, vb, lrow, M = hc["h"], hc["vb"], hc["lrow"], hc["M"]
                    QT, TQT, KT, TKT, Va, TVa = hc["QT"], hc["TQT"], hc["KT"], hc["TKT"], hc["Va"], hc["TVa"]
                    steps = [(i, kb) for i in range(NT) for kb in range(4 * i + 4)]
                    ctx = {}
                    Ob = {}
                    pending = []

                    def score(i, kb):
                        c0 = max(0, kb - 4 * i) * 128
                        n = 512 - c0
                        ps_, Tps = pps.get()
                        P.pe(lambda e: e.matmul(ps_[:, 0:n], lhsT=KT[:, kb * 128:(kb + 1) * 128], rhs=QT[:, i * 512 + c0:(i + 1) * 512], start=True, stop=True),
                             reads=[TKT, TQT], writes=[Tps])
                        ctx[(i, kb)] = (ps_, Tps, c0, n)

                    def rest(idx, i, kb):
                        ps_, Tps, c0, n = ctx.pop((i, kb))
                        nkb = 4 * i + 4
                        if kb == 0:
                            Ob[i] = po.get()
                        O, TO = Ob[i]
                        pt, Tpt = ptr.get()
                        P.act(lambda e: e.activation(out=pt[:, 0:n], in_=ps_[:, 0:n], func=AF.Exp), reads=[Tps], writes=[Tpt])
                        if kb >= 4 * i:
                            P.pool(lambda e: e.tensor_tensor(out=pt[:, 0:128], in0=pt[:, 0:128], in1=trib[:], op=ALU.mult),
                                   reads=[Tpt, Ttrib], writes=[Tpt])
                        P.pe(lambda e: e.matmul(O[0:M, c0:512], lhsT=Va[:, kb, 0:M], rhs=pt[:, 0:n], start=(kb == 0), stop=(kb == nkb - 1)),
                             reads=[TVa, Tpt], writes=[TO])
                        if kb == nkb - 1:
                            st = epi_a(i)
                            pending.append((idx + EPI_DELAY, lambda: epi_b(i, st)))
                        if hook is not None:
                            hook(i, kb, nkb)

                    def epi_a(i):
                        O, TO = Ob.pop(i)
                        osq, Tosq = osqr.get()
                        lsq, Tlsq = lsqr.get()
                        P.act(lambda e: e.activation(out=osq[vb:vb + 64, :], in_=O[vb:vb + 64, :], func=AF.Square), reads=[TO], writes=[Tosq])
                        lrw, Tlrw = lrwr.get()
                        P.dve(lambda e: e.tensor_copy(out=lrw[lrow:lrow + 1, :], in_=O[lrow:lrow + 1, :]), reads=[TO], writes=[Tlrw])
                        P.dve(lambda e: e.scalar_tensor_tensor(out=lsq[lrow:lrow + 1, :], in0=lrw[lrow:lrow + 1, :], scalar=64 * EPS, in1=lrw[lrow:lrow + 1, :],
                                                               op0=ALU.mult, op1=ALU.mult), reads=[Tlrw], writes=[Tlsq])
                        return (O, TO, osq, Tosq, lsq, Tlsq)

                    def epi_b(i, st):
                        O, TO, osq, Tosq, lsq, Tlsq = st
                        pn_, Tpn = pp.get()
                        P.pe(lambda e: e.matmul(pn_[:], lhsT=onesb[vb:vb + 64, :], rhs=osq[vb:vb + 64, :], start=True, stop=False),
                             reads=[Tosq, Tonesb], writes=[Tpn])
                        P.pe(lambda e: e.matmul(pn_[:], lhsT=onesb[lrow:lrow + 1, :], rhs=lsq[lrow:lrow + 1, :], start=False, stop=True),
                             reads=[Tlsq, Tonesb], writes=[Tpn])
                        pending.append((pending_idx[0] + EPI_DELAY2, lambda: epi_c(i, st, pn_, Tpn)))
                        pending.sort(key=lambda t: t[0])

                    def epi_c(i, st, pn_, Tpn):
                        O, TO, osq, Tosq, lsq, Tlsq = st
                        cols = slice(i * 512, (i + 1) * 512)
                        rs, Trs = rsr.get()
                        P.act(lambda e: e.activation(out=rs[vb:vb + 64, :], in_=pn_[vb:vb + 64, :], func=AF.Ln, scale=1.0 / 64), reads=[Tpn], writes=[Trs])
                        P.act(lambda e: e.activation(out=rs[vb:vb + 64, :], in_=rs[vb:vb + 64, :], func=AF.Exp, scale=-0.5), reads=[Trs], writes=[Trs])
                        P.dve(lambda e: e.scalar_tensor_tensor(
                            out=yTa[vb:vb + 64, h // 2, cols], in0=O[vb:vb + 64, :], scalar=gomla[vb:vb + 64, h // 2:h // 2 + 1], in1=rs[vb:vb + 64, :],
                            op0=ALU.mult, op1=ALU.mult), reads=[TO, Trs, Tpar], writes=[TyTa])

                    pending_idx = [0]
                    for idx in range(len(steps) + LA):
                        pending_idx[0] = idx
                        if idx < len(steps):
                            score(*steps[idx])
                        if idx >= LA:
                            rest(idx, *steps[idx - LA])
                        while pending and pending[0][0] <= idx:
                            pending.pop(0)[1]()
                    while pending:
                        pending_idx[0] += 1
                        pending.pop(0)[1]()

                hcs = [head_ctx(h) for h in range(8)]
                prep_start(hcs[0])
                for i in range(NT):
                    prep_a(hcs[0], i)
                    if i > 0:
                        prep_b(hcs[0], i - 1)
                prep_finish(hcs[0])
                for h in range(8):
                    if h + 1 < 8:
                        nxt = hcs[h + 1]
                        prep_start(nxt)
                        prep_a(nxt, 0)

                        def hook(i, kb, nkb, nxt=nxt):
                            if kb == nkb - 1 and i < NT - 1:
                                prep_b(nxt, i)
                                prep_a(nxt, i + 1)
                            if i == NT - 1 and kb == 6:
                                prep_finish(nxt)
                        attention(hcs[h], hook)
                    else:
                        attention(hcs[h], None)
                    if h == 0:
                        dump(P, "QT0", hcs[0]["QT"][:, 0:512], hcs[0]["TQT"])
                        dump(P, "KT0", hcs[0]["KT"][:, 0:512], hcs[0]["TKT"])
                dump(P, "yTa0", yTa[:, 0, 0:1024], TyTa)
                dump(P, "yTa3", yTa[:, 3, 3072:4096], TyTa)
                finals = P.emit(nc, semstack, finals)

        yTb, TyTb = sbt(top, "yTb", [128, 4, S], BF16)
        if "A3" in phases:
            with ExitStack() as ph:
                P = Prog()
                winm, Twinm = sbt(ph, "winm", [128, 8, 1544], BF16)
                xbr = Ring([sbt(ph, "m_xb%d" % i, [128, D]) for i in range(4)])
                hTs = [(sbt(ph, "m_hT%d" % b_, [128, 8, 512], BF16)[0], [Tl("m_hT%d_%d" % (b_, c)) for c in range(8)]) for b_ in range(2)]
                fe = make_fe(ph, nxn=2, npt=1)
                fe["evac"] = "dve"
                zqk = sbt(ph, "zqk", [64, 8, 516], BF16)[0]
                dgw, Tdgw = sbt(ph, "dgw", [64, 32, 64], BF16)
                ser = Ring([sbt(ph, "se%d" % i, [64, 512]) for i in range(2)])
                nconvb, Tnconvb = sbt(ph, "nconvb", [64, 8])
                Tz = [Tl("z%d" % g) for g in range(8)]
                qT, TqT = sbt(ph, "m_qT", [64, 4, 512], BF16)
                kTf, TkTf = sbt(ph, "kTf", [64, 4, 512])
                kpT, TkpT = sbt(ph, "kpT", [64, 4, 512], BF16)
                kptok, Tkptok = sbt(ph, "kptok", [128, 16, 64], BF16)
                vaug, Tvaug = sbt(ph, "vaug", [128, 4, 4, 129], BF16)
                sigo, Tsigo = sbt(ph, "sigo", [128, 4, 512])
                g_ig, Tg_ig = sbt(ph, "g_ig", [4, 512])
                g_x, Tg_x = sbt(ph, "g_x", [4, 512])
                g_a, Tg_a = sbt(ph, "g_a", [4, 512])
                g_b, Tg_b = sbt(ph, "g_b", [4, 512])
                g_lf, Tg_lf = sbt(ph, "g_lf", [4, 512])
                g_bc, Tg_bc = sbt(ph, "g_bc", [4, 512])
                g_u, Tg_u = sbt(ph, "g_u", [4, 512])
                g_w, Tg_w = sbt(ph, "g_w", [4, 512])
                g_cl, Tg_cl = sbt(ph, "g_cl", [4, 512])
                msk, Tmsk = sbt(ph, "msk", [4, 512])
                sm, Tsm = sbt(ph, "sm", [4, 64])
                clampc, Tclampc = sbt(ph, "clampc", [128, 16])
                abc, Tabc = sbt(ph, "abc", [64, 16])
                Sst, TS = sbt(ph, "Sst", [64, 4, 129])
                Sh, TSh = sbt(ph, "Sh", [64, 4, 129])
                Shb, TShb = sbt(ph, "Shb", [64, 4, 129], BF16)
                sTm, TsTm = sbt(ph, "sTm", [128, 4, 128], BF16)
                ybr = Ring([sbt(ph, "yb%d" % i, [128, 128]) for i in range(4)])
                ybnr = Ring([sbt(ph, "ybn%d" % i, [128, 128], BF16) for i in range(4)])
                junk2, Tjunk2 = sbt(ph, "junk2", [128, 128], BF16)
                dtmp, Tdtmp = sbt(ph, "dtmp", [128, 8])
                ss2, Tss2 = sbt(ph, "ss2", [128, 8])
                ptk, Tptk = pst(ph, "ptk", [128, 1024], BF16)
                pty, Tpty = pst(ph, "pty", [128, 1024], BF16)
                pr = Ring([pst(ph, "pr%d" % i, [128, 512]) for i in range(5)])

                P.dma("gpsimd", winm[:], winm_d.rearrange("(k p) n -> p k n", p=128), writes=[Twinm])
                P.pool(lambda e: e.memset(zqk[:], 0.0), writes=Tz)
                P.dve(lambda e: e.tensor_scalar(out=nconvb[:], in0=convb[:], scalar1=-1.0, scalar2=None, op0=ALU.mult), writes=[Tnconvb])
                for gj in range(32):
                    P.dve(lambda e, gj=gj: e.tensor_scalar(out=dgw[:, gj, :], in0=identb[0:64, 0:64], scalar1=convw[:, gj:gj + 1], scalar2=None, op0=ALU.mult),
                          writes=[Tdgw])
                P.pool(lambda e: e.memset(Sst[:], 0.0), writes=[TS])
                P.pool(lambda e: e.memset(sm[:], 0.0), writes=[Tsm])
                P.pool(lambda e: e.memset(vaug[:], 1.0), writes=[Tvaug])
                P.pool(lambda e: e.memset(msk[:], 1.0), writes=[Tmsk])
                P.pool(lambda e: e.memset(msk[:].rearrange("p (c t) -> p c t", t=128)[:, :, 0:1], 0.0), writes=[Tmsk])

                def v3(ap):
                    return ap.rearrange("p (c t) -> p c t", t=128)

                def a3_front(i):
                    fe_tile(P, fe, xbr, i, scA, shA, TscA, hTs[i % 2][0], hTs[i % 2][1])

                def a3_tile(i):
                    hT, ThT = hTs[i % 2]

                    def proj(c0, c1, M):
                        pt, Tpt = pr.get()
                        for k in range(8):
                            P.pe(lambda e, k=k: e.matmul(pt[0:M, :], lhsT=winm[:, k, c0:c1], rhs=hT[:, k, :], start=(k == 0), stop=(k == 7)),
                                 reads=[Twinm, ThT[k]], writes=[Tpt])
                        return pt, Tpt

                    def qk_proj(g):
                        def f():
                            pt, Tpt = proj(g * 64, (g + 1) * 64, 64)
                            P.dve(lambda e: e.tensor_copy(out=zqk[:, g, 4:516], in_=pt[0:64, :]), reads=[Tpt], writes=[Tz[g]])
                        return f

                    def qk_conv(g):
                        def f():
                            pc_, Tpc_ = pr.get()
                            for j in range(4):
                                P.pe(lambda e, j=j: e.matmul(pc_[0:64, :], lhsT=dgw[:, g * 4 + j, :], rhs=zqk[:, g, 1 + j:1 + j + 512], start=(j == 0), stop=(j == 3)),
                                     reads=[Tz[g], Tdgw], writes=[Tpc_])
                            se, Tse = ser.get()
                            P.act(lambda e: e.activation(out=se[:], in_=pc_[0:64, :], func=AF.Exp, bias=nconvb[:, g:g + 1], scale=-1.0), reads=[Tpc_], writes=[Tse])
                            P.act(lambda e: e.activation(out=se[:], in_=se[:], func=AF.Ln, bias=1.0), reads=[Tse], writes=[Tse])
                            P.act(lambda e: e.activation(out=se[:], in_=se[:], func=AF.Exp, scale=-1.0), reads=[Tse], writes=[Tse])
                            if g < 4:
                                P.dve(lambda e: e.scalar_tensor_tensor(out=qT[:, g, :], in0=pc_[0:64, :], scalar=convb[:, g:g + 1], in1=se[:], op0=ALU.add, op1=ALU.mult),
                                      reads=[Tpc_, Tse], writes=[TqT])
                            else:
                                P.dve(lambda e: e.scalar_tensor_tensor(out=kTf[:, g - 4, :], in0=pc_[0:64, :], scalar=convb[:, g:g + 1], in1=se[:], op0=ALU.add, op1=ALU.mult),
                                      reads=[Tpc_, Tse], writes=[TkTf])
                            P.dve(lambda e: e.tensor_copy(out=zqk[:, g, 1:4], in_=zqk[:, g, 513:516]), reads=[Tz[g]], writes=[Tz[g]])
                        return f
                    QK = [qk_proj(0)]
                    for g in range(8):
                        if g + 1 < 8:
                            QK.append(qk_proj(g + 1))
                        QK.append(qk_conv(g))

                    Rb = sm[:, 16:20].unsqueeze(2).to_broadcast([4, 4, 128])
                    gst = {}

                    def g_proj():
                        gst["pi"] = proj(1536, 1540, 4)
                        gst["pf"] = proj(1540, 1544, 4)
                        pi, Tpi = gst["pi"]
                        pf, Tpf = gst["pf"]
                        P.dve(lambda e: e.tensor_scalar(out=g_ig[:], in0=pi[0:4, :], scalar1=bg[:, 0:1], scalar2=None, op0=ALU.add), reads=[Tpi], writes=[Tg_ig])
                        P.dve(lambda e: e.tensor_scalar(out=g_x[:], in0=pf[0:4, :], scalar1=bg[:, 1:2], scalar2=-1.0, op0=ALU.add, op1=ALU.mult), reads=[Tpf], writes=[Tg_x])
                    G = [
                        g_proj,
                        lambda: P.dve(lambda e: e.scalar_tensor_tensor(out=g_a[:], in0=g_x[:], scalar=-1.0, in1=g_x[:], op0=ALU.mult, op1=ALU.max), reads=[Tg_x], writes=[Tg_a]),
                        lambda: P.act(lambda e: e.activation(out=g_b[:], in_=g_a[:], func=AF.Exp, scale=-1.0), reads=[Tg_a], writes=[Tg_b]),
                        lambda: P.act(lambda e: e.activation(out=g_b[:], in_=g_b[:], func=AF.Ln, bias=1.0), reads=[Tg_b], writes=[Tg_b]),
                        lambda: P.dve(lambda e: e.tensor_scalar(out=g_a[:], in0=g_x[:], scalar1=0.0, scalar2=None, op0=ALU.max), reads=[Tg_x, Tg_b], writes=[Tg_a]),
                        lambda: P.dve(lambda e: e.scalar_tensor_tensor(out=g_lf[:], in0=g_b[:], scalar=-1.0, in1=g_a[:], op0=ALU.mult, op1=ALU.subtract),
                                      reads=[Tg_b, Tg_a], writes=[Tg_lf]),
                        lambda: P.dve(lambda e: e.tensor_tensor_scan(out=g_bc[:], data0=msk[:], data1=g_lf[:], initial=0.0, op0=ALU.mult, op1=ALU.add),
                                      reads=[Tmsk, Tg_lf], writes=[Tg_bc]),
                        lambda: P.dve(lambda e: e.tensor_tensor(out=g_u[:], in0=g_ig[:], in1=g_bc[:], op=ALU.subtract), reads=[Tg_ig, Tg_bc], writes=[Tg_u]),
                        lambda: P.dve(lambda e: e.tensor_reduce(out=sm[:, 0:4], in_=v3(g_u[:]), axis=AX.X, op=ALU.max), reads=[Tg_u], writes=[Tsm]),
                        lambda: P.dve(lambda e: e.tensor_copy(out=sm[:, 4:8].unsqueeze(2), in_=v3(g_bc[:])[:, :, 127:128]), reads=[Tg_bc, Tsm], writes=[Tsm]),
                        lambda: P.dve(lambda e: e.tensor_tensor_scan(out=sm[:, 8:12], data0=sm[:, 0:4], data1=sm[:, 4:8], initial=sm[:, 48:49], op0=ALU.max, op1=ALU.add),
                                      reads=[Tsm], writes=[Tsm]),
                        lambda: P.dve(lambda e: e.tensor_copy(out=sm[:, 12:13], in_=sm[:, 48:49]), reads=[Tsm], writes=[Tsm]),
                        lambda: P.dve(lambda e: e.tensor_copy(out=sm[:, 13:16], in_=sm[:, 8:11]), reads=[Tsm], writes=[Tsm]),
                        lambda: P.dve(lambda e: e.tensor_copy(out=sm[:, 48:49], in_=sm[:, 11:12]), reads=[Tsm], writes=[Tsm]),
                        lambda: P.dve(lambda e: e.tensor_tensor(out=sm[:, 16:20], in0=sm[:, 12:16], in1=sm[:, 0:4], op=ALU.max), reads=[Tsm], writes=[Tsm]),
                        lambda: P.dve(lambda e: e.tensor_tensor(out=sm[:, 20:24], in0=sm[:, 12:16], in1=sm[:, 16:20], op=ALU.subtract), reads=[Tsm], writes=[Tsm]),
                        lambda: P.act(lambda e: e.activation(out=sm[:, 24:28], in_=sm[:, 20:24], func=AF.Exp), reads=[Tsm], writes=[Tsm]),
                        lambda: P.dve(lambda e: e.tensor_tensor(out=v3(g_w[:]), in0=v3(g_u[:]), in1=Rb, op=ALU.subtract), reads=[Tg_u, Tsm], writes=[Tg_w]),
                        lambda: P.act(lambda e: e.activation(out=g_w[:], in_=g_w[:], func=AF.Exp), reads=[Tg_w], writes=[Tg_w]),
                        lambda: P.dve(lambda e: e.tensor_tensor(out=v3(g_cl[:]), in0=v3(g_bc[:]), in1=Rb, op=ALU.add), reads=[Tg_bc, Tsm], writes=[Tg_cl]),
                        lambda: P.act(lambda e: e.activation(out=g_cl[:], in_=g_cl[:], func=AF.Exp, scale=-1.0), reads=[Tg_cl], writes=[Tg_cl]),
                    ]

                    def v_proj(c):
                        def f():
                            pv, Tpv = pr.get()
                            for k in range(8):
                                P.pe(lambda e, k=k: e.matmul(pv[:], lhsT=hT[:, k, c * 128:(c + 1) * 128], rhs=winm[:, k, 512:1024], start=(k == 0), stop=(k == 7)),
                                     reads=[Twinm, ThT[k]], writes=[Tpv])
                            P.dve(lambda e: e.tensor_copy(out=vaug[:, c, :, 0:128], in_=pv[:].rearrange("p (h v) -> p h v", v=128)), reads=[Tpv], writes=[Tvaug])
                        return f

                    def o_proj(c):
                        def f():
                            po_, Tpo = pr.get()
                            for k in range(8):
                                P.pe(lambda e, k=k: e.matmul(po_[:], lhsT=hT[:, k, c * 128:(c + 1) * 128], rhs=winm[:, k, 1024:1536], start=(k == 0), stop=(k == 7)),
                                     reads=[Twinm, ThT[k]], writes=[Tpo])
                            P.act(lambda e: e.activation(out=sigo[:, c, :], in_=po_[:], func=AF.Exp, scale=-1.0), reads=[Tpo], writes=[Tsigo])
                            P.act(lambda e: e.activation(out=sigo[:, c, :], in_=sigo[:, c, :], func=AF.Ln, bias=1.0), reads=[Tsigo], writes=[Tsigo])
                            P.act(lambda e: e.activation(out=sigo[:, c, :], in_=sigo[:, c, :], func=AF.Exp, scale=-1.0), reads=[Tsigo], writes=[Tsigo])
                        return f
                    VO = []
                    for c in range(4):
                        VO.append(v_proj(c))
                        VO.append(o_proj(c))

                    FE = []
                    if i + 1 < NT:
                        FE = fe_tile_thunks(P, fe, xbr, i + 1, scA, shA, TscA, hTs[(i + 1) % 2][0], hTs[(i + 1) % 2][1])
                    merge([st_ for st_ in (QK, G, VO, FE) if st_])

                    def kscale(h):
                        pw, Tpw = pr.get()
                        P.pe(lambda e: e.matmul(pw[0:64, :], lhsT=sel4[0:4, h * 64:(h + 1) * 64], rhs=g_w[0:4, :], start=True, stop=True),
                             reads=[Tg_w], writes=[Tpw])
                        P.dve(lambda e: e.scalar_tensor_tensor(out=kpT[:, h, :], in0=kTf[:, h, :], scalar=0.125, in1=pw[0:64, :], op0=ALU.mult, op1=ALU.mult),
                              reads=[TkTf, Tpw], writes=[TkpT])
                    for h in range(4):
                        kscale(h)
                    pc, Tpc = pr.get()
                    for c in range(4):
                        P.pe(lambda e, c=c: e.matmul(pc[:, c * 4:(c + 1) * 4], lhsT=g_cl[0:4, c * 128:(c + 1) * 128], rhs=identf[0:4, 0:4], start=True, stop=True),
                             reads=[Tg_cl], writes=[Tpc])
                    P.act(lambda e: e.copy(out=clampc[:], in_=pc[:, 0:16]), reads=[Tpc], writes=[Tclampc])
                    P.dve(lambda e: e.tensor_tensor(out=sm[:, 32:48].rearrange("p (c h) -> p c h", h=4), in0=sm[:, 24:28].unsqueeze(2).to_broadcast([4, 4, 4]),
                                                    in1=identf[0:4, 0:4].unsqueeze(1).to_broadcast([4, 4, 4]), op=ALU.mult), reads=[Tsm], writes=[Tsm])
                    pa_, Tpa = pr.get()
                    P.pe(lambda e: e.matmul(pa_[0:64, 0:16], lhsT=onesf[0:4, 0:64], rhs=sm[0:4, 32:48], start=True, stop=True), reads=[Tsm], writes=[Tpa])
                    P.act(lambda e: e.copy(out=abc[:], in_=pa_[0:64, 0:16]), reads=[Tpa], writes=[Tabc])
                    for c in range(4):
                        for h in range(4):
                            P.pe(lambda e, c=c, h=h: e.transpose(ptk[:, (c * 4 + h) * 64:(c * 4 + h + 1) * 64], kpT[0:64, h, c * 128:(c + 1) * 128], identb[0:64, 0:64]),
                                 reads=[TkpT], writes=[Tptk])
                    P.act(lambda e: e.copy(out=kptok[:].rearrange("p a b -> p (a b)"), in_=ptk[:]), reads=[Tptk], writes=[Tkptok])
                    cst = {}

                    def X(c):
                        cc = slice(c * 128, (c + 1) * 128)
                        P.dve(lambda e: e.tensor_tensor(out=Sh[:], in0=Sst[:], in1=abc[:, c * 4:(c + 1) * 4].unsqueeze(2).to_broadcast([64, 4, 129]), op=ALU.mult),
                              reads=[TS, Tabc], writes=[TSh])
                        P.act(lambda e: e.copy(out=Shb[:], in_=Sh[:]), reads=[TSh], writes=[TShb])
                        ps_, Tps = pr.get()
                        for h in range(4):
                            P.pe(lambda e, h=h: e.matmul(ps_[:, h * 128:(h + 1) * 128], lhsT=kpT[0:64, h, cc], rhs=qT[0:64, h, cc], start=True, stop=True),
                                 reads=[TkpT, TqT], writes=[Tps])
                        P.dve(lambda e: e.tensor_tensor(out=sTm[:], in0=ps_[:].rearrange("p (h t) -> p h t", t=128),
                                                        in1=trib[:].unsqueeze(1).to_broadcast([128, 4, 128]), op=ALU.mult), reads=[Tps], writes=[TsTm])
                        pns = []
                        for j in range(2):
                            pn, Tpn = pr.get()
                            pd, Tpd = pr.get()
                            for hh in range(2):
                                h = 2 * j + hh
                                P.pe(lambda e, h=h, hh=hh, pn=pn: e.matmul(pn[:, hh * 129:(hh + 1) * 129], lhsT=sTm[:, h, :], rhs=vaug[:, c, h, :], start=True, stop=False),
                                     reads=[TsTm, Tvaug], writes=[Tpn])
                                P.pe(lambda e, h=h, hh=hh, pn=pn: e.matmul(pn[:, hh * 129:(hh + 1) * 129], lhsT=qT[0:64, h, cc], rhs=Shb[0:64, h, :], start=False, stop=True),
                                     reads=[TqT, TShb], writes=[Tpn])
                            for hh in range(2):
                                h = 2 * j + hh
                                P.pe(lambda e, h=h, hh=hh, pd=pd: e.matmul(pd[0:64, hh * 129:(hh + 1) * 129], lhsT=kptok[:, c * 4 + h, :], rhs=vaug[:, c, h, :], start=True, stop=True),
                                     reads=[Tkptok, Tvaug], writes=[Tpd])
                            P.dve(lambda e, j=j, pd=pd: e.tensor_tensor(out=Sst[:, 2 * j:2 * j + 2, :], in0=Sh[:, 2 * j:2 * j + 2, :],
                                                                        in1=pd[0:64, 0:258].rearrange("p (h v) -> p h v", v=129), op=ALU.add), reads=[TSh, Tpd], writes=[TS])
                            pns.append((pn, Tpn))
                        cst[c] = {"pns": pns, "ybs": []}

                    def D1(c):
                        for j in range(2):
                            pn, Tpn = cst[c]["pns"][j]
                            pnv = pn[:, 0:258].rearrange("p (h v) -> p h v", v=129)
                            P.dve(lambda e, pnv=pnv: e.tensor_copy(out=dtmp[:, 6:8].unsqueeze(2), in_=pnv[:, :, 128:129]), reads=[Tpn], writes=[Tdtmp])
                            P.dve(lambda e: e.scalar_tensor_tensor(out=dtmp[:, 0:2], in0=dtmp[:, 6:8], scalar=-1.0, in1=dtmp[:, 6:8],
                                                                   op0=ALU.mult, op1=ALU.max), reads=[Tdtmp], writes=[Tdtmp])
                            P.dve(lambda e, j=j: e.tensor_tensor(out=dtmp[:, 2:4], in0=dtmp[:, 0:2], in1=clampc[:, c * 4 + 2 * j:c * 4 + 2 * j + 2], op=ALU.max),
                                  reads=[Tdtmp, Tclampc], writes=[Tdtmp])
                            P.dve(lambda e: e.reciprocal(out=dtmp[:, 4:6], in_=dtmp[:, 2:4]), reads=[Tdtmp], writes=[Tdtmp])
                            for hh in range(2):
                                h = 2 * j + hh
                                yb, Tyb = ybr.get()
                                P.dve(lambda e, h=h, hh=hh, yb=yb, pn=pn: e.scalar_tensor_tensor(out=yb[:], in0=pn[:, hh * 129:hh * 129 + 128], scalar=dtmp[:, 4 + hh:5 + hh],
                                                                                                  in1=sigo[:, c, h * 128:(h + 1) * 128], op0=ALU.mult, op1=ALU.mult),
                                      reads=[Tpn, Tdtmp, Tsigo], writes=[Tyb])
                                P.act(lambda e, h=h, yb=yb: e.activation(out=junk2[:], in_=yb[:], func=AF.Square, accum_out=ss2[:, h:h + 1]),
                                      reads=[Tyb], writes=[Tjunk2, Tss2])
                                cst[c]["ybs"].append((yb, Tyb))

                    def D2(c):
                        tok = slice(i * 512 + c * 128, i * 512 + (c + 1) * 128)
                        ybs = cst.pop(c)["ybs"]
                        P.dve(lambda e: e.tensor_scalar(out=ss2[:, 4:8], in0=ss2[:, 0:4], scalar1=1.0 / 128, scalar2=EPS, op0=ALU.mult, op1=ALU.add),
                              reads=[Tss2], writes=[Tss2])
                        P.act(lambda e: e.activation(out=ss2[:, 4:8], in_=ss2[:, 4:8], func=AF.Ln), reads=[Tss2], writes=[Tss2])
                        P.act(lambda e: e.activation(out=ss2[:, 4:8], in_=ss2[:, 4:8], func=AF.Exp, scale=-0.5), reads=[Tss2], writes=[Tss2])
                        for h in range(4):
                            yb, Tyb = ybs[h]
                            ybn, Tybn = ybnr.get()
                            P.dve(lambda e, h=h, yb=yb, ybn=ybn: e.tensor_scalar(out=ybn[:], in0=yb[:], scalar1=ss2[:, 4 + h:5 + h], scalar2=None, op0=ALU.mult),
                                  reads=[Tyb, Tss2], writes=[Tybn])
                            P.pe(lambda e, h=h, ybn=ybn: e.transpose(pty[:, h * 128:(h + 1) * 128], ybn[:], identb[:]), reads=[Tybn], writes=[Tpty])
                            P.act(lambda e, h=h: e.mul(out=yTb[:, h, tok], in_=pty[:, h * 128:(h + 1) * 128], mul=gomls[:, h:h + 1]), reads=[Tpty], writes=[TyTb])

                    X(0)
                    D1(0)
                    for c in range(1, 4):
                        X(c)
                        D2(c - 1)
                        D1(c)
                    D2(3)

                a3_front(0)
                for i in range(NT):
                    a3_tile(i)
                if "yTb" in dbg_outs:
                    P.dma("gpsimd", dbg_outs["yTb"], yTb[:], reads=[TyTb])
                finals = P.emit(nc, semstack, finals)

        if "B" in phases:
            with ExitStack() as ph:
                P = Prog()
                wout, Twout = sbt(ph, "wout", [128, 8, D], BF16)
                gta, Tgta = sbt(ph, "gta", [128, D])
                gtf, Tgtf = sbt(ph, "gtf", [128, D])
                gfin, Tgfin = sbt(ph, "gfinb", [128, D])
                dgr = Ring([sbt(ph, "dg%d" % i, [128, 128]) for i in range(2)])
                x1sets = [[sbt(ph, "x1_%d_%d" % (s_, i), [128, D]) for i in range(4)] for s_ in range(2)]
                hfT = sbt(ph, "hfT", [128, 8, 512], BF16)[0]
                ThfT = [Tl("hfT%d" % c) for c in range(8)]
                aT, TaT = sbt(ph, "aT", [128, NJ, 512], BF16)
                sgr = Ring([sbt(ph, "sg%d" % i, [128, 512]) for i in range(4)])
                wgr = Ring([sbt(ph, "wgp%d" % i, [128, 8, 256], BF16) for i in range(2)])
                wur = Ring([sbt(ph, "wup%d" % i, [128, 8, 256], BF16) for i in range(2)])
                wdr = Ring([sbt(ph, "wdp%d" % i, [128, 2, 512], BF16) for i in range(5)])
                fe = make_fe(ph, nxn=4, npt=2)
                pb = Ring([pst(ph, "pb%d" % i, [128, 512]) for i in range(6)])

                P.dma("gpsimd", wout[:], wout_d.rearrange("(k p) n -> p k n", p=128), writes=[Twout])
                P.dma("sync", gfin[:], gfin_d.to_broadcast([128, D]), writes=[Tgfin])

                def bcast(dst, Tdst, col0):
                    for c in range(8):
                        dg, Tdg = dgr.get()
                        P.dve(lambda e, c=c, dg=dg: e.tensor_scalar(out=dg[:], in0=identf[:], scalar1=modc[:, col0 + c:col0 + c + 1], scalar2=None, op0=ALU.mult),
                              writes=[Tdg])
                        pk_, Tpk_ = pb.get()
                        P.pe(lambda e, dg=dg, pk_=pk_: e.matmul(pk_[:, 0:128], lhsT=onesf[:], rhs=dg[:], start=True, stop=True), reads=[Tdg], writes=[Tpk_])
                        P.act(lambda e, c=c, pk_=pk_: e.copy(out=dst[:, c * 128:(c + 1) * 128], in_=pk_[:, 0:128]), reads=[Tpk_], writes=[Tdst])
                bcast(gta, Tgta, 16)
                bcast(gtf, Tgtf, 40)
                wg_v = wg_d.rearrange("(k p) n -> p k n", p=128)
                wu_v = wu_d.rearrange("(k p) n -> p k n", p=128)
                wd_v = wd_d.rearrange("(j p) n -> p j n", p=128)
                xns = {}

                def pro1(t):
                    xs = x1sets[t % 2]
                    for blk in range(4):
                        r0 = t * 512 + blk * 128
                        P.dma("sync", xs[blk][0][:], x_d[r0:r0 + 128, :], writes=[xs[blk][1]])

                    def outproj(blk, half):
                        hs = slice(half * 512, (half + 1) * 512)
                        xb, Txb = xs[blk]
                        po_, Tpo = pb.get()
                        tok = slice(t * 512 + blk * 128, t * 512 + (blk + 1) * 128)
                        for k in range(8):
                            src, Tsrc = (yTa, TyTa) if k < 4 else (yTb, TyTb)
                            P.pe(lambda e, k=k, src=src: e.matmul(po_[:], lhsT=src[:, k % 4, tok], rhs=wout[:, k, hs], start=(k == 0), stop=(k == 7)),
                                 reads=[Twout, Tsrc], writes=[Tpo])
                        sg, Tsg = sgr.get()
                        P.dve(lambda e: e.tensor_tensor(out=sg[:], in0=po_[:], in1=gta[:, hs], op=ALU.mult), reads=[Tpo, Tgta], writes=[Tsg])
                        P.dve(lambda e: e.tensor_tensor(out=xb[:, hs], in0=sg[:], in1=xb[:, hs], op=ALU.add), reads=[Tsg, Txb], writes=[Txb])
                    for blk in range(4):
                        for half in range(2):
                            outproj(blk, half)
                    xns[t] = [fe_stats(P, fe, xs[blk][0][:], xs[blk][1]) for blk in range(4)]
                    if t == 0 and "x1" in dbg_outs:
                        for blk in range(4):
                            P.dma("sync", dbg_outs["x1"][blk * 128:(blk + 1) * 128, :], xs[blk][0][:], reads=[xs[blk][1]])

                def pro2(t):
                    for blk, (xn, Txn) in enumerate(xns.pop(t)):
                        fe_trans(P, fe, xn, Txn, scF, shF, TscF, hfT, ThfT, blk)

                pieces = []
                for t_ in range(NT):
                    for jp in range(NJ // 2):
                        pieces.append(("g", t_, jp, 0))
                        pieces.append(("u", t_, jp, 0))
                    for half in range(2):
                        for jp in range(NJ // 2):
                            pieces.append(("d", t_, jp, half))
                wslot = {}
                wstate = {"issued": 0}

                def issue_upto(n):
                    while wstate["issued"] < min(n, len(pieces)):
                        kind, t_, jp, half = pieces[wstate["issued"]]
                        if kind == "g":
                            buf, Tb = wgr.get()
                            P.dma("gpsimd", buf[:], wg_v[:, :, jp * 256:(jp + 1) * 256], writes=[Tb])
                        elif kind == "u":
                            buf, Tb = wur.get()
                            P.dma("gpsimd", buf[:], wu_v[:, :, jp * 256:(jp + 1) * 256], writes=[Tb])
                        else:
                            buf, Tb = wdr.get()
                            P.dma("gpsimd", buf[:], wd_v[:, 2 * jp:2 * jp + 2, half * 512:(half + 1) * 512], writes=[Tb])
                        wslot[wstate["issued"]] = (buf, Tb)
                        wstate["issued"] += 1

                def take(kind, t_, jp, half):
                    n = wstate.setdefault("next", 0)
                    assert pieces[n] == (kind, t_, jp, half), (pieces[n], kind, t_, jp, half)
                    issue_upto(n + 1)
                    wstate["next"] = n + 1
                    return wslot.pop(n)

                def advance():
                    issue_upto(wstate.get("next", 0) + 4)

                def up(t, jp):
                    wgp, Twgp = take("g", t, jp, 0)
                    wup, Twup = take("u", t, jp, 0)
                    for jj in range(2):
                        j = 2 * jp + jj
                        pg, Tpg = pb.get()
                        pu, Tpu = pb.get()
                        for k in range(8):
                            P.pe(lambda e, k=k, jj=jj, pg=pg: e.matmul(pg[:], lhsT=wgp[:, k, jj * 128:(jj + 1) * 128], rhs=hfT[:, k, :], start=(k == 0), stop=(k == 7)),
                                 reads=[Twgp, ThfT[k]], writes=[Tpg])
                        for k in range(8):
                            P.pe(lambda e, k=k, jj=jj, pu=pu: e.matmul(pu[:], lhsT=wup[:, k, jj * 128:(jj + 1) * 128], rhs=hfT[:, k, :], start=(k == 0), stop=(k == 7)),
                                 reads=[Twup, ThfT[k]], writes=[Tpu])
                        sg, Tsg = sgr.get()
                        P.act(lambda e, pg=pg, sg=sg: e.activation(out=sg[:], in_=pg[:], func=AF.Silu), reads=[Tpg], writes=[Tsg])
                        P.dve(lambda e, j=j, pu=pu, sg=sg: e.tensor_tensor(out=aT[:, j, :], in0=sg[:], in1=pu[:], op=ALU.mult), reads=[Tsg, Tpu], writes=[TaT])
                    advance()

                def down(t, half):
                    xs = x1sets[t % 2]
                    hs = slice(half * 512, (half + 1) * 512)
                    accs = [pb.get() for _ in range(4)]

                    def piece(jp):
                        wdp, Twdp = take("d", t, jp, half)
                        for jj in range(2):
                            j = 2 * jp + jj
                            for blk in range(4):
                                P.pe(lambda e, j=j, jj=jj, blk=blk: e.matmul(accs[blk][0][:], lhsT=aT[:, j, blk * 128:(blk + 1) * 128], rhs=wdp[:, jj, :],
                                                                             start=(j == 0), stop=(j == NJ - 1)), reads=[TaT, Twdp], writes=[accs[blk][1]])
                        advance()
                    for jp in range(NJ // 2):
                        piece(jp)
                    evs = []
                    for blk in range(4):
                        sg, Tsg = sgr.get()
                        P.act(lambda e, blk=blk, sg=sg: e.copy(out=sg[:], in_=accs[blk][0][:]), reads=[accs[blk][1]], writes=[Tsg])
                        evs.append((sg, Tsg))
                    for blk in range(4):
                        xb, Txb = xs[blk]
                        sg, Tsg = evs[blk]
                        P.dve(lambda e, sg=sg: e.tensor_tensor(out=sg[:], in0=sg[:], in1=gtf[:, hs], op=ALU.mult), reads=[Tsg, Tgtf], writes=[Tsg])
                        P.dve(lambda e, xb=xb, sg=sg: e.tensor_tensor(out=xb[:, hs], in0=sg[:], in1=xb[:, hs], op=ALU.add), reads=[Tsg, Txb], writes=[Txb])

                def final(t, blk):
                    xb, Txb = x1sets[t % 2][blk]
                    st, Tst = fe["stat"].get()
                    junk, Tjunk = fe["junk"].get()
                    P.act(lambda e: e.activation(out=junk[:], in_=xb[:], func=AF.Square, accum_out=st[:, 0:1]), reads=[Txb], writes=[Tjunk, Tst])
                    P.dve(lambda e: e.tensor_scalar(out=st[:, 1:2], in0=st[:, 0:1], scalar1=1.0 / D, scalar2=EPS, op0=ALU.mult, op1=ALU.add), reads=[Tst], writes=[Tst])
                    P.act(lambda e: e.activation(out=st[:, 2:3], in_=st[:, 1:2], func=AF.Sqrt), reads=[Tst], writes=[Tst])
                    P.dve(lambda e: e.reciprocal(out=st[:, 3:4], in_=st[:, 2:3]), reads=[Tst], writes=[Tst])
                    P.dve(lambda e: e.scalar_tensor_tensor(out=xb[:], in0=xb[:], scalar=st[:, 3:4], in1=gfin[:], op0=ALU.mult, op1=ALU.mult),
                          reads=[Txb, Tst, Tgfin], writes=[Txb])
                    r0 = t * 512 + blk * 128
                    P.dma("sync", out_d[r0:r0 + 128, :], xb[:], reads=[Txb])

                advance()
                pro1(0)
                pro2(0)
                for i in range(NT):
                    for jp in range(NJ // 2):
                        up(i, jp)
                        if jp == 3 and i + 1 < NT:
                            pro1(i + 1)
                    if i + 1 < NT:
                        pro2(i + 1)
                    down(i, 0)
                    down(i, 1)
                    for blk in range(4):
                        final(i, blk)
                finals = P.emit(nc, semstack, finals)
    return nc


def _col(v, n):
    return np.ascontiguousarray(np.asarray(v, np.float32).reshape(n, 128).T)


def shared_inputs(inp):
    f = lambda a: np.ascontiguousarray(np.asarray(a, np.float32))
    w_in = f(inp["w_in"][0])
    winl = np.concatenate([w_in[:, 0:672], w_in[:, 656:672], w_in[:, 640:656]], axis=1)
    winm = w_in[:, 672:2216]
    w_uq = f(inp["w_uq"][0]).reshape(384, 8, 96)
    wq = np.zeros((384, 8, 256), np.float32)
    wq[:, :, 0:32] = w_uq[:, :, 64:96]
    wq[:, :, 64:128] = w_uq[:, :, 0:64]
    wq[:, :, 128:144] = w_uq[:, :, 80:96]
    wq[:, :, 144:160] = w_uq[:, :, 64:80]
    w_ukv = f(inp["w_ukv"][0]).reshape(256, 8, 128)
    wkv = np.zeros((256, 8, 192), np.float32)
    wkv[:, :, 64:128] = w_ukv[:, :, 0:64]
    wkv[:, :, 128:192] = w_ukv[:, :, 64:128]
    conv_w = f(inp["conv_w"][0])
    convw = np.ascontiguousarray(conv_w.T.reshape(8, 64, 4).transpose(1, 0, 2).reshape(64, 32))
    convb = np.ascontiguousarray(f(inp["conv_b"][0]).reshape(8, 64).T)
    bg = np.ascontiguousarray(f(inp["b_gates"][0]).reshape(2, 4).T)
    ident = np.eye(128, dtype=np.float32)
    tri = np.triu(np.ones((128, 128), np.float32))
    sel4 = np.zeros((4, 256), np.float32)
    for h in range(4):
        sel4[h, h * 64:(h + 1) * 64] = 1.0
    inv = (10000.0 ** (-np.arange(16, dtype=np.float64) / 16.0))
    ropec = np.zeros((32, 4), np.float32)
    ropec[:, 0] = np.tile(inv / (2 * np.pi), 2)
    TWO_PI = 6.28318
    ropec[:, 1] = np.concatenate([-np.ones(16), np.ones(16)])
    ropec[:, 2] = ropec[:, 1] * TWO_PI
    ropec[:, 3] = TWO_PI
    return {
        "w_ada": f(inp["w_ada"][0]), "badaT": _col(inp["b_ada"][0], 48), "gmixT": _col(inp["g_mix"][0], 8),
        "gffnT": _col(inp["g_ffn"][0], 8), "gfin": f(inp["g_final"]).reshape(1, D),
        "winl": np.ascontiguousarray(winl), "winm": np.ascontiguousarray(winm),
        "gqT": _col(inp["g_q"][0], 3), "gkvT": _col(inp["g_kv"][0], 2),
        "wq": wq.reshape(384, 8 * 256), "wkv": wkv.reshape(256, 8 * 192),
        "convw": convw, "convb": convb, "bg": bg,
        "gomlaT": _col(inp["g_out_mla"][0], 4), "gomlsT": _col(inp["g_out_mlstm"][0], 4),
        "w_out": f(inp["w_out"][0]), "w_gate": f(inp["w_gate"][0]), "w_up": f(inp["w_up"][0]),
        "w_down": f(inp["w_down"][0]),
        "ident": ident, "tri": tri, "sel4": sel4, "ropec": ropec,
    }


def core_inputs(inp, shared, b):
    d = dict(shared)
    d["x"] = np.ascontiguousarray(np.asarray(inp["x"][b], np.float32))
    d["cT"] = _col(inp["c"][b], 8)
    d["pos"] = np.ascontiguousarray(np.asarray(inp["positions"][b], np.int32).reshape(1, S))
    return d


def kernel(**inputs):
    nc = build_program()
    shared = shared_inputs(inputs)
    in_maps = [core_inputs(inputs, shared, b) for b in range(8)]
    res = run_bass_kernel_spmd(nc, in_maps, core_ids=list(range(8)))
    return np.stack([np.asarray(r["out"], np.float32) for r in res.results], axis=0)
```

```python
import math
from contextlib import ExitStack

import numpy as np
import concourse.bass as bass
import concourse.mybir as mybir
from concourse.bass_utils import run_bass_kernel_spmd

F32 = mybir.dt.float32
BF16 = mybir.dt.bfloat16
I32 = mybir.dt.int32
AF = mybir.ActivationFunctionType
ALU = mybir.AluOpType
AX = mybir.AxisListType

S = 4096
D = 1024
NT = 8
DFF = 2816
NJ = 22
EPS = 1e-6
SCALE = 96 ** -0.5


class Tl:
    __slots__ = ("name", "w", "r")

    def __init__(self, name=""):
        self.name = name
        self.w = None
        self.r = []


class Op:
    __slots__ = ("eng", "fn", "deps", "dma", "signal", "token", "pre", "prog")

    def __init__(self, eng, fn, dma):
        self.eng = eng
        self.fn = fn
        self.dma = dma
        self.deps = []
        self.signal = False
        self.token = None
        self.pre = None


class Prog:
    ENGS = ("tensor", "vector", "scalar", "gpsimd", "sync")
    NRING = 6

    _uid = [0]

    def __init__(self):
        self.ops = {e: [] for e in self.ENGS}

    @classmethod
    def _sem(cls, nc, semstack):
        cls._uid[0] += 1
        return semstack.enter_context(nc.semaphore("sem%d" % cls._uid[0]))

    def add(self, eng, fn, reads=(), writes=(), dma=False):
        op = Op(eng, fn, dma)
        op.prog = self
        deps = {}

        def need(d, kind):
            if d is None or d.prog is not self:
                return
            if d.dma:
                deps[id(d)] = d
                return
            if d.eng == eng and not dma and eng == "tensor":
                return
            deps[id(d)] = d

        for t in reads:
            need(t.w, "raw")
        for t in writes:
            need(t.w, "waw")
            for r in t.r:
                need(r, "war")
        op.deps = list(deps.values())
        for d in op.deps:
            d.signal = True
        for t in reads:
            t.r.append(op)
        for t in writes:
            t.w = op
            t.r = []
        self.ops[eng].append(op)
        return op

    def pe(self, fn, reads=(), writes=()):
        return self.add("tensor", fn, reads, writes)

    def dve(self, fn, reads=(), writes=()):
        return self.add("vector", fn, reads, writes)

    def act(self, fn, reads=(), writes=()):
        return self.add("scalar", fn, reads, writes)

    def pool(self, fn, reads=(), writes=()):
        return self.add("gpsimd", fn, reads, writes)

    def dma(self, q, out, in_, reads=(), writes=(), **kw):
        return self.add(q, lambda e: e.dma_start(out=out, in_=in_, **kw), reads, writes, dma=True)

    def emit(self, nc, semstack, prev_finals):
        esem = {e: self._sem(nc, semstack) for e in self.ENGS}
        qsem = {}
        for e in self.ENGS:
            if any(o.dma for o in self.ops[e]):
                qsem[e] = [self._sem(nc, semstack) for _ in range(self.NRING)]
        finals = {}
        for e in self.ENGS:
            last = None
            for o in self.ops[e]:
                if not o.dma:
                    last = o
            if last is not None:
                last.signal = True
            cnt = 0
            nd = 0
            for o in self.ops[e]:
                if o.dma:
                    slot = nd % self.NRING
                    rnd = nd // self.NRING
                    if rnd > 0:
                        o.pre = (qsem[e][slot], 16 * rnd)
                    o.token = (qsem[e][slot], 16 * (rnd + 1))
                    finals[("q", e, slot)] = o.token
                    nd += 1
                elif o.signal:
                    cnt += 1
                    o.token = (esem[e], cnt)
                    finals[("e", e)] = o.token

        with nc.Block() as block:
            def run(e):
                def body(eng):
                    waited = {}

                    def wait(tok):
                        sem, val = tok
                        k = id(sem)
                        if waited.get(k, 0) < val:
                            eng.wait_ge(sem, val)
                            waited[k] = val

                    for tok in prev_finals:
                        wait(tok)
                    for o in self.ops[e]:
                        for d in o.deps:
                            wait(d.token)
                        if o.pre is not None:
                            wait(o.pre)
                        ins = o.fn(eng)
                        if o.dma:
                            ins.then_inc(o.token[0], 16)
                        elif o.signal:
                            ins.then_inc(o.token[0], 1)
                    if e == "sync":
                        for k, tok in finals.items():
                            if k[0] == "q":
                                wait(tok)
                return body

            block.tensor(run("tensor"))
            block.vector(run("vector"))
            block.scalar(run("scalar"))
            block.gpsimd(run("gpsimd"))
            block.sync(run("sync"))
        return list(finals.values())


class Ring:
    def __init__(self, items):
        self.items = items
        self.i = 0

    def get(self):
        it = self.items[self.i % len(self.items)]
        self.i += 1
        return it


def build_program(dbg=None, phases=("A3", "B")):
    nc = bass.Bass("TRN2", target_bir_lowering=False)

    def din(name, shape, dt=F32):
        return nc.dram_tensor(name, list(shape), dt, kind="ExternalInput").ap()

    x_d = din("x", [S, D])
    cT_d = din("cT", [128, 8])
    pos_d = din("pos", [1, S], I32)
    wada_d = din("w_ada", [D, 6 * D])
    bada_d = din("badaT", [128, 48])
    gmix_d = din("gmixT", [128, 8])
    gffn_d = din("gffnT", [128, 8])
    gfin_d = din("gfin", [1, D])
    winl_d = din("winl", [D, 704])
    winm_d = din("winm", [D, 1544])
    gq_d = din("gqT", [128, 3])
    gkv_d = din("gkvT", [128, 2])
    wq_d = din("wq", [384, 8 * 256])
    wkv_d = din("wkv", [256, 8 * 192])
    convw_d = din("convw", [64, 32])
    convb_d = din("convb", [64, 8])
    bg_d = din("bg", [4, 2])
    gomla_d = din("gomlaT", [128, 4])
    gomls_d = din("gomlsT", [128, 4])
    wout_d = din("w_out", [D, D])
    wg_d = din("w_gate", [D, DFF])
    wu_d = din("w_up", [D, DFF])
    wd_d = din("w_down", [DFF, D])
    ident_d = din("ident", [128, 128])
    tri_d = din("tri", [128, 128])
    sel4_d = din("sel4", [4, 256])
    ropec_d = din("ropec", [32, 4])
    out_d = nc.dram_tensor("out", [S, D], F32, kind="ExternalOutput").ap()
    dbg_outs = {}
    if dbg:
        for name, shape in dbg.items():
            dbg_outs[name] = nc.dram_tensor("dbg_" + name, list(shape), F32, kind="ExternalOutput").ap()

    semstack = ExitStack()
    top = ExitStack()
    with semstack, top:
        uid = [0]

        def sbt(stack, name, shape, dt=F32):
            uid[0] += 1
            return stack.enter_context(nc.sbuf_tensor("s%d_%s" % (uid[0], name), list(shape), dt)), Tl(name)

        def pst(stack, name, shape, dt=F32):
            uid[0] += 1
            return stack.enter_context(nc.psum_tensor("p%d_%s" % (uid[0], name), list(shape), dt)), Tl(name)

        identf, Tidentf = sbt(top, "identf", [128, 128])
        identb, Tidentb = sbt(top, "identb", [128, 128], BF16)
        trib, Ttrib = sbt(top, "trib", [128, 128], BF16)
        onesf, Tonesf = sbt(top, "onesf", [128, 128])
        onesb, Tonesb = sbt(top, "onesb", [128, 128], BF16)
        sel4, Tsel4 = sbt(top, "sel4", [4, 256])
        ropec, Tropec = sbt(top, "ropec", [32, 4])
        modc, Tmodc = sbt(top, "modc", [128, 48])
        scA, TscA = sbt(top, "scA", [128, 8])
        scF, TscF = sbt(top, "scF", [128, 8])
        gq, Tgq = sbt(top, "gq", [128, 3])
        gkv, Tgkv = sbt(top, "gkv", [128, 2])
        convw, Tconvw = sbt(top, "convw", [64, 32])
        convb, Tconvb = sbt(top, "convb", [64, 8])
        bg, Tbg = sbt(top, "bg", [4, 2])
        gomla, Tgomla = sbt(top, "gomla", [128, 4])
        gomls, Tgomls = sbt(top, "gomls", [128, 4])
        silc, Tsilc = sbt(top, "silc", [128, 8])
        bada, Tbada = sbt(top, "bada", [128, 48])
        gffn, Tgffn = sbt(top, "gffn", [128, 8])
        yTa, TyTa = sbt(top, "yTa", [128, 4, S], BF16)
        Tpar = Tl("params")

        finals = []

        def dump(P, name, ap, T):
            if name in dbg_outs:
                P.dma("gpsimd", dbg_outs[name], ap, reads=[T])

        def fe_stats(P, fe, xb, Txb):
            junk, Tjunk = fe["junk"].get()
            st, Tst = fe["stat"].get()
            xn, Txn = fe["xn"].get()
            P.act(lambda e: e.activation(out=junk[:], in_=xb, func=AF.Square, accum_out=st[:, 0:1]),
                  reads=[Txb], writes=[Tjunk, Tst])
            P.dve(lambda e: e.tensor_scalar(out=st[:, 1:2], in0=st[:, 0:1], scalar1=1.0 / D, scalar2=EPS,
                                            op0=ALU.mult, op1=ALU.add), reads=[Tst], writes=[Tst])
            P.act(lambda e: e.activation(out=st[:, 2:3], in_=st[:, 1:2], func=AF.Ln), reads=[Tst], writes=[Tst])
            P.act(lambda e: e.activation(out=st[:, 3:4], in_=st[:, 2:3], func=AF.Exp, scale=-0.5), reads=[Tst], writes=[Tst])
            P.dve(lambda e: e.tensor_scalar(out=xn[:], in0=xb, scalar1=st[:, 3:4], scalar2=None, op0=ALU.mult),
                  reads=[Txb, Tst], writes=[Txn])
            return xn, Txn

        def fe_trans(P, fe, xn, Txn, sc, sh, Tsc, hT, ThT, blk):
            pT, TpT = fe["pT"].get()
            for c in range(8):
                P.pe(lambda e, c=c: e.transpose(pT[:, c * 128:(c + 1) * 128], xn[:, c * 128:(c + 1) * 128], identb[:]),
                     reads=[Txn, Tidentb], writes=[TpT])
            for c in range(8):
                if fe.get("evac", "act") == "dve":
                    P.dve(lambda e, c=c: e.tensor_scalar(out=hT[:, c, blk * 128:(blk + 1) * 128], in0=pT[:, c * 128:(c + 1) * 128],
                                                         scalar1=sc[:, c:c + 1], scalar2=sh[:, c:c + 1], op0=ALU.mult, op1=ALU.add),
                          reads=[TpT, Tsc], writes=[ThT[c]])
                else:
                    P.act(lambda e, c=c: e.activation(out=hT[:, c, blk * 128:(blk + 1) * 128], in_=pT[:, c * 128:(c + 1) * 128],
                                                      func=AF.Identity, bias=sh[:, c:c + 1], scale=sc[:, c:c + 1]),
                          reads=[TpT, Tsc], writes=[ThT[c]])

        def frontend(P, fe, xb, Txb, sc, sh, Tsc, hT, ThT, blk):
            xn, Txn = fe_stats(P, fe, xb, Txb)
            fe_trans(P, fe, xn, Txn, sc, sh, Tsc, hT, ThT, blk)

        def fe_tile(P, fe, xbr, t, sc, sh, Tsc, hT, ThT):
            xs = []
            for blk in range(4):
                xb, Txb = xbr.get()
                r0 = t * 512 + blk * 128
                P.dma("sync", xb[:], x_d[r0:r0 + 128, :], writes=[Txb])
                xs.append((xb, Txb))
            xn = [None] * 4
            xn[0] = fe_stats(P, fe, xs[0][0][:], xs[0][1])
            for blk in range(4):
                if blk + 1 < 4:
                    xn[blk + 1] = fe_stats(P, fe, xs[blk + 1][0][:], xs[blk + 1][1])
                fe_trans(P, fe, xn[blk][0], xn[blk][1], sc, sh, Tsc, hT, ThT, blk)

        def fe_tile_thunks(P, fe, xbr, t, sc, sh, Tsc, hT, ThT):
            xs = [None] * 4
            xn = [None] * 4

            def load():
                for blk in range(4):
                    xb, Txb = xbr.get()
                    r0 = t * 512 + blk * 128
                    P.dma("sync", xb[:], x_d[r0:r0 + 128, :], writes=[Txb])
                    xs[blk] = (xb, Txb)

            def stats(blk):
                def f():
                    xn[blk] = fe_stats(P, fe, xs[blk][0][:], xs[blk][1])
                return f

            def trans(blk):
                def f():
                    fe_trans(P, fe, xn[blk][0], xn[blk][1], sc, sh, Tsc, hT, ThT, blk)
                return f
            return [load, stats(0), stats(1), trans(0), stats(2), trans(1), stats(3), trans(2), trans(3)]

        def merge(streams):
            items = []
            for si, st_ in enumerate(streams):
                n = len(st_)
                for j, th in enumerate(st_):
                    items.append(((j + 0.5) / n, si, j, th))
            items.sort(key=lambda t: (t[0], t[1], t[2]))
            for it in items:
                it[3]()

        def make_fe(stack, nxn=2, npt=2):
            fe = {}
            fe["junk"] = Ring([sbt(stack, "fe_junk%d" % i, [128, D], BF16) for i in range(1)])
            fe["stat"] = Ring([sbt(stack, "fe_st%d" % i, [128, 4]) for i in range(4)])
            fe["xn"] = Ring([sbt(stack, "fe_xn%d" % i, [128, D], BF16) for i in range(nxn)])
            fe["pT"] = Ring([pst(stack, "fe_pT%d" % i, [128, D], BF16) for i in range(npt)])
            return fe

        def rsqrt_inplace(P, buf, Tbuf, src, Tsrc, scale, eps):
            P.dve(lambda e: e.tensor_scalar(out=buf, in0=src, scalar1=scale, scalar2=eps, op0=ALU.mult, op1=ALU.add),
                  reads=[Tsrc], writes=[Tbuf])
            P.act(lambda e: e.activation(out=buf, in_=buf, func=AF.Ln), reads=[Tbuf], writes=[Tbuf])
            P.act(lambda e: e.activation(out=buf, in_=buf, func=AF.Exp, scale=-0.5), reads=[Tbuf], writes=[Tbuf])

        with ExitStack() as ph:
            P = Prog()
            cT, TcT = sbt(ph, "cT", [128, 8])
            gmix, Tgmix = sbt(ph, "gmix", [128, 8])
            tmp8, Ttmp8 = sbt(ph, "tmp8", [128, 8])
            wst = Ring([sbt(ph, "wada%d" % i, [128, 8, 512]) for i in range(2)])
            pm, Tpm = pst(ph, "pm", [128, 512])

            for dst, src in ((identf, ident_d), (sel4, sel4_d), (ropec, ropec_d), (gq, gq_d), (gkv, gkv_d),
                             (convw, convw_d), (convb, convb_d), (bg, bg_d), (gomla, gomla_d), (gomls, gomls_d)):
                P.dma("sync", dst[:], src, writes=[Tpar])
            Tidentf.w = Tpar.w
            P.dma("sync", cT[:], cT_d, writes=[TcT])
            P.dma("sync", bada[:], bada_d, writes=[Tbada])
            P.dma("sync", gmix[:], gmix_d, writes=[Tgmix])
            P.dma("sync", gffn[:], gffn_d, writes=[Tgffn])
            P.dma("gpsimd", identb[:], ident_d, writes=[Tidentb])
            P.dma("gpsimd", trib[:], tri_d, writes=[Ttrib])
            P.dve(lambda e: e.memset(onesf[:], 1.0), writes=[Tonesf])
            P.dve(lambda e: e.memset(onesb[:], 1.0), writes=[Tonesb])
            P.act(lambda e: e.activation(out=silc[:], in_=cT[:], func=AF.Silu), reads=[TcT], writes=[Tsilc])
            wada_v = wada_d.rearrange("(k p) n -> p k n", p=128)
            for pc in range(4):
                wt, Twt = wst.get()
                P.dma("sync", wt[:], wada_v[:, :, pc * 512:(pc + 1) * 512], writes=[Twt])
                for jj in range(4):
                    j = pc * 4 + jj
                    for k in range(8):
                        P.pe(lambda e, wt=wt, jj=jj, j=j, k=k: e.matmul(
                            pm[:, j:j + 1], lhsT=wt[:, k, jj * 128:(jj + 1) * 128], rhs=silc[:, k:k + 1],
                            start=(k == 0), stop=(k == 7)), reads=[Twt, Tsilc], writes=[Tpm])
            P.dve(lambda e: e.tensor_tensor(out=modc[:, 0:16], in0=pm[:, 0:16], in1=bada[:, 0:16], op=ALU.add),
                  reads=[Tpm, Tbada], writes=[Tmodc])
            P.dve(lambda e: e.tensor_scalar(out=tmp8[:], in0=modc[:, 8:16], scalar1=1.0, scalar2=None, op0=ALU.add),
                  reads=[Tmodc], writes=[Ttmp8])
            P.dve(lambda e: e.tensor_tensor(out=scA[:], in0=tmp8[:], in1=gmix[:], op=ALU.mult),
                  reads=[Ttmp8, Tgmix], writes=[TscA])

            finals = P.emit(nc, semstack, finals)

        shA = modc[:, 0:8]
        shF = modc[:, 24:32]

        with ExitStack() as pa:
            qnT, TqnT = sbt(pa, "qnT", [128, 3, S], BF16)
            kvnT, TkvnT = sbt(pa, "kvnT", [128, 2, S], BF16)
            krT, TkrT = sbt(pa, "krT", [32, S], BF16)
            cosT, TcosT = sbt(pa, "cosT", [32, S])
            sinT, TsinT = sbt(pa, "sinT", [32, S])
            wq, Twq = sbt(pa, "wq", [128, 3, 8 * 256], BF16)
            wkv, Twkv = sbt(pa, "wkv", [128, 2, 8 * 192], BF16)

            with ExitStack() as ph:
                P = Prog()
                winl, Twinl = sbt(ph, "winl", [128, 8, 704], BF16)
                xbr = Ring([sbt(ph, "xb%d" % i, [128, D]) for i in range(4)])
                hTr = Ring([(sbt(ph, "hT%d" % i, [128, 8, 512], BF16)[0], [Tl("hT%d_%d" % (i, c)) for c in range(8)]) for i in range(2)])
                fe = make_fe(ph)
                fe["evac"] = "dve"
                sq, Tsq = sbt(ph, "sq", [128, 3, 512])
                rstd, Trstd = sbt(ph, "rstd", [128, 512])
                posi, Tposi = sbt(ph, "posi", [32, 512], I32)
                ry, Try = sbt(ph, "ry", [32, 512])
                rn, Trn = posi, Tposi
                rf, Trf = sbt(ph, "rf", [32, 512])
                rg, Trg = sbt(ph, "rg", [32, 512])
                t1, Tt1 = rf, Trf
                t2, Tt2 = rg, Trg
                pl = Ring([pst(ph, "pl%d" % i, [128, 512]) for i in range(5)])
                pada = pst(ph, "pada", [128, 512])

                wst1, Twst1 = sbt(ph, "wada_bg", [128, 8, 256])
                tmp8b, Ttmp8b = sbt(ph, "tmp8b", [128, 8])
                wada_v1 = wada_d.rearrange("(k p) n -> p k n", p=128)

                def ada_piece(q_):
                    def f():
                        c0 = 2048 + 256 * q_
                        P.dma("sync", wst1[:], wada_v1[:, :, c0:c0 + 256], writes=[Twst1])
                        pt, Tpt = pada
                        for jj in range(2):
                            for k in range(8):
                                P.pe(lambda e, jj=jj, k=k: e.matmul(pt[:, jj:jj + 1], lhsT=wst1[:, k, jj * 128:(jj + 1) * 128], rhs=silc[:, k:k + 1],
                                                                    start=(k == 0), stop=(k == 7)), reads=[Twst1], writes=[Tpt])
                        j0 = 16 + 2 * q_
                        P.dve(lambda e: e.tensor_tensor(out=modc[:, j0:j0 + 2], in0=pt[:, 0:2], in1=bada[:, j0:j0 + 2], op=ALU.add),
                              reads=[Tpt], writes=[Tmodc])
                    return f
                P.dma("gpsimd", winl[:], winl_d.rearrange("(k p) n -> p k n", p=128), writes=[Twinl])
                P.dma("gpsimd", wq[:], wq_d.rearrange("(k p) n -> p k n", p=128), writes=[Twq])
                P.dma("gpsimd", wkv[:], wkv_d.rearrange("(k p) n -> p k n", p=128), writes=[Twkv])

                hTs = {}

                def a1_front(i):
                    hTs[i] = hTr.get()
                    fe_tile(P, fe, xbr, i, scA, shA, TscA, hTs[i][0], hTs[i][1])

                def a1_tile(i):
                    cols = slice(i * 512, (i + 1) * 512)
                    hT, ThT = hTs.pop(i)

                    def r0():
                        P.dma("sync", posi[:], pos_d[:, cols].to_broadcast([32, 512]), writes=[Tposi])
                        P.dve(lambda e: e.tensor_copy(out=ry[:], in_=posi[:]), reads=[Tposi], writes=[Try])
                        P.dve(lambda e: e.tensor_scalar(out=ry[:], in0=ry[:], scalar1=ropec[:, 0:1], scalar2=None, op0=ALU.mult),
                              reads=[Try, Tpar], writes=[Try])

                    def rw(which):
                        def f():
                            if which == 1:
                                P.dve(lambda e: e.tensor_scalar(out=ry[:], in0=ry[:], scalar1=0.25, scalar2=None, op0=ALU.add),
                                      reads=[Try], writes=[Try])
                            P.dve(lambda e: e.tensor_copy(out=rn[:], in_=ry[:]), reads=[Try], writes=[Trn])
                            P.dve(lambda e: e.tensor_copy(out=rf[:], in_=rn[:]), reads=[Trn], writes=[Trf])
                            P.dve(lambda e: e.tensor_tensor(out=rg[:], in0=ry[:], in1=rf[:], op=ALU.subtract),
                                  reads=[Try, Trf], writes=[Trg])
                            if which == 0:
                                P.act(lambda e: e.activation(out=sinT[:, cols], in_=rg[:], func=AF.Sin, scale=ropec[:, 2:3]),
                                      reads=[Trg, Tpar], writes=[TsinT])
                            else:
                                P.act(lambda e: e.activation(out=cosT[:, cols], in_=rg[:], func=AF.Sin, scale=ropec[:, 3:4]),
                                      reads=[Trg, Tpar], writes=[TcosT])
                        return f
                    ROPE = [r0, rw(0), rw(1)]

                    st_ = {}

                    def inproj(name, c0, c1, M):
                        def f():
                            pt, Tpt = pl.get()
                            for k in range(8):
                                P.pe(lambda e, k=k: e.matmul(pt[0:M, :], lhsT=winl[:, k, c0:c1], rhs=hT[:, k, :],
                                                             start=(k == 0), stop=(k == 7)),
                                     reads=[Twinl, ThT[k]], writes=[Tpt])
                            st_[name] = (pt, Tpt)
                        return f

                    def lat_stats(names, nch):
                        def f():
                            for m, nm in enumerate(names):
                                pt, Tpt = st_[nm]
                                P.act(lambda e, m=m, pt=pt: e.activation(out=sq[:, m, :], in_=pt[:], func=AF.Square),
                                      reads=[Tpt], writes=[Tsq])
                            ps_, Tps = pl.get()
                            for m in range(nch):
                                P.pe(lambda e, m=m: e.matmul(ps_[:], lhsT=onesf[:], rhs=sq[:, m, :], start=(m == 0), stop=(m == nch - 1)),
                                     reads=[Tsq, Tonesf], writes=[Tps])
                            rsqrt_inplace(P, rstd[:], Trstd, ps_[:], Tps, 1.0 / (128 * nch), EPS)
                        return f

                    def lat_final(names, g, dst, Tdst):
                        def f():
                            for m, nm in enumerate(names):
                                pt, Tpt = st_.pop(nm)
                                P.dve(lambda e, m=m, pt=pt: e.scalar_tensor_tensor(
                                    out=dst[:, m, cols], in0=pt[:], scalar=g[:, m:m + 1], in1=rstd[:], op0=ALU.mult, op1=ALU.mult),
                                    reads=[Tpt, Trstd, Tpar], writes=[Tdst])
                        return f

                    def kr_rope():
                        pkr, Tpkr = st_.pop("kr")
                        pks, Tpks = st_.pop("ks")
                        P.dve(lambda e: e.tensor_tensor(out=t1[:], in0=pkr[0:32, :], in1=cosT[:, cols], op=ALU.mult),
                              reads=[Tpkr, TcosT], writes=[Tt1])
                        P.dve(lambda e: e.tensor_tensor(out=t2[:], in0=pks[0:32, :], in1=sinT[:, cols], op=ALU.mult),
                              reads=[Tpks, TsinT], writes=[Tt2])
                        P.dve(lambda e: e.tensor_tensor(out=krT[:, cols], in0=t1[:], in1=t2[:], op=ALU.add),
                              reads=[Tt1, Tt2], writes=[TkrT])
                    qn_ = ["q0", "q1", "q2"]
                    kn_ = ["kv0", "kv1"]
                    MAT = [inproj("q%d" % m, m * 128, (m + 1) * 128, 128) for m in range(3)]
                    MAT += [lat_stats(qn_, 3), lat_final(qn_, gq, qnT, TqnT)]
                    MAT += [inproj("kv%d" % m, 384 + m * 128, 384 + (m + 1) * 128, 128) for m in range(2)]
                    MAT += [lat_stats(kn_, 2), lat_final(kn_, gkv, kvnT, TkvnT),
                            inproj("kr", 640, 672, 32), inproj("ks", 672, 704, 32), kr_rope]

                    FE = []
                    if i + 1 < NT:
                        hTs[i + 1] = hTr.get()
                        FE = fe_tile_thunks(P, fe, xbr, i + 1, scA, shA, TscA, hTs[i + 1][0], hTs[i + 1][1])
                    ADA = [ada_piece(2 * i), ada_piece(2 * i + 1)]
                    merge([st for st in (FE, ROPE, MAT, ADA) if st])

                a1_front(0)
                for i in range(NT):
                    a1_tile(i)
                P.dve(lambda e: e.tensor_scalar(out=tmp8b[:], in0=modc[:, 32:40], scalar1=1.0, scalar2=None, op0=ALU.add),
                      reads=[Tmodc], writes=[Ttmp8b])
                P.dve(lambda e: e.tensor_tensor(out=scF[:], in0=tmp8b[:], in1=gffn[:], op=ALU.mult),
                      reads=[Ttmp8b], writes=[TscF])
                dump(P, "modc", modc[:], Tmodc)
                dump(P, "qnT0", qnT[:, 0, 0:512], TqnT)
                dump(P, "kvnT1", kvnT[:, 1, 512:1024], TkvnT)
                dump(P, "krT", krT[:, 0:1024], TkrT)
                finals = P.emit(nc, semstack, finals)

            with ExitStack() as ph:
                P = Prog()
                QTs = [sbt(ph, "QT%d" % i, [128, S], BF16) for i in range(2)]
                KTs = [sbt(ph, "KT%d" % i, [128, S], BF16) for i in range(2)]
                Vas = [sbt(ph, "Va%d" % i, [128, 32, 128], BF16) for i in range(2)]
                sqqr = Ring([sbt(ph, "sqq%d" % i, [128, 512], BF16) for i in range(2)])
                sqkr = Ring([sbt(ph, "sqk%d" % i, [128, 512], BF16) for i in range(2)])
                mxs = [sbt(ph, "mx%d" % i, [33, 32]) for i in range(2)]
                ptr = Ring([sbt(ph, "pt%d" % i, [128, 512], BF16) for i in range(6)])
                osqr = Ring([sbt(ph, "osq%d" % i, [128, 512], BF16) for i in range(2)])
                lsqr = Ring([sbt(ph, "lsq%d" % i, [128, 512], BF16) for i in range(2)])
                rsr = Ring([sbt(ph, "rs%d" % i, [128, 512]) for i in range(2)])
                lrwr = Ring([sbt(ph, "lrw%d" % i, [128, 512]) for i in range(2)])
                a1, Ta1 = sbt(ph, "a1", [32, 512])
                a2, Ta2 = sbt(ph, "a2", [32, 512])
                pp = Ring([pst(ph, "pp%d" % i, [128, 512]) for i in range(3)])
                pps = Ring([pst(ph, "pps%d" % i, [128, 512]) for i in range(3)])
                po = Ring([pst(ph, "po%d" % i, [128, 512]) for i in range(2)])

                for b in range(2):
                    QT, TQT = QTs[b]
                    KT, TKT = KTs[b]
                    Va, TVa = Vas[b]
                    P.pool(lambda e, QT=QT: e.memset(QT[:], 0.0), writes=[TQT])
                    P.pool(lambda e, KT=KT: e.memset(KT[:], 0.0), writes=[TKT])
                    P.pool(lambda e, KT=KT: e.memset(KT[32:33, :], 1.0), writes=[TKT])
                    P.pool(lambda e, Va=Va: e.memset(Va[:], 0.0), writes=[TVa])
                    lc = 64 if b == 0 else 0
                    P.pool(lambda e, Va=Va, lc=lc: e.memset(Va[:, :, lc:lc + 1], 1.0), writes=[TVa])

                def head_ctx(h):
                    b = h % 2
                    return dict(h=h, b=b, vb=64 * b, lrow=(64 if b == 0 else 0), M=(65 if b == 0 else 128),
                                QT=QTs[b][0], TQT=QTs[b][1], KT=KTs[b][0], TKT=KTs[b][1], Va=Vas[b][0], TVa=Vas[b][1], sq={})

                def prep_start(hc):
                    pass

                def prep_a(hc, i):
                    h, vb = hc["h"], hc["vb"]
                    QT, TQT, KT, TKT, Va, TVa = hc["QT"], hc["TQT"], hc["KT"], hc["TKT"], hc["Va"], hc["TVa"]
                    cols = slice(i * 512, (i + 1) * 512)
                    X, TX = pp.get()
                    for k in range(3):
                        P.pe(lambda e, k=k: e.matmul(X[:], lhsT=wq[:, k, h * 256:h * 256 + 128], rhs=qnT[:, k, cols],
                                                     start=(k == 0), stop=(k == 2)), reads=[Twq, TqnT], writes=[TX])
                    Y, TY = pp.get()
                    for k in range(2):
                        P.pe(lambda e, k=k: e.matmul(Y[:], lhsT=wkv[:, k, h * 192:h * 192 + 128], rhs=kvnT[:, k, cols],
                                                     start=(k == 0), stop=False), reads=[Twkv, TkvnT], writes=[TY])
                    for k in range(3):
                        P.pe(lambda e, k=k: e.matmul(Y[:], lhsT=wq[:, k, h * 256 + 128:h * 256 + 256], rhs=qnT[:, k, cols],
                                                     start=False, stop=(k == 2)), reads=[Twq, TqnT], writes=[TY])
                    Z, TZ = pp.get()
                    for blk in range(4):
                        for k in range(2):
                            P.pe(lambda e, k=k, blk=blk: e.matmul(
                                Z[:, blk * 64:(blk + 1) * 64], lhsT=kvnT[:, k, i * 512 + blk * 128:i * 512 + (blk + 1) * 128],
                                rhs=wkv[:, k, h * 192 + 128:h * 192 + 192], start=(k == 0), stop=(k == 1)),
                                reads=[Twkv, TkvnT], writes=[TZ])
                    P.dve(lambda e: e.tensor_scalar(out=QT[64:128, cols], in0=X[64:128, :], scalar1=SCALE, scalar2=None, op0=ALU.mult),
                          reads=[TX], writes=[TQT])
                    P.dve(lambda e: e.scalar_tensor_tensor(out=a1[:], in0=X[0:32, :], scalar=SCALE, in1=cosT[:, cols],
                                                           op0=ALU.mult, op1=ALU.mult), reads=[TX, TcosT], writes=[Ta1])
                    P.dve(lambda e: e.scalar_tensor_tensor(out=a2[:], in0=Y[0:32, :], scalar=SCALE, in1=sinT[:, cols],
                                                           op0=ALU.mult, op1=ALU.mult), reads=[TY, TsinT], writes=[Ta2])
                    P.dve(lambda e: e.tensor_tensor(out=QT[0:32, cols], in0=a1[:], in1=a2[:], op=ALU.add),
                          reads=[Ta1, Ta2], writes=[TQT])
                    P.dve(lambda e: e.tensor_copy(out=KT[64:128, cols], in_=Y[64:128, :]), reads=[TY], writes=[TKT])
                    P.dve(lambda e: e.tensor_copy(out=KT[0:32, cols], in_=krT[:, cols]), reads=[TkrT], writes=[TKT])
                    P.dve(lambda e: e.tensor_copy(out=Va[:, 4 * i:4 * i + 4, vb:vb + 64], in_=Z[:, 0:256].rearrange("p (b v) -> p b v", v=64)),
                          reads=[TZ], writes=[TVa])
                    sqq, Tsqq = sqqr.get()
                    sqk, Tsqk = sqkr.get()
                    P.dve(lambda e: e.tensor_tensor(out=sqq[0:32, :], in0=QT[0:32, cols], in1=QT[0:32, cols], op=ALU.mult), reads=[TQT], writes=[Tsqq])
                    P.dve(lambda e: e.tensor_tensor(out=sqq[64:128, :], in0=QT[64:128, cols], in1=QT[64:128, cols], op=ALU.mult), reads=[TQT], writes=[Tsqq])
                    P.dve(lambda e: e.tensor_tensor(out=sqk[0:32, :], in0=KT[0:32, cols], in1=KT[0:32, cols], op=ALU.mult), reads=[TKT], writes=[Tsqk])
                    P.dve(lambda e: e.tensor_tensor(out=sqk[64:128, :], in0=KT[64:128, cols], in1=KT[64:128, cols], op=ALU.mult), reads=[TKT], writes=[Tsqk])
                    hc["sq"][i] = (sqq, Tsqq, sqk, Tsqk)

                def prep_b(hc, i):
                    sqq, Tsqq, sqk, Tsqk = hc["sq"].pop(i)
                    mx, Tmx = mxs[hc["b"]]
                    for (sq_, Tsq_, col) in ((sqq, Tsqq, i), (sqk, Tsqk, 16 + i)):
                        pss, Tpss = pp.get()
                        P.pe(lambda e, pss=pss, sq_=sq_: e.matmul(pss[0:33, :], lhsT=onesb[0:32, 0:33], rhs=sq_[0:32, :], start=True, stop=False),
                             reads=[Tsq_, Tonesb], writes=[Tpss])
                        P.pe(lambda e, pss=pss, sq_=sq_: e.matmul(pss[0:33, :], lhsT=onesb[64:128, 0:33], rhs=sq_[64:128, :], start=False, stop=True),
                             reads=[Tsq_, Tonesb], writes=[Tpss])
                        P.dve(lambda e, pss=pss, col=col: e.reduce_max(out=mx[:, col:col + 1], in_=pss[0:33, :], axis=AX.X), reads=[Tpss], writes=[Tmx])

                def prep_finish(hc):
                    QT, TQT, KT, TKT = hc["QT"], hc["TQT"], hc["KT"], hc["TKT"]
                    mx, Tmx = mxs[hc["b"]]
                    prep_b(hc, NT - 1)
                    P.dve(lambda e: e.reduce_max(out=mx[:, 8:9], in_=mx[:, 0:8], axis=AX.X), reads=[Tmx], writes=[Tmx])
                    P.dve(lambda e: e.reduce_max(out=mx[:, 24:25], in_=mx[:, 16:24], axis=AX.X), reads=[Tmx], writes=[Tmx])
                    P.dve(lambda e: e.tensor_tensor(out=mx[:, 25:26], in0=mx[:, 8:9], in1=mx[:, 24:25], op=ALU.mult), reads=[Tmx], writes=[Tmx])
                    P.act(lambda e: e.activation(out=mx[:, 26:27], in_=mx[:, 25:26], func=AF.Ln), reads=[Tmx], writes=[Tmx])
                    P.act(lambda e: e.activation(out=mx[:, 27:28], in_=mx[:, 26:27], func=AF.Exp, scale=0.5), reads=[Tmx], writes=[Tmx])
                    P.dve(lambda e: e.tensor_scalar(out=mx[:, 28:29], in0=mx[:, 27:28], scalar1=-1.05, scalar2=None, op0=ALU.mult),
                          reads=[Tmx], writes=[Tmx])
                    P.dve(lambda e: e.tensor_scalar(out=QT[32:33, :], in0=KT[32:33, :], scalar1=mx[32:33, 28:29], scalar2=None, op0=ALU.mult),
                          reads=[Tmx, TKT], writes=[TQT])

                def attention(hc, hook, LA=2, EPI_DELAY=2, EPI_DELAY2=5):
                    h, vb, lrow, M = hc["h"], hc["vb"], hc["lrow"], hc["M"]
                    QT, TQT, KT, TKT, Va, TVa = hc["QT"], hc["TQT"], hc["KT"], hc["TKT"], hc["Va"], hc["TVa"]
                    steps = [(i, kb) for i in range(NT) for kb in range(4 * i + 4)]
                    ctx = {}
                    Ob = {}
                    pending = []

                    def score(i, kb):
                        c0 = max(0, kb - 4 * i) * 128
                        n = 512 - c0
                        ps_, Tps = pps.get()
                        P.pe(lambda e: e.matmul(ps_[:, 0:n], lhsT=KT[:, kb * 128:(kb + 1) * 128], rhs=QT[:, i * 512 + c0:(i + 1) * 512], start=True, stop=True),
                             reads=[TKT, TQT], writes=[Tps])
                        ctx[(i, kb)] = (ps_, Tps, c0, n)

                    def rest(idx, i, kb):
                        ps_, Tps, c0, n = ctx.pop((i, kb))
                        nkb = 4 * i + 4
                        if kb == 0:
                            Ob[i] = po.get()
                        O, TO = Ob[i]
                        pt, Tpt = ptr.get()
                        P.act(lambda e: e.activation(out=pt[:, 0:n], in_=ps_[:, 0:n], func=AF.Exp), reads=[Tps], writes=[Tpt])
                        if kb >= 4 * i:
                            P.pool(lambda e: e.tensor_tensor(out=pt[:, 0:128], in0=pt[:, 0:128], in1=trib[:], op=ALU.mult),
                                   reads=[Tpt, Ttrib], writes=[Tpt])
                        P.pe(lambda e: e.matmul(O[0:M, c0:512], lhsT=Va[:, kb, 0:M], rhs=pt[:, 0:n], start=(kb == 0), stop=(kb == nkb - 1)),
                             reads=[TVa, Tpt], writes=[TO])
                        if kb == nkb - 1:
                            st = epi_a(i)
                            pending.append((idx + EPI_DELAY, lambda: epi_b(i, st)))
                        if hook is not None:
                            hook(i, kb, nkb)

                    def epi_a(i):
                        O, TO = Ob.pop(i)
                        osq, Tosq = osqr.get()
                        lsq, Tlsq = lsqr.get()
                        P.act(lambda e: e.activation(out=osq[vb:vb + 64, :], in_=O[vb:vb + 64, :], func=AF.Square), reads=[TO], writes=[Tosq])
                        lrw, Tlrw = lrwr.get()
                        P.dve(lambda e: e.tensor_copy(out=lrw[lrow:lrow + 1, :], in_=O[lrow:lrow + 1, :]), reads=[TO], writes=[Tlrw])
                        P.dve(lambda e: e.scalar_tensor_tensor(out=lsq[lrow:lrow + 1, :], in0=lrw[lrow:lrow + 1, :], scalar=64 * EPS, in1=lrw[lrow:lrow + 1, :],
                                                               op0=ALU.mult, op1=ALU.mult), reads=[Tlrw], writes=[Tlsq])
                        return (O, TO, osq, Tosq, lsq, Tlsq)

                    def epi_b(i, st):
                        O, TO, osq, Tosq, lsq, Tlsq = st
                        pn_, Tpn = pp.get()
                        P.pe(lambda e: e.matmul(pn_[:], lhsT=onesb[vb:vb + 64, :], rhs=osq[vb:vb + 64, :], start=True, stop=False),
                             reads=[Tosq, Tonesb], writes=[Tpn])
                        P.pe(lambda e: e.matmul(pn_[:], lhsT=onesb[lrow:lrow + 1, :], rhs=lsq[lrow:lrow + 1, :], start=False, stop=True),
                             reads=[Tlsq, Tonesb], writes=[Tpn])
                        pending.append((pending_idx[0] + EPI_DELAY2, lambda: epi_c(i, st, pn_, Tpn)))
                        pending.sort(key=lambda t: t[0])

                    def epi_c(i, st, pn_, Tpn):
                        O, TO, osq, Tosq, lsq, Tlsq = st
                        cols = slice(i * 512, (i + 1) * 512)
                        rs, Trs = rsr.get()
                        P.act(lambda e: e.activation(out=rs[vb:vb + 64, :], in_=pn_[vb:vb + 64, :], func=AF.Ln, scale=1.0 / 64), reads=[Tpn], writes=[Trs])
                        P.act(lambda e: e.activation(out=rs[vb:vb + 64, :], in_=rs[vb:vb + 64, :], func=AF.Exp, scale=-0.5), reads=[Trs], writes=[Trs])
                        P.dve(lambda e: e.scalar_tensor_tensor(
                            out=yTa[vb:vb + 64, h // 2, cols], in0=O[vb:vb + 64, :], scalar=gomla[vb:vb + 64, h // 2:h // 2 + 1], in1=rs[vb:vb + 64, :],
                            op0=ALU.mult, op1=ALU.mult), reads=[TO, Trs, Tpar], writes=[TyTa])

                    pending_idx = [0]
                    for idx in range(len(steps) + LA):
                        pending_idx[0] = idx
                        if idx < len(steps):
                            score(*steps[idx])
                        if idx >= LA:
                            rest(idx, *steps[idx - LA])
                        while pending and pending[0][0] <= idx:
                            pending.pop(0)[1]()
                    while pending:
                        pending_idx[0] += 1
                        pending.pop(0)[1]()

                hcs = [head_ctx(h) for h in range(8)]
                prep_start(hcs[0])
                for i in range(NT):
                    prep_a(hcs[0], i)
                    if i > 0:
                        prep_b(hcs[0], i - 1)
                prep_finish(hcs[0])
                for h in range(8):
                    if h + 1 < 8:
                        nxt = hcs[h + 1]
                        prep_start(nxt)
                        prep_a(nxt, 0)

                        def hook(i, kb, nkb, nxt=nxt):
                            if kb == nkb - 1 and i < NT - 1:
                                prep_b(nxt, i)
                                prep_a(nxt, i + 1)
                            if i == NT - 1 and kb == 6:
                                prep_finish(nxt)
                        attention(hcs[h], hook)
                    else:
                        attention(hcs[h], None)
                    if h == 0:
                        dump(P, "QT0", hcs[0]["QT"][:, 0:512], hcs[0]["TQT"])
                        dump(P, "KT0", hcs[0]["KT"][:, 0:512], hcs[0]["TKT"])
                dump(P, "yTa0", yTa[:, 0, 0:1024], TyTa)
                dump(P, "yTa3", yTa[:, 3, 3072:4096], TyTa)
                finals = P.emit(nc, semstack, finals)

        yTb, TyTb = sbt(top, "yTb", [128, 4, S], BF16)
        if "A3" in phases:
            with ExitStack() as ph:
                P = Prog()
                winm, Twinm = sbt(ph, "winm", [128, 8, 1544], BF16)
                xbr = Ring([sbt(ph, "m_xb%d" % i, [128, D]) for i in range(4)])
                hTs = [(sbt(ph, "m_hT%d" % b_, [128, 8, 512], BF16)[0], [Tl("m_hT%d_%d" % (b_, c)) for c in range(8)]) for b_ in range(2)]
                fe = make_fe(ph, nxn=2, npt=1)
                fe["evac"] = "dve"
                zqk = sbt(ph, "zqk", [64, 8, 516], BF16)[0]
                dgw, Tdgw = sbt(ph, "dgw", [64, 32, 64], BF16)
                ser = Ring([sbt(ph, "se%d" % i, [64, 512]) for i in range(2)])
                nconvb, Tnconvb = sbt(ph, "nconvb", [64, 8])
                Tz = [Tl("z%d" % g) for g in range(8)]
                qT, TqT = sbt(ph, "m_qT", [64, 4, 512], BF16)
                kTf, TkTf = sbt(ph, "kTf", [64, 4, 512])
                kpT, TkpT = sbt(ph, "kpT", [64, 4, 512], BF16)
                kptok, Tkptok = sbt(ph, "kptok", [128, 16, 64], BF16)
                vaug, Tvaug = sbt(ph, "vaug", [128, 4, 4, 129], BF16)
                sigo, Tsigo = sbt(ph, "sigo", [128, 4, 512])
                g_ig, Tg_ig = sbt(ph, "g_ig", [4, 512])
                g_x, Tg_x = sbt(ph, "g_x", [4, 512])
                g_a, Tg_a = sbt(ph, "g_a", [4, 512])
                g_b, Tg_b = sbt(ph, "g_b", [4, 512])
                g_lf, Tg_lf = sbt(ph, "g_lf", [4, 512])
                g_bc, Tg_bc = sbt(ph, "g_bc", [4, 512])
                g_u, Tg_u = sbt(ph, "g_u", [4, 512])
                g_w, Tg_w = sbt(ph, "g_w", [4, 512])
                g_cl, Tg_cl = sbt(ph, "g_cl", [4, 512])
                msk, Tmsk = sbt(ph, "msk", [4, 512])
                sm, Tsm = sbt(ph, "sm", [4, 64])
                clampc, Tclampc = sbt(ph, "clampc", [128, 16])
                abc, Tabc = sbt(ph, "abc", [64, 16])
                Sst, TS = sbt(ph, "Sst", [64, 4, 129])
                Sh, TSh = sbt(ph, "Sh", [64, 4, 129])
                Shb, TShb = sbt(ph, "Shb", [64, 4, 129], BF16)
                sTm, TsTm = sbt(ph, "sTm", [128, 4, 128], BF16)
                ybr = Ring([sbt(ph, "yb%d" % i, [128, 128]) for i in range(4)])
                ybnr = Ring([sbt(ph, "ybn%d" % i, [128, 128], BF16) for i in range(4)])
                junk2, Tjunk2 = sbt(ph, "junk2", [128, 128], BF16)
                dtmp, Tdtmp = sbt(ph, "dtmp", [128, 8])
                ss2, Tss2 = sbt(ph, "ss2", [128, 8])
                ptk, Tptk = pst(ph, "ptk", [128, 1024], BF16)
                pty, Tpty = pst(ph, "pty", [128, 1024], BF16)
                pr = Ring([pst(ph, "pr%d" % i, [128, 512]) for i in range(5)])

                P.dma("gpsimd", winm[:], winm_d.rearrange("(k p) n -> p k n", p=128), writes=[Twinm])
                P.pool(lambda e: e.memset(zqk[:], 0.0), writes=Tz)
                P.dve(lambda e: e.tensor_scalar(out=nconvb[:], in0=convb[:], scalar1=-1.0, scalar2=None, op0=ALU.mult), writes=[Tnconvb])
                for gj in range(32):
                    P.dve(lambda e, gj=gj: e.tensor_scalar(out=dgw[:, gj, :], in0=identb[0:64, 0:64], scalar1=convw[:, gj:gj + 1], scalar2=None, op0=ALU.mult),
                          writes=[Tdgw])
                P.pool(lambda e: e.memset(Sst[:], 0.0), writes=[TS])
                P.pool(lambda e: e.memset(sm[:], 0.0), writes=[Tsm])
                P.pool(lambda e: e.memset(vaug[:], 1.0), writes=[Tvaug])
                P.pool(lambda e: e.memset(msk[:], 1.0), writes=[Tmsk])
                P.pool(lambda e: e.memset(msk[:].rearrange("p (c t) -> p c t", t=128)[:, :, 0:1], 0.0), writes=[Tmsk])

                def v3(ap):
                    return ap.rearrange("p (c t) -> p c t", t=128)

                def a3_front(i):
                    fe_tile(P, fe, xbr, i, scA, shA, TscA, hTs[i % 2][0], hTs[i % 2][1])

                def a3_tile(i):
                    hT, ThT = hTs[i % 2]

                    def proj(c0, c1, M):
                        pt, Tpt = pr.get()
                        for k in range(8):
                            P.pe(lambda e, k=k: e.matmul(pt[0:M, :], lhsT=winm[:, k, c0:c1], rhs=hT[:, k, :], start=(k == 0), stop=(k == 7)),
                                 reads=[Twinm, ThT[k]], writes=[Tpt])
                        return pt, Tpt

                    def qk_proj(g):
                        def f():
                            pt, Tpt = proj(g * 64, (g + 1) * 64, 64)
                            P.dve(lambda e: e.tensor_copy(out=zqk[:, g, 4:516], in_=pt[0:64, :]), reads=[Tpt], writes=[Tz[g]])
                        return f

                    def qk_conv(g):
                        def f():
                            pc_, Tpc_ = pr.get()
                            for j in range(4):
                                P.pe(lambda e, j=j: e.matmul(pc_[0:64, :], lhsT=dgw[:, g * 4 + j, :], rhs=zqk[:, g, 1 + j:1 + j + 512], start=(j == 0), stop=(j == 3)),
                                     reads=[Tz[g], Tdgw], writes=[Tpc_])
                            se, Tse = ser.get()
                            P.act(lambda e: e.activation(out=se[:], in_=pc_[0:64, :], func=AF.Exp, bias=nconvb[:, g:g + 1], scale=-1.0), reads=[Tpc_], writes=[Tse])
                            P.act(lambda e: e.activation(out=se[:], in_=se[:], func=AF.Ln, bias=1.0), reads=[Tse], writes=[Tse])
                            P.act(lambda e: e.activation(out=se[:], in_=se[:], func=AF.Exp, scale=-1.0), reads=[Tse], writes=[Tse])
                            if g < 4:
                                P.dve(lambda e: e.scalar_tensor_tensor(out=qT[:, g, :], in0=pc_[0:64, :], scalar=convb[:, g:g + 1], in1=se[:], op0=ALU.add, op1=ALU.mult),
                                      reads=[Tpc_, Tse], writes=[TqT])
                            else:
                                P.dve(lambda e: e.scalar_tensor_tensor(out=kTf[:, g - 4, :], in0=pc_[0:64, :], scalar=convb[:, g:g + 1], in1=se[:], op0=ALU.add, op1=ALU.mult),
                                      reads=[Tpc_, Tse], writes=[TkTf])
                            P.dve(lambda e: e.tensor_copy(out=zqk[:, g, 1:4], in_=zqk[:, g, 513:516]), reads=[Tz[g]], writes=[Tz[g]])
                        return f
                    QK = [qk_proj(0)]
                    for g in range(8):
                        if g + 1 < 8:
                            QK.append(qk_proj(g + 1))
                        QK.append(qk_conv(g))

                    Rb = sm[:, 16:20].unsqueeze(2).to_broadcast([4, 4, 128])
                    gst = {}

                    def g_proj():
                        gst["pi"] = proj(1536, 1540, 4)
                        gst["pf"] = proj(1540, 1544, 4)
                        pi, Tpi = gst["pi"]
                        pf, Tpf = gst["pf"]
                        P.dve(lambda e: e.tensor_scalar(out=g_ig[:], in0=pi[0:4, :], scalar1=bg[:, 0:1], scalar2=None, op0=ALU.add), reads=[Tpi], writes=[Tg_ig])
                        P.dve(lambda e: e.tensor_scalar(out=g_x[:], in0=pf[0:4, :], scalar1=bg[:, 1:2], scalar2=-1.0, op0=ALU.add, op1=ALU.mult), reads=[Tpf], writes=[Tg_x])
                    G = [
                        g_proj,
                        lambda: P.dve(lambda e: e.scalar_tensor_tensor(out=g_a[:], in0=g_x[:], scalar=-1.0, in1=g_x[:], op0=ALU.mult, op1=ALU.max), reads=[Tg_x], writes=[Tg_a]),
                        lambda: P.act(lambda e: e.activation(out=g_b[:], in_=g_a[:], func=AF.Exp, scale=-1.0), reads=[Tg_a], writes=[Tg_b]),
                        lambda: P.act(lambda e: e.activation(out=g_b[:], in_=g_b[:], func=AF.Ln, bias=1.0), reads=[Tg_b], writes=[Tg_b]),
                        lambda: P.dve(lambda e: e.tensor_scalar(out=g_a[:], in0=g_x[:], scalar1=0.0, scalar2=None, op0=ALU.max), reads=[Tg_x, Tg_b], writes=[Tg_a]),
                        lambda: P.dve(lambda e: e.scalar_tensor_tensor(out=g_lf[:], in0=g_b[:], scalar=-1.0, in1=g_a[:], op0=ALU.mult, op1=ALU.subtract),
                                      reads=[Tg_b, Tg_a], writes=[Tg_lf]),
                        lambda: P.dve(lambda e: e.tensor_tensor_scan(out=g_bc[:], data0=msk[:], data1=g_lf[:], initial=0.0, op0=ALU.mult, op1=ALU.add),
                                      reads=[Tmsk, Tg_lf], writes=[Tg_bc]),
                        lambda: P.dve(lambda e: e.tensor_tensor(out=g_u[:], in0=g_ig[:], in1=g_bc[:], op=ALU.subtract), reads=[Tg_ig, Tg_bc], writes=[Tg_u]),
                        lambda: P.dve(lambda e: e.tensor_reduce(out=sm[:, 0:4], in_=v3(g_u[:]), axis=AX.X, op=ALU.max), reads=[Tg_u], writes=[Tsm]),
                        lambda: P.dve(lambda e: e.tensor_copy(out=sm[:, 4:8].unsqueeze(2), in_=v3(g_bc[:])[:, :, 127:128]), reads=[Tg_bc, Tsm], writes=[Tsm]),
                        lambda: P.dve(lambda e: e.tensor_tensor_scan(out=sm[:, 8:12], data0=sm[:, 0:4], data1=sm[:, 4:8], initial=sm[:, 48:49], op0=ALU.max, op1=ALU.add),
                                      reads=[Tsm], writes=[Tsm]),
                        lambda: P.dve(lambda e: e.tensor_copy(out=sm[:, 12:13], in_=sm[:, 48:49]), reads=[Tsm], writes=[Tsm]),
                        lambda: P.dve(lambda e: e.tensor_copy(out=sm[:, 13:16], in_=sm[:, 8:11]), reads=[Tsm], writes=[Tsm]),
                        lambda: P.dve(lambda e: e.tensor_copy(out=sm[:, 48:49], in_=sm[:, 11:12]), reads=[Tsm], writes=[Tsm]),
                        lambda: P.dve(lambda e: e.tensor_tensor(out=sm[:, 16:20], in0=sm[:, 12:16], in1=sm[:, 0:4], op=ALU.max), reads=[Tsm], writes=[Tsm]),
                        lambda: P.dve(lambda e: e.tensor_tensor(out=sm[:, 20:24], in0=sm[:, 12:16], in1=sm[:, 16:20], op=ALU.subtract), reads=[Tsm], writes=[Tsm]),
                        lambda: P.act(lambda e: e.activation(out=sm[:, 24:28], in_=sm[:, 20:24], func=AF.Exp), reads=[Tsm], writes=[Tsm]),
                        lambda: P.dve(lambda e: e.tensor_tensor(out=v3(g_w[:]), in0=v3(g_u[:]), in1=Rb, op=ALU.subtract), reads=[Tg_u, Tsm], writes=[Tg_w]),
                        lambda: P.act(lambda e: e.activation(out=g_w[:], in_=g_w[:], func=AF.Exp), reads=[Tg_w], writes=[Tg_w]),
                        lambda: P.dve(lambda e: e.tensor_tensor(out=v3(g_cl[:]), in0=v3(g_bc[:]), in1=Rb, op=ALU.add), reads=[Tg_bc, Tsm], writes=[Tg_cl]),
                        lambda: P.act(lambda e: e.activation(out=g_cl[:], in_=g_cl[:], func=AF.Exp, scale=-1.0), reads=[Tg_cl], writes=[Tg_cl]),
                    ]

                    def v_proj(c):
                        def f():
                            pv, Tpv = pr.get()
                            for k in range(8):
                                P.pe(lambda e, k=k: e.matmul(pv[:], lhsT=hT[:, k, c * 128:(c + 1) * 128], rhs=winm[:, k, 512:1024], start=(k == 0), stop=(k == 7)),
                                     reads=[Twinm, ThT[k]], writes=[Tpv])
                            P.dve(lambda e: e.tensor_copy(out=vaug[:, c, :, 0:128], in_=pv[:].rearrange("p (h v) -> p h v", v=128)), reads=[Tpv], writes=[Tvaug])
                        return f

                    def o_proj(c):
                        def f():
                            po_, Tpo = pr.get()
                            for k in range(8):
                                P.pe(lambda e, k=k: e.matmul(po_[:], lhsT=hT[:, k, c * 128:(c + 1) * 128], rhs=winm[:, k, 1024:1536], start=(k == 0), stop=(k == 7)),
                                     reads=[Twinm, ThT[k]], writes=[Tpo])
                            P.act(lambda e: e.activation(out=sigo[:, c, :], in_=po_[:], func=AF.Exp, scale=-1.0), reads=[Tpo], writes=[Tsigo])
                            P.act(lambda e: e.activation(out=sigo[:, c, :], in_=sigo[:, c, :], func=AF.Ln, bias=1.0), reads=[Tsigo], writes=[Tsigo])
                            P.act(lambda e: e.activation(out=sigo[:, c, :], in_=sigo[:, c, :], func=AF.Exp, scale=-1.0), reads=[Tsigo], writes=[Tsigo])
                        return f
                    VO = []
                    for c in range(4):
                        VO.append(v_proj(c))
                        VO.append(o_proj(c))

                    FE = []
                    if i + 1 < NT:
                        FE = fe_tile_thunks(P, fe, xbr, i + 1, scA, shA, TscA, hTs[(i + 1) % 2][0], hTs[(i + 1) % 2][1])
                    merge([st_ for st_ in (QK, G, VO, FE) if st_])

                    def kscale(h):
                        pw, Tpw = pr.get()
                        P.pe(lambda e: e.matmul(pw[0:64, :], lhsT=sel4[0:4, h * 64:(h + 1) * 64], rhs=g_w[0:4, :], start=True, stop=True),
                             reads=[Tg_w], writes=[Tpw])
                        P.dve(lambda e: e.scalar_tensor_tensor(out=kpT[:, h, :], in0=kTf[:, h, :], scalar=0.125, in1=pw[0:64, :], op0=ALU.mult, op1=ALU.mult),
                              reads=[TkTf, Tpw], writes=[TkpT])
                    for h in range(4):
                        kscale(h)
                    pc, Tpc = pr.get()
                    for c in range(4):
                        P.pe(lambda e, c=c: e.matmul(pc[:, c * 4:(c + 1) * 4], lhsT=g_cl[0:4, c * 128:(c + 1) * 128], rhs=identf[0:4, 0:4], start=True, stop=True),
                             reads=[Tg_cl], writes=[Tpc])
                    P.act(lambda e: e.copy(out=clampc[:], in_=pc[:, 0:16]), reads=[Tpc], writes=[Tclampc])
                    P.dve(lambda e: e.tensor_tensor(out=sm[:, 32:48].rearrange("p (c h) -> p c h", h=4), in0=sm[:, 24:28].unsqueeze(2).to_broadcast([4, 4, 4]),
                                                    in1=identf[0:4, 0:4].unsqueeze(1).to_broadcast([4, 4, 4]), op=ALU.mult), reads=[Tsm], writes=[Tsm])
                    pa_, Tpa = pr.get()
                    P.pe(lambda e: e.matmul(pa_[0:64, 0:16], lhsT=onesf[0:4, 0:64], rhs=sm[0:4, 32:48], start=True, stop=True), reads=[Tsm], writes=[Tpa])
                    P.act(lambda e: e.copy(out=abc[:], in_=pa_[0:64, 0:16]), reads=[Tpa], writes=[Tabc])
                    for c in range(4):
                        for h in range(4):
                            P.pe(lambda e, c=c, h=h: e.transpose(ptk[:, (c * 4 + h) * 64:(c * 4 + h + 1) * 64], kpT[0:64, h, c * 128:(c + 1) * 128], identb[0:64, 0:64]),
                                 reads=[TkpT], writes=[Tptk])
                    P.act(lambda e: e.copy(out=kptok[:].rearrange("p a b -> p (a b)"), in_=ptk[:]), reads=[Tptk], writes=[Tkptok])
                    cst = {}

                    def X(c):
                        cc = slice(c * 128, (c + 1) * 128)
                        P.dve(lambda e: e.tensor_tensor(out=Sh[:], in0=Sst[:], in1=abc[:, c * 4:(c + 1) * 4].unsqueeze(2).to_broadcast([64, 4, 129]), op=ALU.mult),
                              reads=[TS, Tabc], writes=[TSh])
                        P.act(lambda e: e.copy(out=Shb[:], in_=Sh[:]), reads=[TSh], writes=[TShb])
                        ps_, Tps = pr.get()
                        for h in range(4):
                            P.pe(lambda e, h=h: e.matmul(ps_[:, h * 128:(h + 1) * 128], lhsT=kpT[0:64, h, cc], rhs=qT[0:64, h, cc], start=True, stop=True),
                                 reads=[TkpT, TqT], writes=[Tps])
                        P.dve(lambda e: e.tensor_tensor(out=sTm[:], in0=ps_[:].rearrange("p (h t) -> p h t", t=128),
                                                        in1=trib[:].unsqueeze(1).to_broadcast([128, 4, 128]), op=ALU.mult), reads=[Tps], writes=[TsTm])
                        pns = []
                        for j in range(2):
                            pn, Tpn = pr.get()
                            pd, Tpd = pr.get()
                            for hh in range(2):
                                h = 2 * j + hh
                                P.pe(lambda e, h=h, hh=hh, pn=pn: e.matmul(pn[:, hh * 129:(hh + 1) * 129], lhsT=sTm[:, h, :], rhs=vaug[:, c, h, :], start=True, stop=False),
                                     reads=[TsTm, Tvaug], writes=[Tpn])
                                P.pe(lambda e, h=h, hh=hh, pn=pn: e.matmul(pn[:, hh * 129:(hh + 1) * 129], lhsT=qT[0:64, h, cc], rhs=Shb[0:64, h, :], start=False, stop=True),
                                     reads=[TqT, TShb], writes=[Tpn])
                            for hh in range(2):
                                h = 2 * j + hh
                                P.pe(lambda e, h=h, hh=hh, pd=pd: e.matmul(pd[0:64, hh * 129:(hh + 1) * 129], lhsT=kptok[:, c * 4 + h, :], rhs=vaug[:, c, h, :], start=True, stop=True),
                                     reads=[Tkptok, Tvaug], writes=[Tpd])
                            P.dve(lambda e, j=j, pd=pd: e.tensor_tensor(out=Sst[:, 2 * j:2 * j + 2, :], in0=Sh[:, 2 * j:2 * j + 2, :],
                                                                        in1=pd[0:64, 0:258].rearrange("p (h v) -> p h v", v=129), op=ALU.add), reads=[TSh, Tpd], writes=[TS])
                            pns.append((pn, Tpn))
                        cst[c] = {"pns": pns, "ybs": []}

                    def D1(c):
                        for j in range(2):
                            pn, Tpn = cst[c]["pns"][j]
                            pnv = pn[:, 0:258].rearrange("p (h v) -> p h v", v=129)
                            P.dve(lambda e, pnv=pnv: e.tensor_copy(out=dtmp[:, 6:8].unsqueeze(2), in_=pnv[:, :, 128:129]), reads=[Tpn], writes=[Tdtmp])
                            P.dve(lambda e: e.scalar_tensor_tensor(out=dtmp[:, 0:2], in0=dtmp[:, 6:8], scalar=-1.0, in1=dtmp[:, 6:8],
                                                                   op0=ALU.mult, op1=ALU.max), reads=[Tdtmp], writes=[Tdtmp])
                            P.dve(lambda e, j=j: e.tensor_tensor(out=dtmp[:, 2:4], in0=dtmp[:, 0:2], in1=clampc[:, c * 4 + 2 * j:c * 4 + 2 * j + 2], op=ALU.max),
                                  reads=[Tdtmp, Tclampc], writes=[Tdtmp])
                            P.dve(lambda e: e.reciprocal(out=dtmp[:, 4:6], in_=dtmp[:, 2:4]), reads=[Tdtmp], writes=[Tdtmp])
                            for hh in range(2):
                                h = 2 * j + hh
                                yb, Tyb = ybr.get()
                                P.dve(lambda e, h=h, hh=hh, yb=yb, pn=pn: e.scalar_tensor_tensor(out=yb[:], in0=pn[:, hh * 129:hh * 129 + 128], scalar=dtmp[:, 4 + hh:5 + hh],
                                                                                                  in1=sigo[:, c, h * 128:(h + 1) * 128], op0=ALU.mult, op1=ALU.mult),
                                      reads=[Tpn, Tdtmp, Tsigo], writes=[Tyb])
                                P.act(lambda e, h=h, yb=yb: e.activation(out=junk2[:], in_=yb[:], func=AF.Square, accum_out=ss2[:, h:h + 1]),
                                      reads=[Tyb], writes=[Tjunk2, Tss2])
                                cst[c]["ybs"].append((yb, Tyb))

                    def D2(c):
                        tok = slice(i * 512 + c * 128, i * 512 + (c + 1) * 128)
                        ybs = cst.pop(c)["ybs"]
                        P.dve(lambda e: e.tensor_scalar(out=ss2[:, 4:8], in0=ss2[:, 0:4], scalar1=1.0 / 128, scalar2=EPS, op0=ALU.mult, op1=ALU.add),
                              reads=[Tss2], writes=[Tss2])
                        P.act(lambda e: e.activation(out=ss2[:, 4:8], in_=ss2[:, 4:8], func=AF.Ln), reads=[Tss2], writes=[Tss2])
                        P.act(lambda e: e.activation(out=ss2[:, 4:8], in_=ss2[:, 4:8], func=AF.Exp, scale=-0.5), reads=[Tss2], writes=[Tss2])
                        for h in range(4):
                            yb, Tyb = ybs[h]
                            ybn, Tybn = ybnr.get()
                            P.dve(lambda e, h=h, yb=yb, ybn=ybn: e.tensor_scalar(out=ybn[:], in0=yb[:], scalar1=ss2[:, 4 + h:5 + h], scalar2=None, op0=ALU.mult),
                                  reads=[Tyb, Tss2], writes=[Tybn])
                            P.pe(lambda e, h=h, ybn=ybn: e.transpose(pty[:, h * 128:(h + 1) * 128], ybn[:], identb[:]), reads=[Tybn], writes=[Tpty])
                            P.act(lambda e, h=h: e.mul(out=yTb[:, h, tok], in_=pty[:, h * 128:(h + 1) * 128], mul=gomls[:, h:h + 1]), reads=[Tpty], writes=[TyTb])

                    X(0)
                    D1(0)
                    for c in range(1, 4):
                        X(c)
                        D2(c - 1)
                        D1(c)
                    D2(3)

                a3_front(0)
                for i in range(NT):
                    a3_tile(i)
                if "yTb" in dbg_outs:
                    P.dma("gpsimd", dbg_outs["yTb"], yTb[:], reads=[TyTb])
                finals = P.emit(nc, semstack, finals)

        if "B" in phases:
            with ExitStack() as ph:
                P = Prog()
                wout, Twout = sbt(ph, "wout", [128, 8, D], BF16)
                gta, Tgta = sbt(ph, "gta", [128, D])
                gtf, Tgtf = sbt(ph, "gtf", [128, D])
                gfin, Tgfin = sbt(ph, "gfinb", [128, D])
                dgr = Ring([sbt(ph, "dg%d" % i, [128, 128]) for i in range(2)])
                x1sets = [[sbt(ph, "x1_%d_%d" % (s_, i), [128, D]) for i in range(4)] for s_ in range(2)]
                hfT = sbt(ph, "hfT", [128, 8, 512], BF16)[0]
                ThfT = [Tl("hfT%d" % c) for c in range(8)]
                aT, TaT = sbt(ph, "aT", [128, NJ, 512], BF16)
                sgr = Ring([sbt(ph, "sg%d" % i, [128, 512]) for i in range(4)])
                wgr = Ring([sbt(ph, "wgp%d" % i, [128, 8, 256], BF16) for i in range(2)])
                wur = Ring([sbt(ph, "wup%d" % i, [128, 8, 256], BF16) for i in range(2)])
                wdr = Ring([sbt(ph, "wdp%d" % i, [128, 2, 512], BF16) for i in range(5)])
                fe = make_fe(ph, nxn=4, npt=2)
                pb = Ring([pst(ph, "pb%d" % i, [128, 512]) for i in range(6)])

                P.dma("gpsimd", wout[:], wout_d.rearrange("(k p) n -> p k n", p=128), writes=[Twout])
                P.dma("sync", gfin[:], gfin_d.to_broadcast([128, D]), writes=[Tgfin])

                def bcast(dst, Tdst, col0):
                    for c in range(8):
                        dg, Tdg = dgr.get()
                        P.dve(lambda e, c=c, dg=dg: e.tensor_scalar(out=dg[:], in0=identf[:], scalar1=modc[:, col0 + c:col0 + c + 1], scalar2=None, op0=ALU.mult),
                              writes=[Tdg])
                        pk_, Tpk_ = pb.get()
                        P.pe(lambda e, dg=dg, pk_=pk_: e.matmul(pk_[:, 0:128], lhsT=onesf[:], rhs=dg[:], start=True, stop=True), reads=[Tdg], writes=[Tpk_])
                        P.act(lambda e, c=c, pk_=pk_: e.copy(out=dst[:, c * 128:(c + 1) * 128], in_=pk_[:, 0:128]), reads=[Tpk_], writes=[Tdst])
                bcast(gta, Tgta, 16)
                bcast(gtf, Tgtf, 40)
                wg_v = wg_d.rearrange("(k p) n -> p k n", p=128)
                wu_v = wu_d.rearrange("(k p) n -> p k n", p=128)
                wd_v = wd_d.rearrange("(j p) n -> p j n", p=128)
                xns = {}

                def pro1(t):
                    xs = x1sets[t % 2]
                    for blk in range(4):
                        r0 = t * 512 + blk * 128
                        P.dma("sync", xs[blk][0][:], x_d[r0:r0 + 128, :], writes=[xs[blk][1]])

                    def outproj(blk, half):
                        hs = slice(half * 512, (half + 1) * 512)
                        xb, Txb = xs[blk]
                        po_, Tpo = pb.get()
                        tok = slice(t * 512 + blk * 128, t * 512 + (blk + 1) * 128)
                        for k in range(8):
                            src, Tsrc = (yTa, TyTa) if k < 4 else (yTb, TyTb)
                            P.pe(lambda e, k=k, src=src: e.matmul(po_[:], lhsT=src[:, k % 4, tok], rhs=wout[:, k, hs], start=(k == 0), stop=(k == 7)),
                                 reads=[Twout, Tsrc], writes=[Tpo])
                        sg, Tsg = sgr.get()
                        P.dve(lambda e: e.tensor_tensor(out=sg[:], in0=po_[:], in1=gta[:, hs], op=ALU.mult), reads=[Tpo, Tgta], writes=[Tsg])
                        P.dve(lambda e: e.tensor_tensor(out=xb[:, hs], in0=sg[:], in1=xb[:, hs], op=ALU.add), reads=[Tsg, Txb], writes=[Txb])
                    for blk in range(4):
                        for half in range(2):
                            outproj(blk, half)
                    xns[t] = [fe_stats(P, fe, xs[blk][0][:], xs[blk][1]) for blk in range(4)]
                    if t == 0 and "x1" in dbg_outs:
                        for blk in range(4):
                            P.dma("sync", dbg_outs["x1"][blk * 128:(blk + 1) * 128, :], xs[blk][0][:], reads=[xs[blk][1]])

                def pro2(t):
                    for blk, (xn, Txn) in enumerate(xns.pop(t)):
                        fe_trans(P, fe, xn, Txn, scF, shF, TscF, hfT, ThfT, blk)

                pieces = []
                for t_ in range(NT):
                    for jp in range(NJ // 2):
                        pieces.append(("g", t_, jp, 0))
                        pieces.append(("u", t_, jp, 0))
                    for half in range(2):
                        for jp in range(NJ // 2):
                            pieces.append(("d", t_, jp, half))
                wslot = {}
                wstate = {"issued": 0}

                def issue_upto(n):
                    while wstate["issued"] < min(n, len(pieces)):
                        kind, t_, jp, half = pieces[wstate["issued"]]
                        if kind == "g":
                            buf, Tb = wgr.get()
                            P.dma("gpsimd", buf[:], wg_v[:, :, jp * 256:(jp + 1) * 256], writes=[Tb])
                        elif kind == "u":
                            buf, Tb = wur.get()
                            P.dma("gpsimd", buf[:], wu_v[:, :, jp * 256:(jp + 1) * 256], writes=[Tb])
                        else:
                            buf, Tb = wdr.get()
                            P.dma("gpsimd", buf[:], wd_v[:, 2 * jp:2 * jp + 2, half * 512:(half + 1) * 512], writes=[Tb])
                        wslot[wstate["issued"]] = (buf, Tb)
                        wstate["issued"] += 1

                def take(kind, t_, jp, half):
                    n = wstate.setdefault("next", 0)
                    assert pieces[n] == (kind, t_, jp, half), (pieces[n], kind, t_, jp, half)
                    issue_upto(n + 1)
                    wstate["next"] = n + 1
                    return wslot.pop(n)

                def advance():
                    issue_upto(wstate.get("next", 0) + 4)

                def up(t, jp):
                    wgp, Twgp = take("g", t, jp, 0)
                    wup, Twup = take("u", t, jp, 0)
                    for jj in range(2):
                        j = 2 * jp + jj
                        pg, Tpg = pb.get()
                        pu, Tpu = pb.get()
                        for k in range(8):
                            P.pe(lambda e, k=k, jj=jj, pg=pg: e.matmul(pg[:], lhsT=wgp[:, k, jj * 128:(jj + 1) * 128], rhs=hfT[:, k, :], start=(k == 0), stop=(k == 7)),
                                 reads=[Twgp, ThfT[k]], writes=[Tpg])
                        for k in range(8):
                            P.pe(lambda e, k=k, jj=jj, pu=pu: e.matmul(pu[:], lhsT=wup[:, k, jj * 128:(jj + 1) * 128], rhs=hfT[:, k, :], start=(k == 0), stop=(k == 7)),
                                 reads=[Twup, ThfT[k]], writes=[Tpu])
                        sg, Tsg = sgr.get()
                        P.act(lambda e, pg=pg, sg=sg: e.activation(out=sg[:], in_=pg[:], func=AF.Silu), reads=[Tpg], writes=[Tsg])
                        P.dve(lambda e, j=j, pu=pu, sg=sg: e.tensor_tensor(out=aT[:, j, :], in0=sg[:], in1=pu[:], op=ALU.mult), reads=[Tsg, Tpu], writes=[TaT])
                    advance()

                def down(t, half):
                    xs = x1sets[t % 2]
                    hs = slice(half * 512, (half + 1) * 512)
                    accs = [pb.get() for _ in range(4)]

                    def piece(jp):
                        wdp, Twdp = take("d", t, jp, half)
                        for jj in range(2):
                            j = 2 * jp + jj
                            for blk in range(4):
                                P.pe(lambda e, j=j, jj=jj, blk=blk: e.matmul(accs[blk][0][:], lhsT=aT[:, j, blk * 128:(blk + 1) * 128], rhs=wdp[:, jj, :],
                                                                             start=(j == 0), stop=(j == NJ - 1)), reads=[TaT, Twdp], writes=[accs[blk][1]])
                        advance()
                    for jp in range(NJ // 2):
                        piece(jp)
                    evs = []
                    for blk in range(4):
                        sg, Tsg = sgr.get()
                        P.act(lambda e, blk=blk, sg=sg: e.copy(out=sg[:], in_=accs[blk][0][:]), reads=[accs[blk][1]], writes=[Tsg])
                        evs.append((sg, Tsg))
                    for blk in range(4):
                        xb, Txb = xs[blk]
                        sg, Tsg = evs[blk]
                        P.dve(lambda e, sg=sg: e.tensor_tensor(out=sg[:], in0=sg[:], in1=gtf[:, hs], op=ALU.mult), reads=[Tsg, Tgtf], writes=[Tsg])
                        P.dve(lambda e, xb=xb, sg=sg: e.tensor_tensor(out=xb[:, hs], in0=sg[:], in1=xb[:, hs], op=ALU.add), reads=[Tsg, Txb], writes=[Txb])

                def final(t, blk):
                    xb, Txb = x1sets[t % 2][blk]
                    st, Tst = fe["stat"].get()
                    junk, Tjunk = fe["junk"].get()
                    P.act(lambda e: e.activation(out=junk[:], in_=xb[:], func=AF.Square, accum_out=st[:, 0:1]), reads=[Txb], writes=[Tjunk, Tst])
                    P.dve(lambda e: e.tensor_scalar(out=st[:, 1:2], in0=st[:, 0:1], scalar1=1.0 / D, scalar2=EPS, op0=ALU.mult, op1=ALU.add), reads=[Tst], writes=[Tst])
                    P.act(lambda e: e.activation(out=st[:, 2:3], in_=st[:, 1:2], func=AF.Sqrt), reads=[Tst], writes=[Tst])
                    P.dve(lambda e: e.reciprocal(out=st[:, 3:4], in_=st[:, 2:3]), reads=[Tst], writes=[Tst])
                    P.dve(lambda e: e.scalar_tensor_tensor(out=xb[:], in0=xb[:], scalar=st[:, 3:4], in1=gfin[:], op0=ALU.mult, op1=ALU.mult),
                          reads=[Txb, Tst, Tgfin], writes=[Txb])
                    r0 = t * 512 + blk * 128
                    P.dma("sync", out_d[r0:r0 + 128, :], xb[:], reads=[Txb])

                advance()
                pro1(0)
                pro2(0)
                for i in range(NT):
                    for jp in range(NJ // 2):
                        up(i, jp)
                        if jp == 3 and i + 1 < NT:
                            pro1(i + 1)
                    if i + 1 < NT:
                        pro2(i + 1)
                    down(i, 0)
                    down(i, 1)
                    for blk in range(4):
                        final(i, blk)
                finals = P.emit(nc, semstack, finals)
    return nc


def _col(v, n):
    return np.ascontiguousarray(np.asarray(v, np.float32).reshape(n, 128).T)


def shared_inputs(inp):
    f = lambda a: np.ascontiguousarray(np.asarray(a, np.float32))
    w_in = f(inp["w_in"][0])
    winl = np.concatenate([w_in[:, 0:672], w_in[:, 656:672], w_in[:, 640:656]], axis=1)
    winm = w_in[:, 672:2216]
    w_uq = f(inp["w_uq"][0]).reshape(384, 8, 96)
    wq = np.zeros((384, 8, 256), np.float32)
    wq[:, :, 0:32] = w_uq[:, :, 64:96]
    wq[:, :, 64:128] = w_uq[:, :, 0:64]
    wq[:, :, 128:144] = w_uq[:, :, 80:96]
    wq[:, :, 144:160] = w_uq[:, :, 64:80]
    w_ukv = f(inp["w_ukv"][0]).reshape(256, 8, 128)
    wkv = np.zeros((256, 8, 192), np.float32)
    wkv[:, :, 64:128] = w_ukv[:, :, 0:64]
    wkv[:, :, 128:192] = w_ukv[:, :, 64:128]
    conv_w = f(inp["conv_w"][0])
    convw = np.ascontiguousarray(conv_w.T.reshape(8, 64, 4).transpose(1, 0, 2).reshape(64, 32))
    convb = np.ascontiguousarray(f(inp["conv_b"][0]).reshape(8, 64).T)
    bg = np.ascontiguousarray(f(inp["b_gates"][0]).reshape(2, 4).T)
    ident = np.eye(128, dtype=np.float32)
    tri = np.triu(np.ones((128, 128), np.float32))
    sel4 = np.zeros((4, 256), np.float32)
    for h in range(4):
        sel4[h, h * 64:(h + 1) * 64] = 1.0
    inv = (10000.0 ** (-np.arange(16, dtype=np.float64) / 16.0))
    ropec = np.zeros((32, 4), np.float32)
    ropec[:, 0] = np.tile(inv / (2 * np.pi), 2)
    TWO_PI = 6.28318
    ropec[:, 1] = np.concatenate([-np.ones(16), np.ones(16)])
    ropec[:, 2] = ropec[:, 1] * TWO_PI
    ropec[:, 3] = TWO_PI
    return {
        "w_ada": f(inp["w_ada"][0]), "badaT": _col(inp["b_ada"][0], 48), "gmixT": _col(inp["g_mix"][0], 8),
        "gffnT": _col(inp["g_ffn"][0], 8), "gfin": f(inp["g_final"]).reshape(1, D),
        "winl": np.ascontiguousarray(winl), "winm": np.ascontiguousarray(winm),
        "gqT": _col(inp["g_q"][0], 3), "gkvT": _col(inp["g_kv"][0], 2),
        "wq": wq.reshape(384, 8 * 256), "wkv": wkv.reshape(256, 8 * 192),
        "convw": convw, "convb": convb, "bg": bg,
        "gomlaT": _col(inp["g_out_mla"][0], 4), "gomlsT": _col(inp["g_out_mlstm"][0], 4),
        "w_out": f(inp["w_out"][0]), "w_gate": f(inp["w_gate"][0]), "w_up": f(inp["w_up"][0]),
        "w_down": f(inp["w_down"][0]),
        "ident": ident, "tri": tri, "sel4": sel4, "ropec": ropec,
    }


def core_inputs(inp, shared, b):
    d = dict(shared)
    d["x"] = np.ascontiguousarray(np.asarray(inp["x"][b], np.float32))
    d["cT"] = _col(inp["c"][b], 8)
    d["pos"] = np.ascontiguousarray(np.asarray(inp["positions"][b], np.int32).reshape(1, S))
    return d


def kernel(**inputs):
    nc = build_program()
    shared = shared_inputs(inputs)
    in_maps = [core_inputs(inputs, shared, b) for b in range(8)]
    res = run_bass_kernel_spmd(nc, in_maps, core_ids=list(range(8)))
    return np.stack([np.asarray(r["out"], np.float32) for r in res.results], axis=0)
```
